# Optimizing a Trainium2 kernel written in Bass

```python
import jax, jax.numpy as jnp
from jax import lax
import numpy as np

D_MODEL = 4096
BATCH = 2
SEQ = 4096
DEPTH = 1
DEC_BATCH = 8
DEC_SEQ = 32
PAST_LEN = 4096

CHUNK = 64
SB_Q_BLOCK = 128
SB_HEAD_DIM = 128
SB_WIDTH = D_MODEL // 2
SB_HEADS = SB_WIDTH // SB_HEAD_DIM
ML_V_DIM = 512
ML_QK_DIM = ML_V_DIM // 2
ML_V_WIDTH = D_MODEL - SB_WIDTH
ML_HEADS = ML_V_WIDTH // ML_V_DIM
ML_QK_WIDTH = ML_HEADS * ML_QK_DIM
MIX_WIDTH = SB_WIDTH + ML_V_WIDTH
D_FF = 256 * ((8 * D_MODEL // 3 + 255) // 256)
PLE_DIM = 256
EPS = 1e-6
F_BIAS_LO = 3.0
F_BIAS_HI = 6.0
IN_WIDTHS = (SB_WIDTH, SB_WIDTH, SB_WIDTH, ML_QK_WIDTH, ML_QK_WIDTH, ML_V_WIDTH, ML_V_WIDTH, 2 * ML_HEADS)
IN_WIDTH = sum(IN_WIDTHS)
IN_SPLITS = tuple(int(s) for s in np.cumsum(IN_WIDTHS)[:-1])

kernel_name = 'hybrid_stickbreak_mlstm_streaming_step'

F32 = jnp.float32


def rmsnorm(x, g):
    xf = x.astype(F32)
    y = xf * lax.rsqrt(jnp.mean(xf * xf, axis=-1, keepdims=True) + EPS)
    return (y * g.astype(F32)).astype(x.dtype)


def half_ffn(x, ln, w_gate, w_up, w_down):
    h = rmsnorm(x, ln)
    return x + 0.5 * ((jax.nn.silu(h @ w_gate) * (h @ w_up)) @ w_down)


def mixer_heads(h, w_in, b_if):
    B, L, _ = h.shape
    z = h @ w_in
    sb_q, sb_k, sb_v, ml_q, ml_k, ml_v, ml_o, ml_g = jnp.split(z, IN_SPLITS, axis=-1)
    sb_q, sb_k, sb_v = [a.reshape(B, L, SB_HEADS, SB_HEAD_DIM) for a in (sb_q, sb_k, sb_v)]

    def ml_heads(a, d):
        return jnp.transpose(a.reshape(B, L, ML_HEADS, d), (0, 2, 1, 3)).astype(F32)

    ml_q = ml_heads(ml_q, ML_QK_DIM) * (ML_QK_DIM ** -0.5)
    ml_k = ml_heads(ml_k, ML_QK_DIM)
    ml_v = ml_heads(ml_v, ML_V_DIM)
    gates = jnp.transpose((ml_g + b_if).astype(F32), (0, 2, 1))
    i_pre = gates[:, :ML_HEADS]
    log_f = jax.nn.log_sigmoid(gates[:, ML_HEADS:])
    return sb_q, sb_k, sb_v, ml_q, ml_k, ml_v, ml_o, i_pre, log_f


def sb_block(q, k, v, q_pos, k_pos):
    z = jnp.einsum('bqhd,bkhd->bhqk', q, k).astype(F32) * (SB_HEAD_DIM ** -0.5)
    mask = k_pos[None, :] < q_pos[:, None]
    log_1m = jnp.where(mask, -jax.nn.softplus(z), 0.0)
    after = lax.cumsum(log_1m, axis=3, reverse=True) - log_1m
    a = jnp.where(mask, jnp.exp(jax.nn.log_sigmoid(z) + after), 0.0)
    return jnp.einsum('bhqk,bkhd->bqhd', a, v.astype(F32))


def sb_prompt(q, k, v):
    B, S = q.shape[0], q.shape[1]
    nb = S // SB_Q_BLOCK
    pos = jnp.arange(S)
    qb = jnp.moveaxis(q.reshape(B, nb, SB_Q_BLOCK, SB_HEADS, SB_HEAD_DIM), 1, 0)
    pb = pos.reshape(nb, SB_Q_BLOCK)
    out = lax.map(lambda a: sb_block(a[0], k, v, a[1], pos), (qb, pb))
    return jnp.moveaxis(out, 0, 1).reshape(B, S, SB_HEADS, SB_HEAD_DIM)


def sb_step(q, k_new, v_new, k_cache, v_cache):
    P, L = k_cache.shape[1], q.shape[1]
    k = jnp.concatenate([k_cache, k_new.astype(k_cache.dtype)], axis=1)
    v = jnp.concatenate([v_cache, v_new.astype(v_cache.dtype)], axis=1)
    return sb_block(q, k, v, P + jnp.arange(L), jnp.arange(P + L))


def mlstm_chunk(carry, q, k, v, i_pre, log_f):
    C, n, m = carry
    L = q.shape[2]
    b = jnp.cumsum(log_f, axis=-1)
    causal = jnp.tril(jnp.ones((L, L), dtype=bool))
    d_log = jnp.where(causal, b[..., :, None] - b[..., None, :] + i_pre[..., None, :], -jnp.inf)
    inter_log = b + m[..., None]
    m_row = jnp.maximum(inter_log, jnp.max(d_log, axis=-1))
    w_intra = jnp.exp(d_log - m_row[..., None])
    w_inter = jnp.exp(inter_log - m_row)
    s = jnp.einsum('bhtd,bhsd->bhts', q, k) * w_intra
    num = w_inter[..., None] * jnp.einsum('bhvd,bhtd->bhtv', C, q) + jnp.einsum('bhts,bhsv->bhtv', s, v)
    den = w_inter * jnp.einsum('bhd,bhtd->bht', n, q) + jnp.sum(s, axis=-1)
    h = num / jnp.maximum(jnp.abs(den), jnp.exp(-m_row))[..., None]
    m_new = m_row[..., -1]
    w_prev = jnp.exp(b[..., -1] + m - m_new)
    w_tok = jnp.exp(b[..., -1:] - b + i_pre - m_new[..., None])
    C_new = w_prev[..., None, None] * C + jnp.einsum('bhs,bhsv,bhsd->bhvd', w_tok, v, k)
    n_new = w_prev[..., None] * n + jnp.einsum('bhs,bhsd->bhd', w_tok, k)
    return (C_new, n_new, m_new), h


def mlstm_prompt(q, k, v, i_pre, log_f):
    B, H, S = q.shape[0], q.shape[1], q.shape[2]
    nc = S // CHUNK

    def to_chunks(a):
        return jnp.moveaxis(a.reshape(a.shape[:2] + (nc, CHUNK) + a.shape[3:]), 2, 0)

    carry0 = (jnp.zeros((B, H, ML_V_DIM, ML_QK_DIM), F32),
              jnp.zeros((B, H, ML_QK_DIM), F32),
              jnp.zeros((B, H), F32))
    carry, h = lax.scan(lambda c, xs: mlstm_chunk(c, *xs), carry0,
                        (to_chunks(q), to_chunks(k), to_chunks(v), to_chunks(i_pre), to_chunks(log_f)))
    h = jnp.moveaxis(h, 0, 2).reshape(B, H, S, ML_V_DIM)
    return carry, h


def layer_pre(x, ln1, w1g, w1u, w1d, ln_mix, w_in, b_if):
    x = half_ffn(x, ln1, w1g, w1u, w1d)
    return x, mixer_heads(rmsnorm(x, ln_mix), w_in, b_if)


def layer_post(x, sb_h, ml_h, ml_o, p, g_sb, g_ml, w_out, ln2, w2g, w2u, w2d, ln_ple, w_pg, w_pp):
    B, L, _ = x.shape
    sb = rmsnorm(sb_h, g_sb).reshape(B, L, SB_WIDTH)
    ml = rmsnorm(jnp.transpose(ml_h, (0, 2, 1, 3)), g_ml) * jax.nn.sigmoid(
        ml_o.reshape(B, L, ML_HEADS, ML_V_DIM).astype(F32))
    mixed = jnp.concatenate([sb, ml.reshape(B, L, ML_V_WIDTH)], axis=-1).astype(x.dtype)
    x = x + mixed @ w_out
    x = half_ffn(x, ln2, w2g, w2u, w2d)
    gate = jax.nn.sigmoid(rmsnorm(x, ln_ple) @ w_pg)
    return x + gate * (p @ w_pp)


def setup_inputs(seed: int = 0) -> dict:
    key = jax.random.key(seed)
    ks = iter(jax.random.split(key, 40))

    def nrm(shape, scale=1.0):
        return jax.random.normal(next(ks), shape, F32) * scale

    def gain(shape):
        return 1.0 + nrm(shape, 0.02)

    D, F = D_MODEL, D_FF
    b_i = nrm((DEPTH, ML_HEADS), 0.1)
    b_f = jnp.linspace(F_BIAS_LO, F_BIAS_HI, ML_HEADS, dtype=F32)[None, :] + nrm((DEPTH, ML_HEADS), 0.1)
    b_if = jnp.concatenate([b_i, b_f], axis=-1)
    return {
        'x_prompt': nrm((BATCH, SEQ, D)),
        'x_sample': nrm((DEC_BATCH, DEC_SEQ, D)),
        'cache_sb_k': nrm((DEPTH, DEC_BATCH, PAST_LEN, SB_HEADS, SB_HEAD_DIM)),
        'cache_sb_v': nrm((DEPTH, DEC_BATCH, PAST_LEN, SB_HEADS, SB_HEAD_DIM)),
        'state_ml_c': nrm((DEPTH, DEC_BATCH, ML_HEADS, ML_V_DIM, ML_QK_DIM)),
        'state_ml_n': nrm((DEPTH, DEC_BATCH, ML_HEADS, ML_QK_DIM)),
        'state_ml_m': nrm((DEPTH, DEC_BATCH, ML_HEADS)),
        'p_prompt': nrm((DEPTH, BATCH, SEQ, PLE_DIM)),
        'p_sample': nrm((DEPTH, DEC_BATCH, DEC_SEQ, PLE_DIM)),
        'ln_ffn1': gain((DEPTH, D)),
        'w_ffn1_gate': nrm((DEPTH, D, F), D ** -0.5),
        'w_ffn1_up': nrm((DEPTH, D, F), D ** -0.5),
        'w_ffn1_down': nrm((DEPTH, F, D), F ** -0.5),
        'ln_mix': gain((DEPTH, D)),
        'w_in': nrm((DEPTH, D, IN_WIDTH), D ** -0.5),
        'b_if': b_if,
        'g_sb_head': gain((DEPTH, SB_HEADS, SB_HEAD_DIM)),
        'g_ml_head': gain((DEPTH, ML_HEADS, ML_V_DIM)),
        'w_out': nrm((DEPTH, MIX_WIDTH, D), MIX_WIDTH ** -0.5),
        'ln_ffn2': gain((DEPTH, D)),
        'w_ffn2_gate': nrm((DEPTH, D, F), D ** -0.5),
        'w_ffn2_up': nrm((DEPTH, D, F), D ** -0.5),
        'w_ffn2_down': nrm((DEPTH, F, D), F ** -0.5),
        'ln_ple': gain((DEPTH, D)),
        'w_ple_gate': nrm((DEPTH, D, D), D ** -0.5),
        'w_ple_proj': nrm((DEPTH, PLE_DIM, D), PLE_DIM ** -0.5),
        'ln_final': gain((D,)),
    }


def reference(x_prompt, x_sample, cache_sb_k, cache_sb_v, state_ml_c, state_ml_n, state_ml_m,
              p_prompt, p_sample, ln_ffn1, w_ffn1_gate, w_ffn1_up, w_ffn1_down, ln_mix, w_in, b_if,
              g_sb_head, g_ml_head, w_out, ln_ffn2, w_ffn2_gate, w_ffn2_up, w_ffn2_down,
              ln_ple, w_ple_gate, w_ple_proj, ln_final):
    xp, xs = x_prompt, x_sample
    pk, pv, pc, pn, pm = [], [], [], [], []
    sk_l, sv_l, sc, sn, sm = [], [], [], [], []
    for l in range(DEPTH):
        pre_w = (ln_ffn1[l], w_ffn1_gate[l], w_ffn1_up[l], w_ffn1_down[l], ln_mix[l], w_in[l], b_if[l])
        post_w = (g_sb_head[l], g_ml_head[l], w_out[l], ln_ffn2[l], w_ffn2_gate[l], w_ffn2_up[l],
                  w_ffn2_down[l], ln_ple[l], w_ple_gate[l], w_ple_proj[l])
        xp, (sq, sk, sv, mq, mk, mv, mo, mi, mf) = layer_pre(xp, *pre_w)
        sb_h = sb_prompt(sq, sk, sv)
        (c, n, m), ml_h = mlstm_prompt(mq, mk, mv, mi, mf)
        xp = layer_post(xp, sb_h, ml_h, mo, p_prompt[l], *post_w)
        pk.append(sk); pv.append(sv); pc.append(c); pn.append(n); pm.append(m)
        xs, (sq, sk, sv, mq, mk, mv, mo, mi, mf) = layer_pre(xs, *pre_w)
        sb_h = sb_step(sq, sk, sv, cache_sb_k[l], cache_sb_v[l])
        carry = (state_ml_c[l].astype(F32), state_ml_n[l].astype(F32), state_ml_m[l].astype(F32))
        (c, n, m), ml_h = mlstm_chunk(carry, mq, mk, mv, mi, mf)
        xs = layer_post(xs, sb_h, ml_h, mo, p_sample[l], *post_w)
        sk_l.append(sk); sv_l.append(sv); sc.append(c); sn.append(n); sm.append(m)
    y_prompt = rmsnorm(xp, ln_final)
    y_sample = rmsnorm(xs, ln_final)
    return (y_prompt, y_sample,
            jnp.stack(pk), jnp.stack(pv), jnp.stack(pc), jnp.stack(pn), jnp.stack(pm),
            jnp.stack(sk_l), jnp.stack(sv_l), jnp.stack(sc), jnp.stack(sn), jnp.stack(sm))
```

```python
import numpy as np
from contextlib import ExitStack
import concourse.bass as bass
import concourse.mybir as mybir
from concourse.bass_utils import run_bass_kernel_spmd

F32 = mybir.dt.float32
BF16 = mybir.dt.bfloat16
AF = mybir.ActivationFunctionType
ALU = mybir.AluOpType
AX = mybir.AxisListType
EPS = 1e-6


class _Sem:
    def __init__(self, handle, name):
        self.handle = handle
        self.name = name
        self.issued = 0


class _Ev:
    __slots__ = ("sem", "count", "is_dma")

    def __init__(self, sem, count, is_dma):
        self.sem = sem
        self.count = count
        self.is_dma = is_dma


class Buf:
    def __init__(self, ap, name):
        self.ap = ap
        self.name = name
        self.last_write = None
        self.reads = []
        self.dma_sem = None
        self.excl = False


class _Eng:
    def __init__(self, name, obj, sem, is_pe=False):
        self.name = name
        self.obj = obj
        self.sem = sem
        self.seen = {}
        self.is_pe = is_pe
        self.n_wait = 0
        self.n_ins = 0


class Tracker:
    def __init__(self, nc):
        self.nc = nc
        self.es = ExitStack()
        self.scopes = []
        self.nsem = 0
        self.PE = _Eng("pe", nc.tensor, self._sem("pe"), is_pe=True)
        self.ACT = _Eng("act", nc.scalar, self._sem("act"))
        self.DVE = _Eng("dve", nc.vector, self._sem("dve"))
        self.POOL = _Eng("pool", nc.gpsimd, self._sem("pool"))
        self.SP = _Eng("sp", nc.sync, self._sem("sp"))
        self.engs = [self.PE, self.ACT, self.DVE, self.POOL, self.SP]
        self.dma_sems = []
        self.nt = 0

    def _sem(self, name):
        self.nsem += 1
        h = self.es.enter_context(self.nc.semaphore(f"s{self.nsem}_{name}"))
        return _Sem(h, name)

    def push(self):
        self.scopes.append(ExitStack())

    def pop(self):
        self.barrier()
        self.scopes.pop().close()

    def _stack(self):
        return self.scopes[-1] if self.scopes else self.es

    def sbuf(self, name, shape, dtype):
        self.nt += 1
        t = self._stack().enter_context(self.nc.sbuf_tensor(f"{name}_{self.nt}", list(shape), dtype))
        return Buf(t, name)

    def psum(self, name, shape, dtype):
        self.nt += 1
        t = self._stack().enter_context(self.nc.psum_tensor(f"{name}_{self.nt}", list(shape), dtype))
        b = Buf(t, name)
        b.excl = True
        return b

    def view(self, ap, name="v"):
        return Buf(ap, name)

    def _deps(self, reads, writes):
        deps = []
        for b in reads:
            if b.last_write is not None:
                deps.append(b.last_write)
            if b.excl:
                deps.extend(b.reads)
        for b in writes:
            if b.last_write is not None:
                deps.append(b.last_write)
            deps.extend(b.reads)
        return deps

    def _do_waits(self, eng, deps):
        need = {}
        for ev in deps:
            if ev.sem is eng.sem and eng.is_pe:
                continue
            tgt = ev.sem.issued if ev.is_dma else ev.count
            if need.get(ev.sem, 0) < tgt:
                need[ev.sem] = tgt
        for sem, tgt in need.items():
            if eng.seen.get(sem, 0) >= tgt:
                continue
            eng.obj.wait_ge(sem.handle, tgt)
            eng.seen[sem] = tgt
            eng.n_wait += 1

    def _record(self, ev, reads, writes):
        for b in writes:
            b.last_write = ev
            b.reads = []
        for b in reads:
            if b in writes:
                continue
            b.reads = [e for e in b.reads if e.sem is not ev.sem]
            b.reads.append(ev)

    def op(self, eng, fn, reads=(), writes=()):
        self._do_waits(eng, self._deps(reads, writes))
        ins = fn()
        eng.sem.issued += 1
        eng.n_ins += 1
        ins.then_inc(eng.sem.handle, 1)
        self._record(_Ev(eng.sem, eng.sem.issued, False), reads, writes)
        return ins

    def dma(self, eng, out_ap, in_ap, reads=(), writes=(), sem_buf=None, no_waw=False, **kw):
        self._do_waits(eng, self._deps(reads, () if no_waw else writes))
        if sem_buf is None:
            sem_buf = writes[0]
        if sem_buf.dma_sem is None:
            sem_buf.dma_sem = self._sem("dma_" + sem_buf.name)
            self.dma_sems.append(sem_buf.dma_sem)
        sem = sem_buf.dma_sem
        ins = eng.obj.dma_start(out=out_ap, in_=in_ap, **kw)
        sem.issued += 16
        ins.then_inc(sem.handle, 16)
        eng.n_ins += 1
        self._record(_Ev(sem, sem.issued, True), reads, writes)
        return ins

    def collective(self, kind, groups, in_ap, out_ap, reads=(), writes=()):
        eng = self.POOL
        self._do_waits(eng, self._deps(reads, writes))
        sem_buf = writes[0]
        if sem_buf.dma_sem is None:
            sem_buf.dma_sem = self._sem("cc_" + sem_buf.name)
            self.dma_sems.append(sem_buf.dma_sem)
        sem = sem_buf.dma_sem
        ins = eng.obj.collective_compute(kind, mybir.AluOpType.bypass, replica_groups=groups, ins=[in_ap], outs=[out_ap])
        sem.issued += 16
        ins.then_inc(sem.handle, 16)
        eng.n_ins += 1
        self._record(_Ev(sem, sem.issued, True), reads, writes)
        return ins

    def barrier(self):
        for e in self.engs:
            for o in self.engs:
                if o is e or o.sem.issued == 0:
                    continue
                if e.seen.get(o.sem, 0) < o.sem.issued:
                    e.obj.wait_ge(o.sem.handle, o.sem.issued)
                    e.seen[o.sem] = o.sem.issued
            for sem in self.dma_sems:
                if sem.issued > 0 and e.seen.get(sem, 0) < sem.issued:
                    e.obj.wait_ge(sem.handle, sem.issued)
                    e.seen[sem] = sem.issued

    def finish(self):
        self.barrier()
        while self.scopes:
            self.scopes.pop().close()
        self.es.close()


class Cfg:
    def __init__(self, D=4096, SEQ=4096, DEC_SEQ=32, PAST=4096, PLE=256):
        self.D = D
        self.SEQ = SEQ
        self.DEC_SEQ = DEC_SEQ
        self.PAST = PAST
        self.PLE = PLE
        self.F = 256 * ((8 * D // 3 + 255) // 256)
        self.HD = 128
        self.SBW = D // 2
        self.SBH = self.SBW // 128
        self.DV = 512
        self.DQK = 256
        self.MLVW = D - self.SBW
        self.MLH = self.MLVW // self.DV
        self.MLQK = self.MLH * self.DQK
        self.INW = 3 * self.SBW + 2 * self.MLQK + 2 * self.MLVW + 2 * self.MLH
        self.o_q = 0
        self.o_k = self.SBW
        self.o_v = 2 * self.SBW
        self.o_mq = 3 * self.SBW
        self.o_mk = self.o_mq + self.MLQK
        self.o_mv = self.o_mk + self.MLQK
        self.o_mo = self.o_mv + self.MLVW
        self.o_g = self.o_mo + self.MLVW
        self.KC = D // 128
        self.CHUNK = 64
        self.TT = 512
        self.stages = ("A", "SBP", "MLP", "SBS", "MLS", "C")


WEIGHTS = ["w_ffn1_gate", "w_ffn1_up", "w_ffn1_down", "w_in", "w_out", "w_ffn2_gate", "w_ffn2_up",
           "w_ffn2_down", "w_ple_gate", "w_ple_proj"]


def build(cfg):
    c = cfg
    D, KC, F, INW, SEQ, NS, PAST = c.D, c.KC, c.F, c.INW, c.SEQ, c.DEC_SEQ, c.PAST
    SBW, SBH, MLH, DV, DQK, PLE = c.SBW, c.SBH, c.MLH, c.DV, c.DQK, c.PLE
    nc = bass.Bass("TRN2", target_bir_lowering=False)

    def din(name, shape, dt=F32):
        return nc.dram_tensor(name, list(shape), dt, kind="ExternalInput").ap()

    def dout(name, shape, dt=F32):
        return nc.dram_tensor(name, list(shape), dt, kind="ExternalOutput").ap()

    def dscr(name, shape, dt):
        return nc.dram_tensor(name, list(shape), dt, kind="Internal").ap()

    xp = din("xp", [SEQ, D]); xs = din("xs", [NS, D])
    pp = din("pp", [SEQ, PLE]); ps_ = din("ps", [NS, PLE])
    ck = din("ck", [PAST, SBW]); cv = din("cv", [PAST, SBW])
    sc_in = din("sc", [MLH * DV, DQK]); sn_in = din("sn", [128, MLH * 2]); sm_in = din("sm", [MLH, 1])
    W = {}
    W["w_ffn1_gate"] = din("w_ffn1_gate", [D, F]); W["w_ffn1_up"] = din("w_ffn1_up", [D, F])
    W["w_ffn1_down"] = din("w_ffn1_down", [F, D]); W["w_in"] = din("w_in", [D, INW])
    W["w_out"] = din("w_out", [D, D])
    W["w_ffn2_gate"] = din("w_ffn2_gate", [D, F]); W["w_ffn2_up"] = din("w_ffn2_up", [D, F])
    W["w_ffn2_down"] = din("w_ffn2_down", [F, D]); W["w_ple_gate"] = din("w_ple_gate", [D, D])
    W["w_ple_proj"] = din("w_ple_proj", [PLE, D])
    lnT_in = din("lnT", [128, 5 * KC])
    gsbT_in = din("gsbT", [128, SBH])
    gml_in = din("gml", [64, MLH * DV])
    bi_in = din("bi", [MLH, 1]); bf_in = din("bf", [MLH, 1])
    ident_in = din("ident", [128, 128]); ones_in = din("ones", [128, 128])
    tri_in = din("tri", [128, 128])
    maskw_in = din("maskw", [128, 896])
    cmask_in = din("cmask", [64, 64])
    sel_in = din("sel", [MLH, MLH * 128])
    yp = dout("yp", [SEQ, D]); ys = dout("ys", [NS, D])
    kp = dout("kp", [SEQ, SBW]); vp = dout("vp", [SEQ, SBW])
    cp_o = dout("cp", [MLH * DV, DQK]); np_o = dout("np", [128, MLH * 2]); mp_o = dout("mp", [MLH, 1])
    ks = dout("ks", [NS, SBW]); vs = dout("vs", [NS, SBW])
    cs_o = dout("cs", [MLH * DV, DQK]); ns_o = dout("ns", [128, MLH * 2]); ms_o = dout("ms", [MLH, 1])
    NTOK = SEQ + NS
    z_scr = dscr("z_scr", [NTOK, INW], F32)
    xT_scr = dscr("xT_scr", [128, KC, NTOK], F32)
    mixT_scr = dscr("mixT_scr", [D, NTOK], BF16)

    wbf = {name: dscr("wbf_" + name, list(W[name].shape), BF16) for name in W}
    T = Tracker(nc)
    PE, ACT, DVE, POOL, SP = T.PE, T.ACT, T.DVE, T.POOL, T.SP
    d_wbf = {name: Buf(None, "wbf_" + name) for name in W}

    def convert_weights():
        for name in WEIGHTS:
            rows = W[name].shape[0]
            for r0 in range(0, rows, 512):
                r1 = min(rows, r0 + 512)
                T.dma(POOL, wbf[name][r0:r1, :], W[name][r0:r1, :], reads=[d_in], writes=[d_wbf[name]],
                      sem_buf=d_wbf[name], no_waw=True)

    d_in = Buf(None, "d_in")
    d_z = Buf(None, "d_z"); d_xT = Buf(None, "d_xT"); d_mix = Buf(None, "d_mix"); d_out = Buf(None, "d_out")
    d_kv = Buf(None, "d_kv")

    ident = T.sbuf("ident", [128, 128], F32)
    identb = T.sbuf("identb", [128, 128], BF16)
    ones = T.sbuf("ones", [128, 128], F32)
    onesb = T.sbuf("onesb", [128, 128], BF16)
    tri = T.sbuf("tri", [128, 128], F32)
    lnT = T.sbuf("lnT", [128, 5 * KC], F32)
    gsbT = T.sbuf("gsbT", [128, SBH], F32)
    for dst, src in ((ident, ident_in), (ones, ones_in), (tri, tri_in), (lnT, lnT_in), (gsbT, gsbT_in)):
        T.dma(SP, dst.ap[:], src[:], reads=[d_in], writes=[dst], sem_buf=ident)
    T.op(DVE, lambda: nc.vector.tensor_copy(identb.ap[:], ident.ap[:]), reads=[ident], writes=[identb])
    T.op(DVE, lambda: nc.vector.tensor_copy(onesb.ap[:], ones.ap[:]), reads=[ones], writes=[onesb])

    tiles = []
    for t0 in range(0, SEQ, c.TT):
        tt = min(c.TT, SEQ - t0)
        tiles.append(dict(g0=t0, tt=tt, x=xp[t0:t0 + tt, :], p=pp[t0:t0 + tt, :], y=yp[t0:t0 + tt, :],
                          ko=kp[t0:t0 + tt, :], vo=vp[t0:t0 + tt, :]))
    tiles.append(dict(g0=SEQ, tt=NS, x=xs, p=ps_, y=ys, ko=ks, vo=vs))
    TTM = c.TT

    WCOLS = 256
    NSLOT = 3
    FPASS = 16

    def phase_AC_buffers():
        B = {}
        xT_t = T.sbuf("xT", [128, KC, TTM], F32)
        hT_t = T.sbuf("hT", [128, KC, TTM], BF16)
        aT_t = T.sbuf("aT", [128, FPASS, TTM], BF16)
        B["xT"] = [T.view(xT_t.ap[:, k, :], f"xT{k}") for k in range(KC)]
        B["hT"] = [T.view(hT_t.ap[:, k, :], f"hT{k}") for k in range(KC)]
        B["aT"] = [T.view(aT_t.ap[:, k, :], f"aT{k}") for k in range(FPASS)]
        B["xT_t"] = xT_t
        B["hT_t"] = hT_t
        B["slots"] = [T.sbuf(f"wslot{i}", [128, max(KC, FPASS) * WCOLS], BF16) for i in range(NSLOT)]
        B["slot_i"] = 0
        B["stage"] = [T.sbuf(f"stage{i}", [128, 2048], F32) for i in range(2)]
        B["stage_i"] = 0
        B["sq"] = [T.sbuf(f"sq{i}", [128, TTM], F32) for i in range(2)]
        B["rstd"] = T.sbuf("rstd", [128, TTM], F32)
        B["sg"] = [T.sbuf(f"sg{i}", [128, TTM], F32) for i in range(2)]
        B["pT"] = T.sbuf("pT", [128, 2, TTM], BF16)
        B["ps_a"] = [T.psum(f"psa{i}", [128, 512], F32) for i in range(6)]
        B["ps_i"] = 0
        B["ps_ss"] = T.psum("ps_ss", [128, 512], F32)
        B["ps_tr"] = T.psum("ps_tr", [128, 512], F32)
        return B

    def next_ps(B):
        p = B["ps_a"][B["ps_i"] % len(B["ps_a"])]
        B["ps_i"] += 1
        return p

    def next_slot(B):
        s = B["slots"][B["slot_i"] % NSLOT]
        B["slot_i"] += 1
        return s

    def next_stage(B):
        s = B["stage"][B["stage_i"] % 2]
        B["stage_i"] += 1
        return s

    def load_w(B, w_ap, r0, nrows, c0, ncols):
        s = next_slot(B)
        nk = nrows // 128
        dst = s.ap[:, 0:nk * ncols].rearrange("p (k n) -> p k n", k=nk)
        src = wbf[w_ap][r0:r0 + nrows, c0:c0 + ncols].rearrange("(k p) n -> p k n", p=128)
        T.dma(SP, dst, src, reads=[d_wbf[w_ap]], writes=[s])
        return s, dst

    def load_xT(B, tile):
        tt = tile["tt"]
        for s0 in range(0, tt, 128):
            ts = min(128, tt - s0)
            for half in range(0, D, 2048):
                hw = min(2048, D - half)
                st = next_stage(B)
                T.dma(ACT, st.ap[0:ts, 0:hw], tile["x"][s0:s0 + ts, half:half + hw], reads=[d_in], writes=[st])
                for k4 in range(0, hw // 128, 4):
                    n4 = min(4, hw // 128 - k4)
                    pt = B["ps_tr"]
                    for j in range(n4):
                        kk = k4 + j
                        T.op(PE, lambda kk=kk, j=j: nc.tensor.transpose(
                            pt.ap[:, j * 128:j * 128 + ts], st.ap[0:ts, kk * 128:(kk + 1) * 128], ident.ap[0:ts, 0:ts]),
                            reads=[st, ident], writes=[pt])
                    for j in range(n4):
                        kc = half // 128 + k4 + j
                        T.op(DVE, lambda kc=kc, j=j: nc.vector.tensor_copy(
                            B["xT_t"].ap[:, kc, s0:s0 + ts], pt.ap[:, j * 128:j * 128 + ts]),
                            reads=[pt], writes=[B["xT"][kc]])

    def rms_stats(B, tt):
        pss = B["ps_ss"]
        for kc in range(KC):
            sq = B["sq"][kc % 2]
            T.op(ACT, lambda kc=kc, sq=sq: nc.scalar.activation(sq.ap[:, 0:tt], B["xT_t"].ap[:, kc, 0:tt], AF.Square),
                 reads=[B["xT"][kc]], writes=[sq])
            T.op(PE, lambda kc=kc, sq=sq: nc.tensor.matmul(pss.ap[:, 0:tt], ones.ap[:], sq.ap[:, 0:tt],
                                                         start=(kc == 0), stop=(kc == KC - 1)),
                 reads=[sq, ones], writes=[pss])
        r = B["rstd"]
        T.op(ACT, lambda: nc.scalar.activation(r.ap[:, 0:tt], pss.ap[:, 0:tt], AF.Sqrt, bias=EPS, scale=1.0 / D),
             reads=[pss], writes=[r])
        T.op(DVE, lambda: nc.vector.reciprocal(r.ap[:, 0:tt], r.ap[:, 0:tt]), reads=[r], writes=[r])

    def make_hT(B, tt, ln_idx):
        rms_stats(B, tt)
        r = B["rstd"]
        for kc in range(KC):
            eng, e = (DVE, nc.vector)
            T.op(eng, lambda kc=kc, e=e: e.scalar_tensor_tensor(
                out=B["hT"][kc].ap[:, 0:tt], in0=B["xT"][kc].ap[:, 0:tt],
                scalar=lnT.ap[:, ln_idx * KC + kc:ln_idx * KC + kc + 1], in1=r.ap[:, 0:tt],
                op0=ALU.mult, op1=ALU.mult), reads=[B["xT"][kc], r, lnT], writes=[B["hT"][kc]])

    def ffn(B, tt, wg, wu, wd):
        nft = F // 128
        for f0 in range(0, nft, FPASS):
            nf = min(FPASS, nft - f0)
            for fg in range(0, nf, 2):
                fcol = (f0 + fg) * 128
                sg_, gv = load_w(B, wg, 0, D, fcol, WCOLS)
                su_, uv = load_w(B, wu, 0, D, fcol, WCOLS)
                for j in range(2):
                    pg = next_ps(B); pu = next_ps(B)
                    for kc in range(KC):
                        T.op(PE, lambda kc=kc: nc.tensor.matmul(pg.ap[:, 0:tt], gv[:, kc, j * 128:(j + 1) * 128],
                                                                B["hT"][kc].ap[:, 0:tt], start=(kc == 0), stop=(kc == KC - 1)),
                             reads=[sg_, B["hT"][kc]], writes=[pg])
                    for kc in range(KC):
                        T.op(PE, lambda kc=kc: nc.tensor.matmul(pu.ap[:, 0:tt], uv[:, kc, j * 128:(j + 1) * 128],
                                                                B["hT"][kc].ap[:, 0:tt], start=(kc == 0), stop=(kc == KC - 1)),
                             reads=[su_, B["hT"][kc]], writes=[pu])
                    sgb = B["sg"][(fg + j) % 2]
                    T.op(ACT, lambda: nc.scalar.activation(sgb.ap[:, 0:tt], pg.ap[:, 0:tt], AF.Silu),
                         reads=[pg], writes=[sgb])
                    a = B["aT"][fg + j]
                    T.op(DVE, lambda a=a: nc.vector.tensor_tensor(out=a.ap[:, 0:tt], in0=sgb.ap[:, 0:tt],
                                                                  in1=pu.ap[:, 0:tt], op=ALU.mult),
                         reads=[sgb, pu], writes=[a])
            for dg in range(0, D, WCOLS):
                sd_, dvw = load_w(B, wd, f0 * 128, nf * 128, dg, WCOLS)
                for j in range(WCOLS // 128):
                    po = next_ps(B)
                    for fk in range(nf):
                        T.op(PE, lambda fk=fk: nc.tensor.matmul(po.ap[:, 0:tt], dvw[:, fk, j * 128:(j + 1) * 128],
                                                                B["aT"][fk].ap[:, 0:tt], start=(fk == 0), stop=(fk == nf - 1)),
                             reads=[sd_, B["aT"][fk]], writes=[po])
                    xk = B["xT"][dg // 128 + j]
                    T.op(DVE, lambda xk=xk: nc.vector.scalar_tensor_tensor(
                        out=xk.ap[:, 0:tt], in0=po.ap[:, 0:tt], scalar=0.5, in1=xk.ap[:, 0:tt],
                        op0=ALU.mult, op1=ALU.add), reads=[po, xk], writes=[xk])

    def in_proj(B, tile):
        tt, g0 = tile["tt"], tile["g0"]
        for c0 in range(0, INW, WCOLS):
            ncol = min(WCOLS, INW - c0)
            s_, wv = load_w(B, "w_in", 0, D, c0, ncol)
            for s0 in range(0, tt, 128):
                ts = min(128, tt - s0)
                po = next_ps(B)
                for kc in range(KC):
                    T.op(PE, lambda kc=kc: nc.tensor.matmul(po.ap[0:ts, 0:ncol], B["hT"][kc].ap[:, s0:s0 + ts],
                                                            wv[:, kc, :], start=(kc == 0), stop=(kc == KC - 1)),
                         reads=[s_, B["hT"][kc]], writes=[po])
                st = next_stage(B)
                T.op(ACT, lambda: nc.scalar.copy(st.ap[0:ts, 0:ncol], po.ap[0:ts, 0:ncol]), reads=[po], writes=[st])
                T.dma(ACT, z_scr[g0 + s0:g0 + s0 + ts, c0:c0 + ncol], st.ap[0:ts, 0:ncol], reads=[st], writes=[d_z], sem_buf=st)
                if c.o_k <= c0 < c.o_k + SBW:
                    T.dma(ACT, tile["ko"][s0:s0 + ts, c0 - c.o_k:c0 - c.o_k + ncol], st.ap[0:ts, 0:ncol],
                          reads=[st], writes=[d_out], sem_buf=st)
                if c.o_v <= c0 < c.o_v + SBW:
                    T.dma(ACT, tile["vo"][s0:s0 + ts, c0 - c.o_v:c0 - c.o_v + ncol], st.ap[0:ts, 0:ncol],
                          reads=[st], writes=[d_out], sem_buf=st)

    def phase_A(B, tile):
        tt, g0 = tile["tt"], tile["g0"]
        sub = getattr(c, "sub", 9)
        load_xT(B, tile)
        if tile is tiles[0]:
            convert_weights()
        if sub >= 2:
            make_hT(B, tt, 0)
        if sub >= 3:
            ffn(B, tt, "w_ffn1_gate", "w_ffn1_up", "w_ffn1_down")
        if sub >= 4:
            make_hT(B, tt, 1)
        if sub >= 5:
            in_proj(B, tile)
        for kc in range(KC):
            T.dma(ACT, xT_scr[:, kc, g0:g0 + tt], B["xT"][kc].ap[:, 0:tt], reads=[B["xT"][kc]], writes=[d_xT],
                  sem_buf=B["xT_t"])

    def proj_accum(B, tt, w_ap, rhs_bufs, nk, post):
        for dg in range(0, D, WCOLS):
            s_, wv = load_w(B, w_ap, 0, nk * 128, dg, WCOLS)
            for j in range(WCOLS // 128):
                po = next_ps(B)
                for k in range(nk):
                    T.op(PE, lambda k=k: nc.tensor.matmul(po.ap[:, 0:tt], wv[:, k, j * 128:(j + 1) * 128],
                                                          rhs_bufs[k].ap[:, 0:tt], start=(k == 0), stop=(k == nk - 1)),
                         reads=[s_, rhs_bufs[k]], writes=[po])
                post(dg // 128 + j, po)

    def phase_C(B, tile):
        tt, g0 = tile["tt"], tile["g0"]
        for kc in range(KC):
            T.dma(ACT, B["xT"][kc].ap[:, 0:tt], xT_scr[:, kc, g0:g0 + tt], reads=[d_xT], writes=[B["xT"][kc]], sem_buf=B["xT_t"])
            T.dma(ACT, B["hT"][kc].ap[:, 0:tt], mixT_scr[kc * 128:(kc + 1) * 128, g0:g0 + tt], reads=[d_mix],
                  writes=[B["hT"][kc]], sem_buf=B["hT_t"])

        def post_add(dt, po):
            xk = B["xT"][dt]
            T.op(DVE, lambda: nc.vector.tensor_tensor(out=xk.ap[:, 0:tt], in0=po.ap[:, 0:tt], in1=xk.ap[:, 0:tt], op=ALU.add),
                 reads=[po, xk], writes=[xk])

        proj_accum(B, tt, "w_out", B["hT"], KC, post_add)
        make_hT(B, tt, 2)
        ffn(B, tt, "w_ffn2_gate", "w_ffn2_up", "w_ffn2_down")
        make_hT(B, tt, 3)
        pT = B["pT"]
        for s0 in range(0, tt, 128):
            ts = min(128, tt - s0)
            st = next_stage(B)
            T.dma(ACT, st.ap[0:ts, 0:PLE], tile["p"][s0:s0 + ts, :], reads=[d_in], writes=[st])
            pt = B["ps_tr"]
            for j in range(PLE // 128):
                T.op(PE, lambda j=j: nc.tensor.transpose(pt.ap[:, j * 128:j * 128 + ts], st.ap[0:ts, j * 128:(j + 1) * 128],
                                                         ident.ap[0:ts, 0:ts]), reads=[st, ident], writes=[pt])
            for j in range(PLE // 128):
                T.op(DVE, lambda j=j: nc.vector.tensor_copy(pT.ap[:, j, s0:s0 + ts], pt.ap[:, j * 128:j * 128 + ts]),
                     reads=[pt], writes=[pT])
        pT_views = [T.view(pT.ap[:, j, :], "pTv") for j in range(PLE // 128)]
        gate_sb = {}

        def post_gate(dt, po):
            g = B["aT"][dt % FPASS]
            T.op(ACT, lambda: nc.scalar.activation(g.ap[:, 0:tt], po.ap[:, 0:tt], AF.Sigmoid), reads=[po], writes=[g])
            gate_sb[dt] = g

        for dg in range(0, D, WCOLS):
            s_, wv = load_w(B, "w_ple_gate", 0, D, dg, WCOLS)
            s2_, wv2 = load_w(B, "w_ple_proj", 0, PLE, dg, WCOLS)
            for j in range(WCOLS // 128):
                dt = dg // 128 + j
                po = next_ps(B)
                for k in range(KC):
                    T.op(PE, lambda k=k: nc.tensor.matmul(po.ap[:, 0:tt], wv[:, k, j * 128:(j + 1) * 128],
                                                          B["hT"][k].ap[:, 0:tt], start=(k == 0), stop=(k == KC - 1)),
                         reads=[s_, B["hT"][k]], writes=[po])
                post_gate(dt, po)
                pq = next_ps(B)
                npk = PLE // 128
                for k in range(npk):
                    T.op(PE, lambda k=k: nc.tensor.matmul(pq.ap[:, 0:tt], wv2[:, k, j * 128:(j + 1) * 128],
                                                          pT.ap[:, k, 0:tt], start=(k == 0), stop=(k == npk - 1)),
                         reads=[s2_, pT], writes=[pq])
                g = gate_sb[dt]
                T.op(DVE, lambda g=g, pq=pq: nc.vector.tensor_tensor(out=g.ap[:, 0:tt], in0=g.ap[:, 0:tt], in1=pq.ap[:, 0:tt],
                                                                     op=ALU.mult), reads=[g, pq], writes=[g])
                xk = B["xT"][dt]
                T.op(POOL, lambda g=g, xk=xk: nc.gpsimd.tensor_tensor(out=xk.ap[:, 0:tt], in0=xk.ap[:, 0:tt], in1=g.ap[:, 0:tt],
                                                                      op=ALU.add), reads=[g, xk], writes=[xk])
        rms_stats(B, tt)
        r = B["rstd"]
        for kc in range(KC):
            xk = B["xT"][kc]
            T.op(DVE, lambda kc=kc, xk=xk: nc.vector.scalar_tensor_tensor(
                out=xk.ap[:, 0:tt], in0=xk.ap[:, 0:tt], scalar=lnT.ap[:, 4 * KC + kc:4 * KC + kc + 1], in1=r.ap[:, 0:tt],
                op0=ALU.mult, op1=ALU.mult), reads=[xk, r, lnT], writes=[xk])
        for s0 in range(0, tt, 128):
            ts = min(128, tt - s0)
            for half in range(0, D, 2048):
                hw = min(2048, D - half)
                st = next_stage(B)
                for k4 in range(0, hw // 128, 4):
                    n4 = min(4, hw // 128 - k4)
                    pt = B["ps_tr"]
                    for j in range(n4):
                        kc = half // 128 + k4 + j
                        T.op(PE, lambda kc=kc, j=j: nc.tensor.transpose(pt.ap[0:ts, j * 128:(j + 1) * 128],
                                                                       B["xT"][kc].ap[:, s0:s0 + ts], ident.ap[:]),
                             reads=[B["xT"][kc], ident], writes=[pt])
                    T.op(ACT, lambda k4=k4, n4=n4: nc.scalar.copy(st.ap[0:ts, k4 * 128:(k4 + n4) * 128], pt.ap[0:ts, 0:n4 * 128]),
                         reads=[pt], writes=[st])
                T.dma(ACT, tile["y"][s0:s0 + ts, half:half + hw], st.ap[0:ts, 0:hw], reads=[st], writes=[d_out], sem_buf=st)

    def sb_attention(tok0, nq_total, TQ, key_srcs, n_new):
        nkeys = sum(s[2] for s in key_srcs)
        nkb = (nkeys + 127) // 128
        n_old = nkeys - n_new
        scale = 128.0 ** -0.5
        T.push()
        maskw = T.sbuf("maskw", [128, 896], F32)
        T.dma(SP, maskw.ap[:], maskw_in[:], reads=[d_in], writes=[maskw])
        ktm = [T.sbuf(f"ktm{i}", [128, nkb, 128], BF16) for i in range(1)]
        qtm = T.sbuf("qtm", [128, (nq_total + 127) // 128, 128], BF16)
        vtm = [T.sbuf(f"vtm{i}", [128, nkb, 128], BF16) for i in range(2)]
        KTb = [T.sbuf(f"KT{i}", [128, nkb * 128], BF16) for i in range(2)]
        QTb = [T.sbuf(f"QT{i}", [128, ((nq_total + 127) // 128) * 128], BF16) for i in range(2)]
        e_sb = [T.sbuf(f"e{i}", [128, TQ], F32) for i in range(2)]
        sp_sb = [T.sbuf(f"sp{i}", [128, TQ], F32) for i in range(2)]
        t_sb = [T.sbuf(f"t{i}", [128, TQ], F32) for i in range(2)]
        u_sb = [T.sbuf(f"u{i}", [128, TQ], F32) for i in range(2)]
        A_sb = [T.sbuf(f"A{i}", [128, TQ], BF16) for i in range(2)]
        carry = T.sbuf("carry", [128, TQ], F32)
        o_sb = T.sbuf("o_sb", [128, TQ], F32)
        osq = T.sbuf("osq", [128, TQ], F32)
        orst = T.sbuf("orst", [128, TQ], F32)
        mixo = [T.sbuf(f"mixo{i}", [128, TQ], BF16) for i in range(2)]
        pS = [T.psum(f"pS{i}", [128, 512], F32) for i in range(2)]
        pG = [T.psum(f"pG{i}", [128, 512], F32) for i in range(2)]
        pR = [T.psum(f"pR{i}", [128, 512], F32) for i in range(2)]
        pO = T.psum("pO", [128, 512], F32)
        pT_ = T.psum("pTr", [128, 512], BF16)
        it = 0
        for h in range(SBH):
            KT = KTb[h % 2]; QT = QTb[h % 2]; V = vtm[h % 2]; K_ = ktm[0]
            r = 0
            for (kd, vd, n) in key_srcs:
                nfull = n // 128
                col = slice(h * 128, (h + 1) * 128)
                if nfull:
                    b0 = r // 128
                    T.dma(POOL, K_.ap[:, b0:b0 + nfull, :], kd[0:nfull * 128, col].rearrange("(b p) d -> p b d", p=128),
                          reads=[d_in, d_z, d_kv], writes=[K_])
                    T.dma(POOL, V.ap[:, b0:b0 + nfull, :], vd[0:nfull * 128, col].rearrange("(b p) d -> p b d", p=128),
                          reads=[d_in, d_z, d_kv], writes=[V])
                rem = n - nfull * 128
                if rem:
                    b0 = (r + nfull * 128) // 128
                    T.dma(POOL, K_.ap[0:rem, b0, :], kd[nfull * 128:n, col], reads=[d_in, d_z, d_kv], writes=[K_])
                    T.dma(POOL, V.ap[0:rem, b0, :], vd[nfull * 128:n, col], reads=[d_in, d_z, d_kv], writes=[V])
                r += n
                assert r % 128 == 0 or (kd is key_srcs[-1][0])
            nqb = (nq_total + 127) // 128
            qfull = nq_total // 128
            qcol = slice(c.o_q + h * 128, c.o_q + (h + 1) * 128)
            if qfull:
                T.dma(POOL, qtm.ap[:, 0:qfull, :], z_scr[tok0:tok0 + qfull * 128, qcol].rearrange("(b p) d -> p b d", p=128),
                      reads=[d_z], writes=[qtm])
            if nq_total - qfull * 128:
                rem = nq_total - qfull * 128
                T.dma(POOL, qtm.ap[0:rem, qfull, :], z_scr[tok0 + qfull * 128:tok0 + nq_total, qcol], reads=[d_z], writes=[qtm])

            def transpose_blocks(src, dstT, ntok):
                nb = (ntok + 127) // 128
                for b4 in range(0, nb, 4):
                    n4 = min(4, nb - b4)
                    for j in range(n4):
                        b = b4 + j
                        sz = min(128, ntok - b * 128)
                        T.op(PE, lambda b=b, j=j, sz=sz: nc.tensor.transpose(pT_.ap[:, j * 128:j * 128 + sz], src.ap[0:sz, b, :],
                                                                            identb.ap[0:sz, 0:sz]),
                             reads=[src, identb], writes=[pT_])
                    w = min(n4 * 128, ntok - b4 * 128)
                    T.op(DVE, lambda b4=b4, w=w: nc.vector.tensor_copy(dstT.ap[:, b4 * 128:b4 * 128 + w], pT_.ap[:, 0:w]),
                         reads=[pT_], writes=[dstT])

            lvl = getattr(c, 'sblvl', 9)
            if lvl >= 1:
                transpose_blocks(K_, KT, nkeys)
                transpose_blocks(qtm, QT, nq_total)
            if lvl < 2:
                continue
            for q0 in range(0, nq_total, TQ):
                tq = min(TQ, nq_total - q0)
                kmax = n_old + min(n_new, q0 + tq)
                kbs = [(kb * 128, min(128, kmax - kb * 128)) for kb in range((kmax + 127) // 128)]
                T.op(POOL, lambda: nc.gpsimd.memset(carry.ap[:, 0:tq], 0.0), reads=[], writes=[carry])
                for idx, (k0, ksz) in enumerate(reversed(kbs)):
                    i2 = it % 2
                    it += 1
                    S = pS[i2]; G = pG[i2]; R = pR[i2]
                    e = e_sb[i2]; spb = sp_sb[i2]; tb = t_sb[i2]; ub = u_sb[i2]; Ab = A_sb[i2]
                    first = idx == 0
                    last = idx == len(kbs) - 1
                    T.op(PE, lambda: nc.tensor.matmul(S.ap[0:ksz, 0:tq], KT.ap[:, k0:k0 + ksz], QT.ap[:, q0:q0 + tq],
                                                      start=True, stop=True), reads=[KT, QT], writes=[S])
                    T.op(ACT, lambda: nc.scalar.activation(e.ap[0:ksz, 0:tq], S.ap[0:ksz, 0:tq], AF.Exp, scale=scale),
                         reads=[S], writes=[e])
                    T.op(ACT, lambda: nc.scalar.activation(spb.ap[0:ksz, 0:tq], e.ap[0:ksz, 0:tq], AF.Ln, bias=1.0),
                         reads=[e], writes=[spb])
                    if lvl < 3:
                        continue
                    need_mask = (k0 + ksz - n_old) > q0
                    if need_mask:
                        off = (k0 - n_old) - q0
                        mslice = maskw.ap[0:ksz, 384 - off:384 - off + tq]
                        assert 0 <= 384 - off and 384 - off + tq <= 896, (off, tq)
                        T.op(POOL, lambda: nc.gpsimd.tensor_tensor(out=spb.ap[0:ksz, 0:tq], in0=spb.ap[0:ksz, 0:tq], in1=mslice,
                                                                   op=ALU.mult), reads=[spb, maskw], writes=[spb])
                    if lvl < 4:
                        continue
                    T.op(PE, lambda: nc.tensor.matmul(G.ap[0:ksz, 0:tq], tri.ap[0:ksz, 0:ksz], spb.ap[0:ksz, 0:tq],
                                                      start=True, stop=True), reads=[tri, spb], writes=[G])
                    if not last:
                        T.op(PE, lambda: nc.tensor.matmul(R.ap[:, 0:tq], ones.ap[0:ksz, :], spb.ap[0:ksz, 0:tq],
                                                          start=True, stop=True), reads=[ones, spb], writes=[R])
                    if lvl < 5:
                        continue
                    T.op(DVE, lambda: nc.vector.scalar_tensor_tensor(out=tb.ap[0:ksz, 0:tq], in0=S.ap[0:ksz, 0:tq], scalar=scale,
                                                                     in1=carry.ap[0:ksz, 0:tq], op0=ALU.mult, op1=ALU.subtract),
                         reads=[S, carry], writes=[tb])
                    if lvl < 5.1:
                        continue
                    T.op(DVE, lambda: nc.vector.tensor_tensor(out=ub.ap[0:ksz, 0:tq], in0=tb.ap[0:ksz, 0:tq], in1=G.ap[0:ksz, 0:tq],
                                                              op=ALU.subtract), reads=[tb, G], writes=[ub])
                    if lvl < 5.2:
                        continue
                    T.op(ACT, lambda: nc.scalar.activation(Ab.ap[0:ksz, 0:tq], ub.ap[0:ksz, 0:tq], AF.Exp), reads=[ub], writes=[Ab])
                    if lvl < 6:
                        continue
                    if need_mask:
                        T.op(POOL, lambda: nc.gpsimd.tensor_tensor(out=Ab.ap[0:ksz, 0:tq], in0=Ab.ap[0:ksz, 0:tq], in1=mslice,
                                                                   op=ALU.mult), reads=[Ab, maskw], writes=[Ab])
                    if lvl < 7:
                        continue
                    if not last:
                        T.op(DVE, lambda: nc.vector.tensor_tensor(out=carry.ap[:, 0:tq], in0=carry.ap[:, 0:tq], in1=R.ap[:, 0:tq],
                                                                  op=ALU.add), reads=[carry, R], writes=[carry])
                    T.op(PE, lambda: nc.tensor.matmul(pO.ap[:, 0:tq], V.ap[0:ksz, k0 // 128, :], Ab.ap[0:ksz, 0:tq],
                                                      start=first, stop=last), reads=[V, Ab], writes=[pO])
                if lvl < 8:
                    continue
                T.op(DVE, lambda: nc.vector.tensor_copy(o_sb.ap[:, 0:tq], pO.ap[:, 0:tq]), reads=[pO], writes=[o_sb])
                T.op(ACT, lambda: nc.scalar.activation(osq.ap[:, 0:tq], o_sb.ap[:, 0:tq], AF.Square), reads=[o_sb], writes=[osq])
                G = pG[it % 2]
                T.op(PE, lambda: nc.tensor.matmul(G.ap[:, 0:tq], ones.ap[:], osq.ap[:, 0:tq], start=True, stop=True),
                     reads=[ones, osq], writes=[G])
                T.op(ACT, lambda: nc.scalar.activation(orst.ap[:, 0:tq], G.ap[:, 0:tq], AF.Sqrt, bias=EPS, scale=1.0 / 128),
                     reads=[G], writes=[orst])
                T.op(DVE, lambda: nc.vector.reciprocal(orst.ap[:, 0:tq], orst.ap[:, 0:tq]), reads=[orst], writes=[orst])
                mo = mixo[(q0 // TQ) % 2]
                T.op(DVE, lambda: nc.vector.scalar_tensor_tensor(out=mo.ap[:, 0:tq], in0=o_sb.ap[:, 0:tq], scalar=gsbT.ap[:, h:h + 1],
                                                                 in1=orst.ap[:, 0:tq], op0=ALU.mult, op1=ALU.mult),
                     reads=[o_sb, orst, gsbT], writes=[mo])
                T.dma(SP, mixT_scr[h * 128:(h + 1) * 128, tok0 + q0:tok0 + q0 + tq], mo.ap[:, 0:tq], reads=[mo], writes=[d_mix],
                      sem_buf=mo)
        T.pop()

    def mlstm(tok0, L, nchunk, init):
        NT = L * nchunk
        SEGC = min(16, nchunk)
        SEGT = SEGC * L
        NV = DV // 128
        T.push()
        sel = T.sbuf("sel", [MLH, MLH * 128], F32)
        cmask = T.sbuf("cmask", [64, 64], F32)
        bi = T.sbuf("bi", [MLH, 1], F32); bfb = T.sbuf("bfb", [MLH, 1], F32)
        gml = T.sbuf("gml", [64, MLH * DV], F32)
        for dst, src in ((sel, sel_in), (cmask, cmask_in), (bi, bi_in), (bfb, bf_in), (gml, gml_in)):
            T.dma(SP, dst.ap[:], src[:], reads=[d_in], writes=[dst], sem_buf=sel)
        Mx = T.sbuf("Mx", [MLH, NT + 1], F32)
        m0 = T.sbuf("m0", [MLH, 1], F32)
        mout = T.sbuf("mout", [MLH, 1], F32)
        acol = T.sbuf("acol", [64, nchunk, MLH], F32)
        ccol = T.sbuf("ccol", [64, nchunk, MLH], F32)
        pg = T.psum("pg", [128, 512], F32)
        T.push()
        nblk = (NT + 127) // 128
        gtm = T.sbuf("gtm", [128, nblk, 2 * MLH], F32)
        nfull = NT // 128
        gcols = slice(c.o_g, c.o_g + 2 * MLH)
        if nfull:
            T.dma(SP, gtm.ap[:, 0:nfull, :], z_scr[tok0:tok0 + nfull * 128, gcols].rearrange("(b p) g -> p b g", p=128),
                  reads=[d_z], writes=[gtm])
        if NT - nfull * 128:
            T.dma(SP, gtm.ap[0:NT - nfull * 128, nfull, :], z_scr[tok0 + nfull * 128:tok0 + NT, gcols], reads=[d_z], writes=[gtm])
        GI = T.sbuf("GI", [MLH, NT], F32); GF = T.sbuf("GF", [MLH, NT], F32)
        Bc = T.sbuf("Bc", [MLH, NT], F32); av = T.sbuf("av", [MLH, NT], F32)
        nbm = T.sbuf("nbm", [MLH, NT], F32)
        onesr = T.sbuf("onesr", [MLH, NT], F32)
        T.op(POOL, lambda: nc.gpsimd.memset(onesr.ap[:], 1.0), writes=[onesr])
        if init is None:
            T.op(POOL, lambda: nc.gpsimd.memset(m0.ap[:], 0.0), writes=[m0])
        else:
            T.dma(SP, m0.ap[:], init[2][:], reads=[d_in], writes=[m0])
        for b in range(nblk):
            sz = min(128, NT - b * 128)
            T.op(PE, lambda b=b, sz=sz: nc.tensor.transpose(pg.ap[0:MLH, 0:sz], gtm.ap[0:sz, b, 0:MLH], ident.ap[0:sz, 0:sz]),
                 reads=[gtm, ident], writes=[pg])
            T.op(PE, lambda b=b, sz=sz: nc.tensor.transpose(pg.ap[0:MLH, 128:128 + sz], gtm.ap[0:sz, b, MLH:2 * MLH],
                                                           ident.ap[0:sz, 0:sz]), reads=[gtm, ident], writes=[pg])
            T.op(DVE, lambda b=b, sz=sz: nc.vector.tensor_copy(GI.ap[:, b * 128:b * 128 + sz], pg.ap[0:MLH, 0:sz]),
                 reads=[pg], writes=[GI])
            T.op(DVE, lambda b=b, sz=sz: nc.vector.tensor_copy(GF.ap[:, b * 128:b * 128 + sz], pg.ap[0:MLH, 128:128 + sz]),
                 reads=[pg], writes=[GF])
        T.op(DVE, lambda: nc.vector.tensor_scalar(out=GF.ap[:], in0=GF.ap[:], scalar1=bfb.ap[:, 0:1], scalar2=-1.0,
                                                  op0=ALU.add, op1=ALU.mult), reads=[GF, bfb], writes=[GF])
        T.op(ACT, lambda: nc.scalar.activation(GF.ap[:], GF.ap[:], AF.Exp), reads=[GF], writes=[GF])
        T.op(ACT, lambda: nc.scalar.activation(GF.ap[:], GF.ap[:], AF.Ln, bias=1.0), reads=[GF], writes=[GF])
        T.op(DVE, lambda: nc.vector.tensor_tensor_scan(out=Bc.ap[:], data0=onesr.ap[:], data1=GF.ap[:], initial=0.0,
                                                       op0=ALU.mult, op1=ALU.add), reads=[onesr, GF], writes=[Bc])
        T.op(DVE, lambda: nc.vector.scalar_tensor_tensor(out=av.ap[:], in0=GI.ap[:], scalar=bi.ap[:, 0:1], in1=Bc.ap[:],
                                                         op0=ALU.add, op1=ALU.add), reads=[GI, bi, Bc], writes=[av])
        T.op(DVE, lambda: nc.vector.tensor_copy(Mx.ap[:, 0:1], m0.ap[:]), reads=[m0], writes=[Mx])
        T.op(DVE, lambda: nc.vector.tensor_tensor_scan(out=Mx.ap[:, 1:NT + 1], data0=onesr.ap[:], data1=av.ap[:], initial=m0.ap[:, 0:1],
                                                       op0=ALU.mult, op1=ALU.max), reads=[onesr, av, m0, Mx], writes=[Mx])
        T.op(DVE, lambda: nc.vector.tensor_tensor(out=nbm.ap[:], in0=Bc.ap[:], in1=Mx.ap[:, 1:NT + 1], op=ALU.subtract),
             reads=[Bc, Mx], writes=[nbm])
        T.op(DVE, lambda: nc.vector.tensor_scalar(out=mout.ap[:], in0=nbm.ap[:, NT - 1:NT], scalar1=-1.0, scalar2=None, op0=ALU.mult),
             reads=[nbm], writes=[mout])
        out_m = mp_o if init is None else ms_o
        T.dma(SP, out_m[:], mout.ap[:], reads=[mout], writes=[d_out], sem_buf=mout)
        for ch in range(nchunk):
            T.op(PE, lambda ch=ch: nc.tensor.transpose(pg.ap[0:L, 0:MLH], av.ap[:, ch * L:(ch + 1) * L], ident.ap[0:MLH, 0:MLH]),
                 reads=[av, ident], writes=[pg])
            T.op(PE, lambda ch=ch: nc.tensor.transpose(pg.ap[0:L, 128:128 + MLH], nbm.ap[:, ch * L:(ch + 1) * L], ident.ap[0:MLH, 0:MLH]),
                 reads=[nbm, ident], writes=[pg])
            T.op(DVE, lambda ch=ch: nc.vector.tensor_copy(acol.ap[0:L, ch, :], pg.ap[0:L, 0:MLH]), reads=[pg], writes=[acol])
            T.op(ACT, lambda ch=ch: nc.scalar.activation(ccol.ap[0:L, ch, :], pg.ap[0:L, 128:128 + MLH], AF.Exp), reads=[pg], writes=[ccol])
        T.pop()
        Mb = T.sbuf("Mb", [128, SEGT + 1], F32)
        NMb = T.sbuf("NMb", [128, SEGT + 1], F32)
        qtm = T.sbuf("mq_tm", [L, SEGC, DQK], BF16)
        ktm = T.sbuf("mk_tm", [L, SEGC, DQK], BF16)
        vtm = T.sbuf("mv_tm", [L, SEGC, DV], BF16)
        qT = T.sbuf("mqT", [128, 2, SEGT], BF16)
        kT = T.sbuf("mkT", [128, 2, SEGT], BF16)
        mixml = T.sbuf("mixml", [128, NV, SEGT], BF16)
        CT = T.sbuf("CT", [128, 2, DV], F32)
        CTb = T.sbuf("CTb", [128, 2, DV], BF16)
        ncol = T.sbuf("ncol", [128, 2], F32)
        ncolb = T.sbuf("ncolb", [128, 2], BF16)
        cstage = T.sbuf("cstage", [128, NV, DQK], F32)
        WT = [T.sbuf(f"WT{i}", [64, 64], F32) for i in range(2)]
        sT = [T.sbuf(f"sT{i}", [64, 64], BF16) for i in range(2)]
        wib = [T.sbuf(f"wib{i}", [128, 64], F32) for i in range(2)]
        qs = [T.sbuf(f"qs{i}", [128, 2, 64], BF16) for i in range(2)]
        wtok = [T.sbuf(f"wtok{i}", [64, 1], F32) for i in range(2)]
        wprev = [T.sbuf(f"wprev{i}", [128, 1], F32) for i in range(2)]
        kst = [T.sbuf(f"kst{i}", [64, DQK], BF16) for i in range(2)]
        rr = [T.sbuf(f"rr{i}", [64, 2], F32) for i in range(2)]
        hsb = [T.sbuf(f"hsb{i}", [64, DV], F32) for i in range(2)]
        hsq = T.sbuf("hsq", [64, DV], F32)
        osb = [T.sbuf(f"osb{i}", [64, DV], F32) for i in range(2)]
        ymb = [T.sbuf(f"ymb{i}", [64, DV], BF16) for i in range(2)]
        p_qk = T.psum("p_qk", [128, 512], F32)
        p_num = T.psum("p_num", [128, 512], F32)
        p_den = T.psum("p_den", [128, 512], F32)
        p_up = [T.psum(f"p_up{i}", [128, 512], F32) for i in range(2)]
        p_trb = T.psum("p_trb", [128, 512], BF16)
        qscale = float(DQK) ** -0.5
        out_c, out_n = (cp_o, np_o) if init is None else (cs_o, ns_o)
        for h in range(MLH):
            if init is None:
                T.op(POOL, lambda: nc.gpsimd.memset(CT.ap[:], 0.0), writes=[CT])
                T.op(POOL, lambda: nc.gpsimd.memset(CTb.ap[:], 0.0), writes=[CTb])
                T.op(POOL, lambda: nc.gpsimd.memset(ncol.ap[:], 0.0), writes=[ncol])
                T.op(POOL, lambda: nc.gpsimd.memset(ncolb.ap[:], 0.0), writes=[ncolb])
            else:
                T.dma(SP, cstage.ap[:], init[0][h * DV:(h + 1) * DV, :].rearrange("(a p) d -> p a d", p=128), reads=[d_in], writes=[cstage])
                for dc in range(2):
                    for vc in range(NV):
                        T.op(PE, lambda dc=dc, vc=vc: nc.tensor.transpose(pg.ap[:, vc * 128:(vc + 1) * 128],
                                                                         cstage.ap[:, vc, dc * 128:(dc + 1) * 128], ident.ap[:]),
                             reads=[cstage, ident], writes=[pg])
                    T.op(DVE, lambda dc=dc: nc.vector.tensor_copy(CT.ap[:, dc, :], pg.ap[:, 0:DV]), reads=[pg], writes=[CT])
                    T.op(ACT, lambda dc=dc: nc.scalar.copy(CTb.ap[:, dc, :], pg.ap[:, 0:DV]), reads=[pg], writes=[CTb])
                T.dma(SP, ncol.ap[:], init[1][:, 2 * h:2 * h + 2], reads=[d_in], writes=[ncol])
                T.op(DVE, lambda: nc.vector.tensor_copy(ncolb.ap[:], ncol.ap[:]), reads=[ncol], writes=[ncolb])
            for seg0c in range(0, nchunk, SEGC):
                nsc = min(SEGC, nchunk - seg0c)
                s0 = seg0c * L
                st_ = nsc * L
                for c0 in range(0, st_ + 1, 512):
                    w = min(512, st_ + 1 - c0)
                    T.op(PE, lambda c0=c0, w=w: nc.tensor.matmul(pg.ap[:, 0:w], sel.ap[:, h * 128:(h + 1) * 128], Mx.ap[:, s0 + c0:s0 + c0 + w],
                                                                 start=True, stop=True), reads=[sel, Mx], writes=[pg])
                    T.op(DVE, lambda c0=c0, w=w: nc.vector.tensor_copy(Mb.ap[:, c0:c0 + w], pg.ap[:, 0:w]), reads=[pg], writes=[Mb])
                    T.op(ACT, lambda c0=c0, w=w: nc.scalar.mul(NMb.ap[:, c0:c0 + w], pg.ap[:, 0:w], -1.0), reads=[pg], writes=[NMb])
                r0 = tok0 + s0
                T.dma(POOL, qtm.ap[:, 0:nsc, :], z_scr[r0:r0 + st_, c.o_mq + h * DQK:c.o_mq + (h + 1) * DQK].rearrange("(c p) d -> p c d", p=L),
                      reads=[d_z], writes=[qtm])
                T.dma(POOL, ktm.ap[:, 0:nsc, :], z_scr[r0:r0 + st_, c.o_mk + h * DQK:c.o_mk + (h + 1) * DQK].rearrange("(c p) d -> p c d", p=L),
                      reads=[d_z], writes=[ktm])
                T.dma(POOL, vtm.ap[:, 0:nsc, :], z_scr[r0:r0 + st_, c.o_mv + h * DV:c.o_mv + (h + 1) * DV].rearrange("(c p) d -> p c d", p=L),
                      reads=[d_z], writes=[vtm])
                for (src, dst, scl) in ((qtm, qT, qscale), (ktm, kT, 1.0)):
                    for ch in range(nsc):
                        for dc in range(2):
                            T.op(PE, lambda ch=ch, dc=dc, src=src: nc.tensor.transpose(
                                p_trb.ap[:, dc * 64:dc * 64 + L], src.ap[0:L, ch, dc * 128:(dc + 1) * 128], identb.ap[0:L, 0:L]),
                                reads=[src, identb], writes=[p_trb])
                        T.op(ACT, lambda ch=ch, dst=dst, scl=scl: nc.scalar.mul(
                            dst.ap[:, :, ch * L:(ch + 1) * L], p_trb.ap[:, 0:128].rearrange("p (a b) -> p a b", a=2)[:, :, 0:L], scl),
                            reads=[p_trb], writes=[dst])
                for chl in range(nsc):
                    ch = seg0c + chl
                    i2 = ch % 2
                    t0 = chl * L
                    cs = slice(t0, t0 + L)
                    T.op(ACT, lambda: nc.scalar.activation(WT[i2].ap[0:L, 0:L], NMb.ap[0:L, 1 + t0:1 + t0 + L], AF.Exp,
                                                           bias=acol.ap[0:L, ch, h:h + 1]), reads=[NMb, acol], writes=[WT[i2]])
                    T.op(POOL, lambda: nc.gpsimd.tensor_tensor(out=WT[i2].ap[0:L, 0:L], in0=WT[i2].ap[0:L, 0:L], in1=cmask.ap[0:L, 0:L],
                                                               op=ALU.mult), reads=[WT[i2], cmask], writes=[WT[i2]])
                    T.op(ACT, lambda: nc.scalar.activation(wib[i2].ap[:, 0:L], NMb.ap[:, 1 + t0:1 + t0 + L], AF.Exp,
                                                           bias=Mb.ap[:, t0:t0 + 1]), reads=[NMb, Mb], writes=[wib[i2]])
                    T.op(ACT, lambda: nc.scalar.activation(wtok[i2].ap[0:L, :], acol.ap[0:L, ch, h:h + 1], AF.Exp,
                                                           bias=NMb.ap[0:L, t0 + L:t0 + L + 1]), reads=[NMb, acol], writes=[wtok[i2]])
                    T.op(ACT, lambda: nc.scalar.activation(wprev[i2].ap[:], NMb.ap[:, t0 + L:t0 + L + 1], AF.Exp,
                                                           bias=Mb.ap[:, t0:t0 + 1]), reads=[NMb, Mb], writes=[wprev[i2]])
                    for dc in range(2):
                        T.op(POOL, lambda dc=dc: nc.gpsimd.tensor_tensor(out=qs[i2].ap[:, dc, 0:L], in0=qT.ap[:, dc, cs], in1=wib[i2].ap[:, 0:L],
                                                                         op=ALU.mult), reads=[qT, wib[i2]], writes=[qs[i2]])
                    T.op(POOL, lambda: nc.gpsimd.tensor_scalar(out=kst[i2].ap[0:L, :], in0=ktm.ap[0:L, chl, :], scalar1=wtok[i2].ap[0:L, 0:1],
                                                               scalar2=None, op0=ALU.mult), reads=[ktm, wtok[i2]], writes=[kst[i2]])
                    for dc in range(2):
                        T.op(PE, lambda dc=dc: nc.tensor.matmul(p_qk.ap[0:L, 0:L], kT.ap[:, dc, cs], qT.ap[:, dc, cs], start=(dc == 0), stop=(dc == 1)),
                             reads=[kT, qT], writes=[p_qk])
                    T.op(DVE, lambda: nc.vector.tensor_tensor(out=sT[i2].ap[0:L, 0:L], in0=p_qk.ap[0:L, 0:L], in1=WT[i2].ap[0:L, 0:L], op=ALU.mult),
                         reads=[p_qk, WT[i2]], writes=[sT[i2]])
                    for dc in range(2):
                        T.op(PE, lambda dc=dc: nc.tensor.matmul(p_num.ap[0:L, 0:DV], qs[i2].ap[:, dc, 0:L], CTb.ap[:, dc, :], start=(dc == 0), stop=False),
                             reads=[qs[i2], CTb], writes=[p_num])
                    T.op(PE, lambda: nc.tensor.matmul(p_num.ap[0:L, 0:DV], sT[i2].ap[0:L, 0:L], vtm.ap[0:L, chl, :], start=False, stop=True),
                         reads=[sT[i2], vtm], writes=[p_num])
                    for dc in range(2):
                        T.op(PE, lambda dc=dc: nc.tensor.matmul(p_den.ap[0:L, 0:1], qs[i2].ap[:, dc, 0:L], ncolb.ap[:, dc:dc + 1], start=(dc == 0), stop=False),
                             reads=[qs[i2], ncolb], writes=[p_den])
                    T.op(PE, lambda: nc.tensor.matmul(p_den.ap[0:L, 0:1], sT[i2].ap[0:L, 0:L], onesb.ap[0:L, 0:1], start=False, stop=True),
                         reads=[sT[i2], onesb], writes=[p_den])
                    for dc in range(2):
                        T.op(PE, lambda dc=dc: nc.tensor.matmul(p_up[dc].ap[:, 0:DV], kst[i2].ap[0:L, dc * 128:(dc + 1) * 128], vtm.ap[0:L, chl, :],
                                                                start=True, stop=True), reads=[kst[i2], vtm], writes=[p_up[dc]])
                    for dc in range(2):
                        T.op(PE, lambda dc=dc: nc.tensor.matmul(p_den.ap[:, 8 + dc:9 + dc], kst[i2].ap[0:L, dc * 128:(dc + 1) * 128], onesb.ap[0:L, 0:1],
                                                                start=True, stop=True), reads=[kst[i2], onesb], writes=[p_den])
                    T.op(DVE, lambda: nc.vector.tensor_scalar(out=rr[i2].ap[0:L, 0:1], in0=p_den.ap[0:L, 0:1], scalar1=-1.0, scalar2=None, op0=ALU.mult),
                         reads=[p_den], writes=[rr[i2]])
                    T.op(DVE, lambda: nc.vector.tensor_tensor(out=rr[i2].ap[0:L, 0:1], in0=rr[i2].ap[0:L, 0:1], in1=p_den.ap[0:L, 0:1], op=ALU.max),
                         reads=[rr[i2], p_den], writes=[rr[i2]])
                    T.op(DVE, lambda: nc.vector.tensor_tensor(out=rr[i2].ap[0:L, 0:1], in0=rr[i2].ap[0:L, 0:1], in1=ccol.ap[0:L, ch, h:h + 1], op=ALU.max),
                         reads=[rr[i2], ccol], writes=[rr[i2]])
                    T.op(DVE, lambda: nc.vector.reciprocal(rr[i2].ap[0:L, 0:1], rr[i2].ap[0:L, 0:1]), reads=[rr[i2]], writes=[rr[i2]])
                    T.op(DVE, lambda: nc.vector.tensor_scalar(out=hsb[i2].ap[0:L, :], in0=p_num.ap[0:L, 0:DV], scalar1=rr[i2].ap[0:L, 0:1],
                                                              scalar2=None, op0=ALU.mult), reads=[p_num, rr[i2]], writes=[hsb[i2]])
                    for dc in range(2):
                        T.op(DVE, lambda dc=dc: nc.vector.scalar_tensor_tensor(out=CT.ap[:, dc, :], in0=CT.ap[:, dc, :], scalar=wprev[i2].ap[:, 0:1],
                                                                               in1=p_up[dc].ap[:, 0:DV], op0=ALU.mult, op1=ALU.add),
                             reads=[CT, wprev[i2], p_up[dc]], writes=[CT])
                    T.op(ACT, lambda: nc.scalar.copy(CTb.ap[:], CT.ap[:]), reads=[CT], writes=[CTb])
                    T.op(DVE, lambda: nc.vector.scalar_tensor_tensor(out=ncol.ap[:], in0=ncol.ap[:], scalar=wprev[i2].ap[:, 0:1],
                                                                     in1=p_den.ap[:, 8:10], op0=ALU.mult, op1=ALU.add),
                         reads=[ncol, wprev[i2], p_den], writes=[ncol])
                    T.op(DVE, lambda: nc.vector.tensor_copy(ncolb.ap[:], ncol.ap[:]), reads=[ncol], writes=[ncolb])
                    T.op(ACT, lambda: nc.scalar.activation(hsq.ap[0:L, :], hsb[i2].ap[0:L, :], AF.Square, accum_out=rr[i2].ap[0:L, 1:2]),
                         reads=[hsb[i2]], writes=[hsq, rr[i2]])
                    T.op(ACT, lambda: nc.scalar.activation(rr[i2].ap[0:L, 1:2], rr[i2].ap[0:L, 1:2], AF.Sqrt, bias=EPS, scale=1.0 / DV),
                         reads=[rr[i2]], writes=[rr[i2]])
                    T.op(DVE, lambda: nc.vector.reciprocal(rr[i2].ap[0:L, 1:2], rr[i2].ap[0:L, 1:2]), reads=[rr[i2]], writes=[rr[i2]])
                    T.dma(SP, osb[i2].ap[0:L, :], z_scr[r0 + t0:r0 + t0 + L, c.o_mo + h * DV:c.o_mo + (h + 1) * DV], reads=[d_z], writes=[osb[i2]])
                    T.op(ACT, lambda: nc.scalar.activation(osb[i2].ap[0:L, :], osb[i2].ap[0:L, :], AF.Sigmoid), reads=[osb[i2]], writes=[osb[i2]])
                    T.op(POOL, lambda: nc.gpsimd.tensor_tensor(out=osb[i2].ap[0:L, :], in0=osb[i2].ap[0:L, :], in1=gml.ap[0:L, h * DV:(h + 1) * DV],
                                                               op=ALU.mult), reads=[osb[i2], gml], writes=[osb[i2]])
                    T.op(DVE, lambda: nc.vector.scalar_tensor_tensor(out=ymb[i2].ap[0:L, :], in0=hsb[i2].ap[0:L, :], scalar=rr[i2].ap[0:L, 1:2],
                                                                     in1=osb[i2].ap[0:L, :], op0=ALU.mult, op1=ALU.mult),
                         reads=[hsb[i2], rr[i2], osb[i2]], writes=[ymb[i2]])
                    for vc in range(NV):
                        T.op(PE, lambda vc=vc: nc.tensor.transpose(p_trb.ap[:, 128 + vc * 64:128 + vc * 64 + L], ymb[i2].ap[0:L, vc * 128:(vc + 1) * 128],
                                                                   identb.ap[0:L, 0:L]), reads=[ymb[i2], identb], writes=[p_trb])
                    T.op(ACT, lambda: nc.scalar.copy(mixml.ap[:, :, cs], p_trb.ap[:, 128:128 + 64 * NV].rearrange("p (a b) -> p a b", b=64)[:, :, 0:L]),
                         reads=[p_trb], writes=[mixml])
                for vc in range(NV):
                    row = SBW + h * DV + vc * 128
                    T.dma(SP, mixT_scr[row:row + 128, r0:r0 + st_], mixml.ap[:, vc, 0:st_], reads=[mixml], writes=[d_mix], sem_buf=mixml)
            for vc in range(NV):
                for dc in range(2):
                    T.op(PE, lambda vc=vc, dc=dc: nc.tensor.transpose(pg.ap[:, dc * 128:(dc + 1) * 128], CT.ap[:, dc, vc * 128:(vc + 1) * 128], ident.ap[:]),
                         reads=[CT, ident], writes=[pg])
                T.op(DVE, lambda vc=vc: nc.vector.tensor_copy(cstage.ap[:, vc, :], pg.ap[:, 0:DQK]), reads=[pg], writes=[cstage])
            T.dma(SP, out_c[h * DV:(h + 1) * DV, :].rearrange("(a p) d -> p a d", p=128), cstage.ap[:], reads=[cstage], writes=[d_out], sem_buf=cstage)
            T.dma(SP, out_n[:, 2 * h:2 * h + 2], ncol.ap[:], reads=[ncol], writes=[d_out], sem_buf=ncol)
        T.pop()

    st = c.stages
    if "A" in st:
        T.push()
        B = phase_AC_buffers()
        for tile in tiles:
            phase_A(B, tile)
        T.pop()
    if "SBP" in st:
        sb_attention(0, SEQ, 512, [(z_scr[0:SEQ, c.o_k:c.o_k + SBW], z_scr[0:SEQ, c.o_v:c.o_v + SBW], SEQ)], SEQ)
    if "MLP" in st:
        mlstm(0, c.CHUNK, SEQ // c.CHUNK, None)
    if "SBS" in st:
        sb_attention(SEQ, NS, NS, [(ck, cv, PAST), (z_scr[SEQ:SEQ + NS, c.o_k:c.o_k + SBW], z_scr[SEQ:SEQ + NS, c.o_v:c.o_v + SBW], NS)], NS)
    if "MLS" in st:
        mlstm(SEQ, NS, 1, (sc_in, sn_in, sm_in))
    if "C" in st:
        T.push()
        B = phase_AC_buffers()
        for tile in tiles:
            phase_C(B, tile)
        T.pop()
    T.finish()
    return nc, T


def _consts(cfg):
    MLH = cfg.MLH
    k = np.arange(128)
    tri = (k[:, None] >= k[None, :]).astype(np.float32)
    j = np.arange(896)
    maskw = ((j[None, :] - 384) > k[:, None]).astype(np.float32)
    s = np.arange(64)
    cmask = (s[:, None] <= s[None, :]).astype(np.float32)
    sel = np.zeros((MLH, MLH * 128), np.float32)
    for h in range(MLH):
        sel[h, h * 128:(h + 1) * 128] = 1.0
    return dict(ident=np.eye(128, dtype=np.float32), ones=np.ones((128, 128), np.float32), tri=tri, maskw=maskw,
                cmask=cmask, sel=sel)


def make_in_maps(cfg, inp, n_cores):
    c = cfg
    f = lambda a: np.ascontiguousarray(np.asarray(a, dtype=np.float32))
    B = inp["x_prompt"].shape[0]
    lns = [inp["ln_ffn1"][0], inp["ln_mix"][0], inp["ln_ffn2"][0], inp["ln_ple"][0], inp["ln_final"]]
    lnT = np.concatenate([f(v).reshape(c.KC, 128).T for v in lns], axis=1)
    common = dict(lnT=f(lnT), gsbT=f(f(inp["g_sb_head"][0]).T),
                  gml=f(np.broadcast_to(f(inp["g_ml_head"][0]).reshape(1, -1), (64, c.MLH * c.DV))),
                  bi=f(f(inp["b_if"][0])[:c.MLH].reshape(c.MLH, 1)), bf=f(f(inp["b_if"][0])[c.MLH:].reshape(c.MLH, 1)))
    common.update(_consts(c))
    for w in WEIGHTS:
        common[w] = f(inp[w][0])
    maps = []
    for core in range(n_cores):
        pb = core % B
        sb = core % inp["x_sample"].shape[0]
        m = dict(common)
        m["xp"] = f(inp["x_prompt"][pb]); m["xs"] = f(inp["x_sample"][sb])
        m["pp"] = f(inp["p_prompt"][0, pb]); m["ps"] = f(inp["p_sample"][0, sb])
        m["ck"] = f(inp["cache_sb_k"][0, sb]).reshape(c.PAST, c.SBW)
        m["cv"] = f(inp["cache_sb_v"][0, sb]).reshape(c.PAST, c.SBW)
        m["sc"] = f(inp["state_ml_c"][0, sb]).reshape(c.MLH * c.DV, c.DQK)
        m["sn"] = f(f(inp["state_ml_n"][0, sb]).reshape(c.MLH * 2, 128).T)
        m["sm"] = f(inp["state_ml_m"][0, sb]).reshape(c.MLH, 1)
        maps.append(m)
    return maps


def assemble(cfg, res, B, DB):
    c = cfg
    r = res

    def n_un(a):
        return np.ascontiguousarray(a.T).reshape(c.MLH, c.DQK)

    yp = np.stack([r[b]["yp"] for b in range(B)])
    ys = np.stack([r[b]["ys"] for b in range(DB)])
    kp = np.stack([r[b]["kp"].reshape(c.SEQ, c.SBH, 128) for b in range(B)])[None]
    vp = np.stack([r[b]["vp"].reshape(c.SEQ, c.SBH, 128) for b in range(B)])[None]
    cp = np.stack([r[b]["cp"].reshape(c.MLH, c.DV, c.DQK) for b in range(B)])[None]
    np_ = np.stack([n_un(r[b]["np"]) for b in range(B)])[None]
    mp = np.stack([r[b]["mp"].reshape(c.MLH) for b in range(B)])[None]
    ks = np.stack([r[b]["ks"].reshape(c.DEC_SEQ, c.SBH, 128) for b in range(DB)])[None]
    vs = np.stack([r[b]["vs"].reshape(c.DEC_SEQ, c.SBH, 128) for b in range(DB)])[None]
    cs = np.stack([r[b]["cs"].reshape(c.MLH, c.DV, c.DQK) for b in range(DB)])[None]
    ns = np.stack([n_un(r[b]["ns"]) for b in range(DB)])[None]
    ms = np.stack([r[b]["ms"].reshape(c.MLH) for b in range(DB)])[None]
    return tuple(np.ascontiguousarray(a, dtype=np.float32) for a in (yp, ys, kp, vp, cp, np_, mp, ks, vs, cs, ns, ms))


def kernel(**inputs):
    cfg = Cfg()
    n = 8
    nc, _ = build(cfg)
    in_maps = make_in_maps(cfg, inputs, n)
    res = run_bass_kernel_spmd(nc, in_maps, core_ids=list(range(n)))
    return assemble(cfg, res.results, inputs["x_prompt"].shape[0], inputs["x_sample"].shape[0])
```

```python
import numpy as np
from contextlib import ExitStack
import concourse.bass as bass
import concourse.mybir as mybir
from concourse.bass_utils import run_bass_kernel_spmd

F32 = mybir.dt.float32
BF16 = mybir.dt.bfloat16
AF = mybir.ActivationFunctionType
ALU = mybir.AluOpType
AX = mybir.AxisListType
EPS = 1e-6


class _Sem:
    def __init__(self, handle, name):
        self.handle = handle
        self.name = name
        self.issued = 0


class _Ev:
    __slots__ = ("sem", "count", "is_dma")

    def __init__(self, sem, count, is_dma):
        self.sem = sem
        self.count = count
        self.is_dma = is_dma


class Buf:
    def __init__(self, ap, name):
        self.ap = ap
        self.name = name
        self.last_write = None
        self.reads = []
        self.dma_sem = None
        self.excl = False


class _Eng:
    def __init__(self, name, obj, sem, is_pe=False):
        self.name = name
        self.obj = obj
        self.sem = sem
        self.seen = {}
        self.is_pe = is_pe
        self.n_wait = 0
        self.n_ins = 0


class Tracker:
    def __init__(self, nc):
        self.nc = nc
        self.es = ExitStack()
        self.scopes = []
        self.nsem = 0
        self.PE = _Eng("pe", nc.tensor, self._sem("pe"), is_pe=True)
        self.ACT = _Eng("act", nc.scalar, self._sem("act"))
        self.DVE = _Eng("dve", nc.vector, self._sem("dve"))
        self.POOL = _Eng("pool", nc.gpsimd, self._sem("pool"))
        self.SP = _Eng("sp", nc.sync, self._sem("sp"))
        self.engs = [self.PE, self.ACT, self.DVE, self.POOL, self.SP]
        self.dma_sems = []
        self.nt = 0

    def _sem(self, name):
        self.nsem += 1
        h = self.es.enter_context(self.nc.semaphore(f"s{self.nsem}_{name}"))
        return _Sem(h, name)

    def push(self):
        self.scopes.append(ExitStack())

    def pop(self):
        self.barrier()
        self.scopes.pop().close()

    def _stack(self):
        return self.scopes[-1] if self.scopes else self.es

    def sbuf(self, name, shape, dtype):
        self.nt += 1
        t = self._stack().enter_context(self.nc.sbuf_tensor(f"{name}_{self.nt}", list(shape), dtype))
        return Buf(t, name)

    def psum(self, name, shape, dtype):
        self.nt += 1
        t = self._stack().enter_context(self.nc.psum_tensor(f"{name}_{self.nt}", list(shape), dtype))
        b = Buf(t, name)
        b.excl = True
        return b

    def view(self, ap, name="v"):
        return Buf(ap, name)

    def _deps(self, reads, writes):
        deps = []
        for b in reads:
            if b.last_write is not None:
                deps.append(b.last_write)
            if b.excl:
                deps.extend(b.reads)
        for b in writes:
            if b.last_write is not None:
                deps.append(b.last_write)
            deps.extend(b.reads)
        return deps

    def _do_waits(self, eng, deps):
        need = {}
        for ev in deps:
            if ev.sem is eng.sem and eng.is_pe:
                continue
            tgt = ev.sem.issued if ev.is_dma else ev.count
            if need.get(ev.sem, 0) < tgt:
                need[ev.sem] = tgt
        for sem, tgt in need.items():
            if eng.seen.get(sem, 0) >= tgt:
                continue
            eng.obj.wait_ge(sem.handle, tgt)
            eng.seen[sem] = tgt
            eng.n_wait += 1

    def _record(self, ev, reads, writes):
        for b in writes:
            b.last_write = ev
            b.reads = []
        for b in reads:
            if b in writes:
                continue
            b.reads = [e for e in b.reads if e.sem is not ev.sem]
            b.reads.append(ev)

    def op(self, eng, fn, reads=(), writes=()):
        self._do_waits(eng, self._deps(reads, writes))
        ins = fn()
        eng.sem.issued += 1
        eng.n_ins += 1
        ins.then_inc(eng.sem.handle, 1)
        self._record(_Ev(eng.sem, eng.sem.issued, False), reads, writes)
        return ins

    def dma(self, eng, out_ap, in_ap, reads=(), writes=(), sem_buf=None, no_waw=False, **kw):
        self._do_waits(eng, self._deps(reads, () if no_waw else writes))
        if sem_buf is None:
            sem_buf = writes[0]
        if sem_buf.dma_sem is None:
            sem_buf.dma_sem = self._sem("dma_" + sem_buf.name)
            self.dma_sems.append(sem_buf.dma_sem)
        sem = sem_buf.dma_sem
        ins = eng.obj.dma_start(out=out_ap, in_=in_ap, **kw)
        sem.issued += 16
        ins.then_inc(sem.handle, 16)
        eng.n_ins += 1
        self._record(_Ev(sem, sem.issued, True), reads, writes)
        return ins

    def collective(self, kind, groups, in_ap, out_ap, reads=(), writes=()):
        eng = self.POOL
        self._do_waits(eng, self._deps(reads, writes))
        sem_buf = writes[0]
        if sem_buf.dma_sem is None:
            sem_buf.dma_sem = self._sem("cc_" + sem_buf.name)
            self.dma_sems.append(sem_buf.dma_sem)
        sem = sem_buf.dma_sem
        ins = eng.obj.collective_compute(kind, mybir.AluOpType.bypass, replica_groups=groups, ins=[in_ap], outs=[out_ap])
        sem.issued += 16
        ins.then_inc(sem.handle, 16)
        eng.n_ins += 1
        self._record(_Ev(sem, sem.issued, True), reads, writes)
        return ins

    def barrier(self):
        for e in self.engs:
            for o in self.engs:
                if o is e or o.sem.issued == 0:
                    continue
                if e.seen.get(o.sem, 0) < o.sem.issued:
                    e.obj.wait_ge(o.sem.handle, o.sem.issued)
                    e.seen[o.sem] = o.sem.issued
            for sem in self.dma_sems:
                if sem.issued > 0 and e.seen.get(sem, 0) < sem.issued:
                    e.obj.wait_ge(sem.handle, sem.issued)
                    e.seen[sem] = sem.issued

    def finish(self):
        self.barrier()
        while self.scopes:
            self.scopes.pop().close()
        self.es.close()


class Cfg:
    def __init__(self, D=4096, SEQ=4096, DEC_SEQ=32, PAST=4096, PLE=256):
        self.D = D
        self.SEQ = SEQ
        self.DEC_SEQ = DEC_SEQ
        self.PAST = PAST
        self.PLE = PLE
        self.F = 256 * ((8 * D // 3 + 255) // 256)
        self.HD = 128
        self.SBW = D // 2
        self.SBH = self.SBW // 128
        self.DV = 512
        self.DQK = 256
        self.MLVW = D - self.SBW
        self.MLH = self.MLVW // self.DV
        self.MLQK = self.MLH * self.DQK
        self.INW = 3 * self.SBW + 2 * self.MLQK + 2 * self.MLVW + 2 * self.MLH
        self.o_q = 0
        self.o_k = self.SBW
        self.o_v = 2 * self.SBW
        self.o_mq = 3 * self.SBW
        self.o_mk = self.o_mq + self.MLQK
        self.o_mv = self.o_mk + self.MLQK
        self.o_mo = self.o_mv + self.MLVW
        self.o_g = self.o_mo + self.MLVW
        self.KC = D // 128
        self.CHUNK = 64
        self.TT = 512
        self.NRANK = 4
        self.stages = ("A", "SBP", "MLP", "SBS", "MLS", "C")


WEIGHTS = ["w_ffn1_gate", "w_ffn1_up", "w_ffn1_down", "w_in", "w_out", "w_ffn2_gate", "w_ffn2_up",
           "w_ffn2_down", "w_ple_gate", "w_ple_proj"]


def build(cfg):
    c = cfg
    D, KC, F, INW, SEQ, NS, PAST = c.D, c.KC, c.F, c.INW, c.SEQ, c.DEC_SEQ, c.PAST
    SBW, SBH, MLH, DV, DQK, PLE = c.SBW, c.SBH, c.MLH, c.DV, c.DQK, c.PLE
    nc = bass.Bass("TRN2", target_bir_lowering=False)

    def din(name, shape, dt=F32):
        return nc.dram_tensor(name, list(shape), dt, kind="ExternalInput").ap()

    def dout(name, shape, dt=F32):
        return nc.dram_tensor(name, list(shape), dt, kind="ExternalOutput").ap()

    def dscr(name, shape, dt):
        return nc.dram_tensor(name, list(shape), dt, kind="Internal").ap()

    xp = din("xp", [SEQ, D]); xs = din("xs", [NS, D])
    NQ = SEQ // c.NRANK
    TC = min(c.TT, NQ)
    NJ = SEQ // TC
    NCT = NQ // TC
    pp = din("ppq", [NQ, PLE]); ps_ = din("ps", [NS, PLE])
    selq_in = din("selq", [128, NCT * NJ])
    ck = din("ck", [PAST, SBW]); cv = din("cv", [PAST, SBW])
    sc_in = din("sc", [MLH * DV, DQK]); sn_in = din("sn", [128, MLH * 2]); sm_in = din("sm", [MLH, 1])
    W = {}
    W["w_ffn1_gate"] = din("w_ffn1_gate", [D, F]); W["w_ffn1_up"] = din("w_ffn1_up", [D, F])
    W["w_ffn1_down"] = din("w_ffn1_down", [F, D]); W["w_in"] = din("w_in", [D, INW])
    W["w_out"] = din("w_out", [D, D])
    W["w_ffn2_gate"] = din("w_ffn2_gate", [D, F]); W["w_ffn2_up"] = din("w_ffn2_up", [D, F])
    W["w_ffn2_down"] = din("w_ffn2_down", [F, D]); W["w_ple_gate"] = din("w_ple_gate", [D, D])
    W["w_ple_proj"] = din("w_ple_proj", [PLE, D])
    lnT_in = din("lnT", [128, 5 * KC])
    gsbT_in = din("gsbT", [128, SBH])
    gml_in = din("gml", [64, MLH * DV])
    bi_in = din("bi", [MLH, 1]); bf_in = din("bf", [MLH, 1])
    ident_in = din("ident", [128, 128]); ones_in = din("ones", [128, 128])
    tri_in = din("tri", [128, 128])
    maskw_in = din("maskw", [128, 896])
    cmask_in = din("cmask", [64, 64])
    sel_in = din("sel", [MLH, MLH * 128])
    yp = dout("ypq", [NQ, D]); ys = dout("ys", [NS, D])
    kp = dout("kp", [SEQ, SBW]); vp = dout("vp", [SEQ, SBW])
    cp_o = dout("cp", [MLH * DV, DQK]); np_o = dout("np", [128, MLH * 2]); mp_o = dout("mp", [MLH, 1])
    ks = dout("ks", [NS, SBW]); vs = dout("vs", [NS, SBW])
    cs_o = dout("cs", [MLH * DV, DQK]); ns_o = dout("ns", [128, MLH * 2]); ms_o = dout("ms", [MLH, 1])
    NTOK = SEQ + NS
    z_scr = dscr("z_scr", [NTOK, INW], F32)
    xT_scr = dscr("xT_scr", [128, KC, NTOK], F32)
    mixT_scr = dscr("mixT_scr", [D, NTOK], BF16)

    wbf = {name: dscr("wbf_" + name, list(W[name].shape), BF16) for name in W}
    T = Tracker(nc)
    PE, ACT, DVE, POOL, SP = T.PE, T.ACT, T.DVE, T.POOL, T.SP
    d_wbf = {name: Buf(None, "wbf_" + name) for name in W}

    def convert_weights():
        for name in WEIGHTS:
            rows = W[name].shape[0]
            for r0 in range(0, rows, 512):
                r1 = min(rows, r0 + 512)
                T.dma(POOL, wbf[name][r0:r1, :], W[name][r0:r1, :], reads=[d_in], writes=[d_wbf[name]],
                      sem_buf=d_wbf[name], no_waw=True)
            sem = d_wbf[name].dma_sem
            POOL.obj.wait_ge(sem.handle, sem.issued)
            POOL.seen[sem] = sem.issued

    d_in = Buf(None, "d_in")
    d_z = Buf(None, "d_z"); d_xT = Buf(None, "d_xT"); d_mix = Buf(None, "d_mix"); d_out = Buf(None, "d_out")
    d_kv = Buf(None, "d_kv")

    ident = T.sbuf("ident", [128, 128], F32)
    identb = T.sbuf("identb", [128, 128], BF16)
    ones = T.sbuf("ones", [128, 128], F32)
    onesb = T.sbuf("onesb", [128, 128], BF16)
    tri = T.sbuf("tri", [128, 128], F32)
    lnT = T.sbuf("lnT", [128, 5 * KC], F32)
    gsbT = T.sbuf("gsbT", [128, SBH], F32)
    for dst, src in ((ident, ident_in), (ones, ones_in), (tri, tri_in), (lnT, lnT_in), (gsbT, gsbT_in)):
        T.dma(SP, dst.ap[:], src[:], reads=[d_in], writes=[dst], sem_buf=ident)
    T.op(DVE, lambda: nc.vector.tensor_copy(identb.ap[:], ident.ap[:]), reads=[ident], writes=[identb])
    T.op(DVE, lambda: nc.vector.tensor_copy(onesb.ap[:], ones.ap[:]), reads=[ones], writes=[onesb])
    trib = T.sbuf("trib", [128, 128], BF16)
    T.op(DVE, lambda: nc.vector.tensor_copy(trib.ap[:], tri.ap[:]), reads=[tri], writes=[trib])

    tiles = []
    for t0 in range(0, SEQ, c.TT):
        tt = min(c.TT, SEQ - t0)
        tiles.append(dict(g0=t0, tt=tt, x=xp[t0:t0 + tt, :], ko=kp[t0:t0 + tt, :], vo=vp[t0:t0 + tt, :]))
    tiles.append(dict(g0=SEQ, tt=NS, x=xs, ko=ks, vo=vs))
    tilesC = []
    for i in range(NCT):
        tilesC.append(dict(g0=None, ci=i, tt=TC, p=pp[i * TC:(i + 1) * TC, :], y=yp[i * TC:(i + 1) * TC, :]))
    tilesC.append(dict(g0=SEQ, tt=NS, p=ps_, y=ys))
    TTM = c.TT

    WCOLS = 256
    NSLOT = 3
    FPASS = 16

    def phase_AC_buffers():
        B = {}
        xT_t = T.sbuf("xT", [128, KC, TTM], F32)
        hT_t = T.sbuf("hT", [128, KC, TTM], BF16)
        aT_t = T.sbuf("aT", [128, FPASS, TTM], BF16)
        B["xT"] = [T.view(xT_t.ap[:, k, :], f"xT{k}") for k in range(KC)]
        B["hT"] = [T.view(hT_t.ap[:, k, :], f"hT{k}") for k in range(KC)]
        B["aT"] = [T.view(aT_t.ap[:, k, :], f"aT{k}") for k in range(FPASS)]
        B["xT_t"] = xT_t
        B["aT_t"] = aT_t
        B["hT_t"] = hT_t
        B["slots"] = [T.sbuf(f"wslot{i}", [128, max(KC, FPASS) * WCOLS], BF16) for i in range(NSLOT)]
        B["slot_i"] = 0
        B["stage"] = [T.sbuf(f"stage{i}", [128, 2048], F32) for i in range(2)]
        B["stage_i"] = 0
        B["sq"] = [T.sbuf(f"sq{i}", [128, TTM], F32) for i in range(2)]
        B["rstd"] = T.sbuf("rstd", [128, TTM], F32)
        B["sg"] = [T.sbuf(f"sg{i}", [128, TTM], F32) for i in range(2)]
        B["pT"] = T.sbuf("pT", [128, 2, TTM], BF16)
        B["ps_a"] = [T.psum(f"psa{i}", [128, 512], F32) for i in range(6)]
        B["ps_i"] = 0
        B["ps_ss"] = T.psum("ps_ss", [128, 512], F32)
        B["ps_tr"] = T.psum("ps_tr", [128, 512], F32)
        return B

    def next_ps(B):
        p = B["ps_a"][B["ps_i"] % len(B["ps_a"])]
        B["ps_i"] += 1
        return p

    def next_slot(B):
        s = B["slots"][B["slot_i"] % NSLOT]
        B["slot_i"] += 1
        return s

    def next_stage(B):
        s = B["stage"][B["stage_i"] % 2]
        B["stage_i"] += 1
        return s

    def load_w(B, w_ap, r0, nrows, c0, ncols):
        s = next_slot(B)
        nk = nrows // 128
        dst = s.ap[:, 0:nk * ncols].rearrange("p (k n) -> p k n", k=nk)
        src = wbf[w_ap][r0:r0 + nrows, c0:c0 + ncols].rearrange("(k p) n -> p k n", p=128)
        T.dma(SP, dst, src, reads=[d_wbf[w_ap]], writes=[s])
        return s, dst

    def load_xT(B, tile):
        tt = tile["tt"]
        for s0 in range(0, tt, 128):
            ts = min(128, tt - s0)
            for half in range(0, D, 2048):
                hw = min(2048, D - half)
                st = next_stage(B)
                T.dma(ACT, st.ap[0:ts, 0:hw], tile["x"][s0:s0 + ts, half:half + hw], reads=[d_in], writes=[st])
                for k4 in range(0, hw // 128, 4):
                    n4 = min(4, hw // 128 - k4)
                    pt = B["ps_tr"]
                    for j in range(n4):
                        kk = k4 + j
                        T.op(PE, lambda kk=kk, j=j: nc.tensor.transpose(
                            pt.ap[:, j * 128:j * 128 + ts], st.ap[0:ts, kk * 128:(kk + 1) * 128], ident.ap[0:ts, 0:ts]),
                            reads=[st, ident], writes=[pt])
                    for j in range(n4):
                        kc = half // 128 + k4 + j
                        T.op(DVE, lambda kc=kc, j=j: nc.vector.tensor_copy(
                            B["xT_t"].ap[:, kc, s0:s0 + ts], pt.ap[:, j * 128:j * 128 + ts]),
                            reads=[pt], writes=[B["xT"][kc]])

    def rms_stats(B, tt):
        pss = B["ps_ss"]
        for kc in range(KC):
            sq = B["sq"][kc % 2]
            T.op(ACT, lambda kc=kc, sq=sq: nc.scalar.activation(sq.ap[:, 0:tt], B["xT_t"].ap[:, kc, 0:tt], AF.Square),
                 reads=[B["xT"][kc]], writes=[sq])
            T.op(PE, lambda kc=kc, sq=sq: nc.tensor.matmul(pss.ap[:, 0:tt], ones.ap[:], sq.ap[:, 0:tt],
                                                         start=(kc == 0), stop=(kc == KC - 1)),
                 reads=[sq, ones], writes=[pss])
        r = B["rstd"]
        T.op(ACT, lambda: nc.scalar.activation(r.ap[:, 0:tt], pss.ap[:, 0:tt], AF.Sqrt, bias=EPS, scale=1.0 / D),
             reads=[pss], writes=[r])
        T.op(DVE, lambda: nc.vector.reciprocal(r.ap[:, 0:tt], r.ap[:, 0:tt]), reads=[r], writes=[r])

    def make_hT(B, tt, ln_idx):
        rms_stats(B, tt)
        r = B["rstd"]
        for kc in range(KC):
            eng, e = (DVE, nc.vector)
            T.op(eng, lambda kc=kc, e=e: e.scalar_tensor_tensor(
                out=B["hT"][kc].ap[:, 0:tt], in0=B["xT"][kc].ap[:, 0:tt],
                scalar=lnT.ap[:, ln_idx * KC + kc:ln_idx * KC + kc + 1], in1=r.ap[:, 0:tt],
                op0=ALU.mult, op1=ALU.mult), reads=[B["xT"][kc], r, lnT], writes=[B["hT"][kc]])

    def ffn(B, tt, wg, wu, wd):
        nft = F // 128
        for f0 in range(0, nft, FPASS):
            nf = min(FPASS, nft - f0)
            for fg in range(0, nf, 2):
                fcol = (f0 + fg) * 128
                sg_, gv = load_w(B, wg, 0, D, fcol, WCOLS)
                su_, uv = load_w(B, wu, 0, D, fcol, WCOLS)
                for j in range(2):
                    pg = next_ps(B); pu = next_ps(B)
                    for kc in range(KC):
                        T.op(PE, lambda kc=kc: nc.tensor.matmul(pg.ap[:, 0:tt], gv[:, kc, j * 128:(j + 1) * 128],
                                                                B["hT"][kc].ap[:, 0:tt], start=(kc == 0), stop=(kc == KC - 1)),
                             reads=[sg_, B["hT"][kc]], writes=[pg])
                    for kc in range(KC):
                        T.op(PE, lambda kc=kc: nc.tensor.matmul(pu.ap[:, 0:tt], uv[:, kc, j * 128:(j + 1) * 128],
                                                                B["hT"][kc].ap[:, 0:tt], start=(kc == 0), stop=(kc == KC - 1)),
                             reads=[su_, B["hT"][kc]], writes=[pu])
                    sgb = B["sg"][(fg + j) % 2]
                    T.op(ACT, lambda: nc.scalar.activation(sgb.ap[:, 0:tt], pg.ap[:, 0:tt], AF.Silu),
                         reads=[pg], writes=[sgb])
                    a = B["aT"][fg + j]
                    T.op(DVE, lambda a=a: nc.vector.tensor_tensor(out=a.ap[:, 0:tt], in0=sgb.ap[:, 0:tt],
                                                                  in1=pu.ap[:, 0:tt], op=ALU.mult),
                         reads=[sgb, pu], writes=[a])
            for dg in range(0, D, WCOLS):
                sd_, dvw = load_w(B, wd, f0 * 128, nf * 128, dg, WCOLS)
                for j in range(WCOLS // 128):
                    po = next_ps(B)
                    for fk in range(nf):
                        T.op(PE, lambda fk=fk: nc.tensor.matmul(po.ap[:, 0:tt], dvw[:, fk, j * 128:(j + 1) * 128],
                                                                B["aT"][fk].ap[:, 0:tt], start=(fk == 0), stop=(fk == nf - 1)),
                             reads=[sd_, B["aT"][fk]], writes=[po])
                    xk = B["xT"][dg // 128 + j]
                    T.op(DVE, lambda xk=xk: nc.vector.scalar_tensor_tensor(
                        out=xk.ap[:, 0:tt], in0=po.ap[:, 0:tt], scalar=0.5, in1=xk.ap[:, 0:tt],
                        op0=ALU.mult, op1=ALU.add), reads=[po, xk], writes=[xk])

    def in_proj(B, tile):
        tt, g0 = tile["tt"], tile["g0"]
        for c0 in range(0, INW, WCOLS):
            ncol = min(WCOLS, INW - c0)
            s_, wv = load_w(B, "w_in", 0, D, c0, ncol)
            for s0 in range(0, tt, 128):
                ts = min(128, tt - s0)
                po = next_ps(B)
                for kc in range(KC):
                    T.op(PE, lambda kc=kc: nc.tensor.matmul(po.ap[0:ts, 0:ncol], B["hT"][kc].ap[:, s0:s0 + ts],
                                                            wv[:, kc, :], start=(kc == 0), stop=(kc == KC - 1)),
                         reads=[s_, B["hT"][kc]], writes=[po])
                st = next_stage(B)
                T.op(ACT, lambda: nc.scalar.copy(st.ap[0:ts, 0:ncol], po.ap[0:ts, 0:ncol]), reads=[po], writes=[st])
                T.dma(ACT, z_scr[g0 + s0:g0 + s0 + ts, c0:c0 + ncol], st.ap[0:ts, 0:ncol], reads=[st], writes=[d_z], sem_buf=st)
                if c.o_k <= c0 < c.o_k + SBW:
                    T.dma(ACT, tile["ko"][s0:s0 + ts, c0 - c.o_k:c0 - c.o_k + ncol], st.ap[0:ts, 0:ncol],
                          reads=[st], writes=[d_out], sem_buf=st)
                if c.o_v <= c0 < c.o_v + SBW:
                    T.dma(ACT, tile["vo"][s0:s0 + ts, c0 - c.o_v:c0 - c.o_v + ncol], st.ap[0:ts, 0:ncol],
                          reads=[st], writes=[d_out], sem_buf=st)

    def phase_A(B, tile):
        tt, g0 = tile["tt"], tile["g0"]
        sub = getattr(c, "sub", 9)
        load_xT(B, tile)
        if tile is tiles[0]:
            convert_weights()
        if sub >= 2:
            make_hT(B, tt, 0)
        if sub >= 3:
            ffn(B, tt, "w_ffn1_gate", "w_ffn1_up", "w_ffn1_down")
        if sub >= 4:
            make_hT(B, tt, 1)
        if sub >= 5:
            in_proj(B, tile)
        for kc in range(KC):
            T.dma(ACT, xT_scr[:, kc, g0:g0 + tt], B["xT"][kc].ap[:, 0:tt], reads=[B["xT"][kc]], writes=[d_xT],
                  sem_buf=B["xT_t"])

    def proj_accum(B, tt, w_ap, rhs_bufs, nk, post):
        for dg in range(0, D, WCOLS):
            s_, wv = load_w(B, w_ap, 0, nk * 128, dg, WCOLS)
            for j in range(WCOLS // 128):
                po = next_ps(B)
                for k in range(nk):
                    T.op(PE, lambda k=k: nc.tensor.matmul(po.ap[:, 0:tt], wv[:, k, j * 128:(j + 1) * 128],
                                                          rhs_bufs[k].ap[:, 0:tt], start=(k == 0), stop=(k == nk - 1)),
                         reads=[s_, rhs_bufs[k]], writes=[po])
                post(dg // 128 + j, po)

    def phase_C(B, tile):
        tt, g0 = tile["tt"], tile["g0"]
        if g0 is None:
            if "selq" not in B:
                B["selq"] = T.sbuf("selq", [128, NCT * NJ], F32)
                T.dma(ACT, B["selq"].ap[:], selq_in[:], reads=[d_in], writes=[B["selq"]])
            selq = B["selq"]
            ci = tile["ci"]
            XCH = 2048 // TC
            HCH = min(NJ, (FPASS // 2 * TTM) // TC)
            for kc in range(KC):
                for j0 in range(0, NJ, XCH):
                    nj = min(XCH, NJ - j0)
                    st = next_stage(B)
                    T.dma(ACT, st.ap[:, 0:nj * TC], xT_scr[:, kc, j0 * TC:(j0 + nj) * TC], reads=[d_xT], writes=[st])
                    for j in range(nj):
                        sc_ = selq.ap[:, ci * NJ + j0 + j:ci * NJ + j0 + j + 1]
                        xk = B["xT"][kc]
                        if j0 + j == 0:
                            T.op(DVE, lambda: nc.vector.tensor_scalar(out=xk.ap[:, 0:tt], in0=st.ap[:, j * TC:(j + 1) * TC], scalar1=sc_,
                                                                      scalar2=None, op0=ALU.mult), reads=[st, selq], writes=[xk])
                        else:
                            T.op(DVE, lambda: nc.vector.scalar_tensor_tensor(out=xk.ap[:, 0:tt], in0=st.ap[:, j * TC:(j + 1) * TC], scalar=sc_,
                                                                             in1=xk.ap[:, 0:tt], op0=ALU.mult, op1=ALU.add),
                                 reads=[st, selq, xk], writes=[xk])
                for j0 in range(0, NJ, HCH):
                    nj = min(HCH, NJ - j0)
                    half = (B.get("hsel_i", 0) % 2) * (FPASS // 2)
                    B["hsel_i"] = B.get("hsel_i", 0) + 1
                    nb = (nj * TC + TTM - 1) // TTM
                    hb = B["aT"][half:half + nb]
                    flat = B["aT_t"].ap[:, half:half + FPASS // 2, :].rearrange("p a t -> p (a t)")
                    T.dma(ACT, flat[:, 0:nj * TC], mixT_scr[kc * 128:(kc + 1) * 128, j0 * TC:(j0 + nj) * TC], reads=[d_mix], writes=hb,
                          sem_buf=B["aT_t"])
                    for j in range(nj):
                        sc_ = selq.ap[:, ci * NJ + j0 + j:ci * NJ + j0 + j + 1]
                        hk = B["hT"][kc]
                        src = flat[:, j * TC:(j + 1) * TC]
                        if j0 + j == 0:
                            T.op(DVE, lambda: nc.vector.tensor_scalar(out=hk.ap[:, 0:tt], in0=src, scalar1=sc_, scalar2=None, op0=ALU.mult),
                                 reads=hb + [selq], writes=[hk])
                        else:
                            T.op(DVE, lambda: nc.vector.scalar_tensor_tensor(out=hk.ap[:, 0:tt], in0=src, scalar=sc_, in1=hk.ap[:, 0:tt],
                                                                             op0=ALU.mult, op1=ALU.add), reads=hb + [selq, hk], writes=[hk])
        else:
            for kc in range(KC):
                T.dma(ACT, B["xT"][kc].ap[:, 0:tt], xT_scr[:, kc, g0:g0 + tt], reads=[d_xT], writes=[B["xT"][kc]], sem_buf=B["xT_t"])
                T.dma(ACT, B["hT"][kc].ap[:, 0:tt], mixT_scr[kc * 128:(kc + 1) * 128, g0:g0 + tt], reads=[d_mix],
                      writes=[B["hT"][kc]], sem_buf=B["hT_t"])

        def post_add(dt, po):
            xk = B["xT"][dt]
            T.op(DVE, lambda: nc.vector.tensor_tensor(out=xk.ap[:, 0:tt], in0=po.ap[:, 0:tt], in1=xk.ap[:, 0:tt], op=ALU.add),
                 reads=[po, xk], writes=[xk])

        proj_accum(B, tt, "w_out", B["hT"], KC, post_add)
        make_hT(B, tt, 2)
        ffn(B, tt, "w_ffn2_gate", "w_ffn2_up", "w_ffn2_down")
        make_hT(B, tt, 3)
        pT = B["pT"]
        for s0 in range(0, tt, 128):
            ts = min(128, tt - s0)
            st = next_stage(B)
            T.dma(ACT, st.ap[0:ts, 0:PLE], tile["p"][s0:s0 + ts, :], reads=[d_in], writes=[st])
            pt = B["ps_tr"]
            for j in range(PLE // 128):
                T.op(PE, lambda j=j: nc.tensor.transpose(pt.ap[:, j * 128:j * 128 + ts], st.ap[0:ts, j * 128:(j + 1) * 128],
                                                         ident.ap[0:ts, 0:ts]), reads=[st, ident], writes=[pt])
            for j in range(PLE // 128):
                T.op(DVE, lambda j=j: nc.vector.tensor_copy(pT.ap[:, j, s0:s0 + ts], pt.ap[:, j * 128:j * 128 + ts]),
                     reads=[pt], writes=[pT])
        pT_views = [T.view(pT.ap[:, j, :], "pTv") for j in range(PLE // 128)]
        gate_sb = {}

        def post_gate(dt, po):
            g = B["aT"][dt % FPASS]
            T.op(ACT, lambda: nc.scalar.activation(g.ap[:, 0:tt], po.ap[:, 0:tt], AF.Sigmoid), reads=[po], writes=[g])
            gate_sb[dt] = g

        for dg in range(0, D, WCOLS):
            s_, wv = load_w(B, "w_ple_gate", 0, D, dg, WCOLS)
            s2_, wv2 = load_w(B, "w_ple_proj", 0, PLE, dg, WCOLS)
            for j in range(WCOLS // 128):
                dt = dg // 128 + j
                po = next_ps(B)
                for k in range(KC):
                    T.op(PE, lambda k=k: nc.tensor.matmul(po.ap[:, 0:tt], wv[:, k, j * 128:(j + 1) * 128],
                                                          B["hT"][k].ap[:, 0:tt], start=(k == 0), stop=(k == KC - 1)),
                         reads=[s_, B["hT"][k]], writes=[po])
                post_gate(dt, po)
                pq = next_ps(B)
                npk = PLE // 128
                for k in range(npk):
                    T.op(PE, lambda k=k: nc.tensor.matmul(pq.ap[:, 0:tt], wv2[:, k, j * 128:(j + 1) * 128],
                                                          pT.ap[:, k, 0:tt], start=(k == 0), stop=(k == npk - 1)),
                         reads=[s2_, pT], writes=[pq])
                g = gate_sb[dt]
                T.op(DVE, lambda g=g, pq=pq: nc.vector.tensor_tensor(out=g.ap[:, 0:tt], in0=g.ap[:, 0:tt], in1=pq.ap[:, 0:tt],
                                                                     op=ALU.mult), reads=[g, pq], writes=[g])
                xk = B["xT"][dt]
                T.op(POOL, lambda g=g, xk=xk: nc.gpsimd.tensor_tensor(out=xk.ap[:, 0:tt], in0=xk.ap[:, 0:tt], in1=g.ap[:, 0:tt],
                                                                      op=ALU.add), reads=[g, xk], writes=[xk])
        rms_stats(B, tt)
        r = B["rstd"]
        for kc in range(KC):
            xk = B["xT"][kc]
            T.op(DVE, lambda kc=kc, xk=xk: nc.vector.scalar_tensor_tensor(
                out=xk.ap[:, 0:tt], in0=xk.ap[:, 0:tt], scalar=lnT.ap[:, 4 * KC + kc:4 * KC + kc + 1], in1=r.ap[:, 0:tt],
                op0=ALU.mult, op1=ALU.mult), reads=[xk, r, lnT], writes=[xk])
        for s0 in range(0, tt, 128):
            ts = min(128, tt - s0)
            for half in range(0, D, 2048):
                hw = min(2048, D - half)
                st = next_stage(B)
                for k4 in range(0, hw // 128, 4):
                    n4 = min(4, hw // 128 - k4)
                    pt = B["ps_tr"]
                    for j in range(n4):
                        kc = half // 128 + k4 + j
                        T.op(PE, lambda kc=kc, j=j: nc.tensor.transpose(pt.ap[0:ts, j * 128:(j + 1) * 128],
                                                                       B["xT"][kc].ap[:, s0:s0 + ts], ident.ap[:]),
                             reads=[B["xT"][kc], ident], writes=[pt])
                    T.op(ACT, lambda k4=k4, n4=n4: nc.scalar.copy(st.ap[0:ts, k4 * 128:(k4 + n4) * 128], pt.ap[0:ts, 0:n4 * 128]),
                         reads=[pt], writes=[st])
                T.dma(ACT, tile["y"][s0:s0 + ts, half:half + hw], st.ap[0:ts, 0:hw], reads=[st], writes=[d_out], sem_buf=st)

    def sb_attention(tok0, nq_total, TQ, key_srcs, n_new):
        nkeys = sum(s[2] for s in key_srcs)
        nkb = (nkeys + 127) // 128
        n_old = nkeys - n_new
        scale = 128.0 ** -0.5
        T.push()
        maskw = T.sbuf("maskw", [128, 896], F32)
        T.dma(SP, maskw.ap[:], maskw_in[:], reads=[d_in], writes=[maskw])
        ktm = [T.sbuf(f"ktm{i}", [128, nkb, 128], BF16) for i in range(1)]
        qtm = T.sbuf("qtm", [128, (nq_total + 127) // 128, 128], BF16)
        vtm = [T.sbuf(f"vtm{i}", [128, nkb, 128], BF16) for i in range(2)]
        KTb = [T.sbuf(f"KT{i}", [128, nkb * 128], BF16) for i in range(2)]
        QTb = [T.sbuf(f"QT{i}", [128, ((nq_total + 127) // 128) * 128], BF16) for i in range(2)]
        e_sb = [T.sbuf(f"e{i}", [128, TQ], F32) for i in range(2)]
        sp_sb = [T.sbuf(f"sp{i}", [128, TQ], F32) for i in range(2)]
        hi_sb = [T.sbuf(f"hi{i}", [128, TQ], BF16) for i in range(2)]
        lo_sb = [T.sbuf(f"lo{i}", [128, TQ], BF16) for i in range(2)]
        t_sb = [T.sbuf(f"t{i}", [128, TQ], F32) for i in range(2)]
        u_sb = [T.sbuf(f"u{i}", [128, TQ], F32) for i in range(2)]
        A_sb = [T.sbuf(f"A{i}", [128, TQ], BF16) for i in range(2)]
        carry = T.sbuf("carry", [128, TQ], F32)
        o_sb = T.sbuf("o_sb", [128, TQ], F32)
        osq = T.sbuf("osq", [128, TQ], F32)
        orst = T.sbuf("orst", [128, TQ], F32)
        mixo = [T.sbuf(f"mixo{i}", [128, TQ], BF16) for i in range(2)]
        pS = [T.psum(f"pS{i}", [128, 512], F32) for i in range(2)]
        pG = [T.psum(f"pG{i}", [128, 512], F32) for i in range(2)]
        pR = [T.psum(f"pR{i}", [128, 512], F32) for i in range(2)]
        pO = T.psum("pO", [128, 512], F32)
        pT_ = T.psum("pTr", [128, 512], BF16)
        it = 0
        for h in range(SBH):
            KT = KTb[h % 2]; QT = QTb[h % 2]; V = vtm[h % 2]; K_ = ktm[0]
            r = 0
            for (kd, vd, n) in key_srcs:
                nfull = n // 128
                col = slice(h * 128, (h + 1) * 128)
                if nfull:
                    b0 = r // 128
                    T.dma(POOL, K_.ap[:, b0:b0 + nfull, :], kd[0:nfull * 128, col].rearrange("(b p) d -> p b d", p=128),
                          reads=[d_in, d_z, d_kv], writes=[K_])
                    T.dma(POOL, V.ap[:, b0:b0 + nfull, :], vd[0:nfull * 128, col].rearrange("(b p) d -> p b d", p=128),
                          reads=[d_in, d_z, d_kv], writes=[V])
                rem = n - nfull * 128
                if rem:
                    b0 = (r + nfull * 128) // 128
                    T.dma(POOL, K_.ap[0:rem, b0, :], kd[nfull * 128:n, col], reads=[d_in, d_z, d_kv], writes=[K_])
                    T.dma(POOL, V.ap[0:rem, b0, :], vd[nfull * 128:n, col], reads=[d_in, d_z, d_kv], writes=[V])
                r += n
                assert r % 128 == 0 or (kd is key_srcs[-1][0])
            nqb = (nq_total + 127) // 128
            qfull = nq_total // 128
            qcol = slice(c.o_q + h * 128, c.o_q + (h + 1) * 128)
            if qfull:
                T.dma(POOL, qtm.ap[:, 0:qfull, :], z_scr[tok0:tok0 + qfull * 128, qcol].rearrange("(b p) d -> p b d", p=128),
                      reads=[d_z], writes=[qtm])
            if nq_total - qfull * 128:
                rem = nq_total - qfull * 128
                T.dma(POOL, qtm.ap[0:rem, qfull, :], z_scr[tok0 + qfull * 128:tok0 + nq_total, qcol], reads=[d_z], writes=[qtm])

            def transpose_blocks(src, dstT, ntok):
                nb = (ntok + 127) // 128
                for b4 in range(0, nb, 4):
                    n4 = min(4, nb - b4)
                    for j in range(n4):
                        b = b4 + j
                        sz = min(128, ntok - b * 128)
                        T.op(PE, lambda b=b, j=j, sz=sz: nc.tensor.transpose(pT_.ap[:, j * 128:j * 128 + sz], src.ap[0:sz, b, :],
                                                                            identb.ap[0:sz, 0:sz]),
                             reads=[src, identb], writes=[pT_])
                    w = min(n4 * 128, ntok - b4 * 128)
                    T.op(DVE, lambda b4=b4, w=w: nc.vector.tensor_copy(dstT.ap[:, b4 * 128:b4 * 128 + w], pT_.ap[:, 0:w]),
                         reads=[pT_], writes=[dstT])

            lvl = getattr(c, 'sblvl', 9)
            if lvl >= 1:
                transpose_blocks(K_, KT, nkeys)
                transpose_blocks(qtm, QT, nq_total)
            if lvl < 2:
                continue
            for q0 in range(0, nq_total, TQ):
                tq = min(TQ, nq_total - q0)
                kmax = n_old + min(n_new, q0 + tq)
                kbs = [(kb * 128, min(128, kmax - kb * 128)) for kb in range((kmax + 127) // 128)]
                T.op(POOL, lambda: nc.gpsimd.memset(carry.ap[:, 0:tq], 0.0), reads=[], writes=[carry])
                for idx, (k0, ksz) in enumerate(reversed(kbs)):
                    i2 = it % 2
                    it += 1
                    S = pS[i2]; G = pG[i2]; R = pR[i2]
                    e = e_sb[i2]; spb = sp_sb[i2]; tb = t_sb[i2]; ub = u_sb[i2]; Ab = A_sb[i2]
                    first = idx == 0
                    last = idx == len(kbs) - 1
                    T.op(PE, lambda: nc.tensor.matmul(S.ap[0:ksz, 0:tq], KT.ap[:, k0:k0 + ksz], QT.ap[:, q0:q0 + tq],
                                                      start=True, stop=True), reads=[KT, QT], writes=[S])
                    T.op(ACT, lambda: nc.scalar.activation(e.ap[0:ksz, 0:tq], S.ap[0:ksz, 0:tq], AF.Exp, scale=scale),
                         reads=[S], writes=[e])
                    T.op(ACT, lambda: nc.scalar.activation(spb.ap[0:ksz, 0:tq], e.ap[0:ksz, 0:tq], AF.Ln, bias=1.0),
                         reads=[e], writes=[spb])
                    if lvl < 3:
                        continue
                    need_mask = (k0 + ksz - n_old) > q0
                    if need_mask:
                        off = (k0 - n_old) - q0
                        mslice = maskw.ap[0:ksz, 384 - off:384 - off + tq]
                        assert 0 <= 384 - off and 384 - off + tq <= 896, (off, tq)
                        T.op(POOL, lambda: nc.gpsimd.tensor_tensor(out=spb.ap[0:ksz, 0:tq], in0=spb.ap[0:ksz, 0:tq], in1=mslice,
                                                                   op=ALU.mult), reads=[spb, maskw], writes=[spb])
                    if lvl < 4:
                        continue
                    hib = hi_sb[i2]; lob = lo_sb[i2]
                    T.op(POOL, lambda: nc.gpsimd.tensor_copy(hib.ap[0:ksz, 0:tq], spb.ap[0:ksz, 0:tq]), reads=[spb], writes=[hib])
                    T.op(POOL, lambda: nc.gpsimd.tensor_tensor(out=lob.ap[0:ksz, 0:tq], in0=spb.ap[0:ksz, 0:tq], in1=hib.ap[0:ksz, 0:tq],
                                                               op=ALU.subtract), reads=[spb, hib], writes=[lob])
                    T.op(PE, lambda: nc.tensor.matmul(G.ap[0:ksz, 0:tq], trib.ap[0:ksz, 0:ksz], hib.ap[0:ksz, 0:tq],
                                                      start=True, stop=False), reads=[trib, hib], writes=[G])
                    T.op(PE, lambda: nc.tensor.matmul(G.ap[0:ksz, 0:tq], trib.ap[0:ksz, 0:ksz], lob.ap[0:ksz, 0:tq],
                                                      start=False, stop=True), reads=[trib, lob], writes=[G])
                    if not last:
                        T.op(PE, lambda: nc.tensor.matmul(R.ap[:, 0:tq], onesb.ap[0:ksz, :], hib.ap[0:ksz, 0:tq],
                                                          start=True, stop=False), reads=[onesb, hib], writes=[R])
                        T.op(PE, lambda: nc.tensor.matmul(R.ap[:, 0:tq], onesb.ap[0:ksz, :], lob.ap[0:ksz, 0:tq],
                                                          start=False, stop=True), reads=[onesb, lob], writes=[R])
                    if lvl < 5:
                        continue
                    T.op(DVE, lambda: nc.vector.scalar_tensor_tensor(out=tb.ap[0:ksz, 0:tq], in0=S.ap[0:ksz, 0:tq], scalar=scale,
                                                                     in1=carry.ap[0:ksz, 0:tq], op0=ALU.mult, op1=ALU.subtract),
                         reads=[S, carry], writes=[tb])
                    if lvl < 5.1:
                        continue
                    T.op(DVE, lambda: nc.vector.tensor_tensor(out=ub.ap[0:ksz, 0:tq], in0=tb.ap[0:ksz, 0:tq], in1=G.ap[0:ksz, 0:tq],
                                                              op=ALU.subtract), reads=[tb, G], writes=[ub])
                    if lvl < 5.2:
                        continue
                    T.op(ACT, lambda: nc.scalar.activation(Ab.ap[0:ksz, 0:tq], ub.ap[0:ksz, 0:tq], AF.Exp), reads=[ub], writes=[Ab])
                    if lvl < 6:
                        continue
                    if need_mask:
                        T.op(POOL, lambda: nc.gpsimd.tensor_tensor(out=Ab.ap[0:ksz, 0:tq], in0=Ab.ap[0:ksz, 0:tq], in1=mslice,
                                                                   op=ALU.mult), reads=[Ab, maskw], writes=[Ab])
                    if lvl < 7:
                        continue
                    if not last:
                        T.op(DVE, lambda: nc.vector.tensor_tensor(out=carry.ap[:, 0:tq], in0=carry.ap[:, 0:tq], in1=R.ap[:, 0:tq],
                                                                  op=ALU.add), reads=[carry, R], writes=[carry])
                    T.op(PE, lambda: nc.tensor.matmul(pO.ap[:, 0:tq], V.ap[0:ksz, k0 // 128, :], Ab.ap[0:ksz, 0:tq],
                                                      start=first, stop=last), reads=[V, Ab], writes=[pO])
                if lvl < 8:
                    continue
                T.op(DVE, lambda: nc.vector.tensor_copy(o_sb.ap[:, 0:tq], pO.ap[:, 0:tq]), reads=[pO], writes=[o_sb])
                T.op(ACT, lambda: nc.scalar.activation(osq.ap[:, 0:tq], o_sb.ap[:, 0:tq], AF.Square), reads=[o_sb], writes=[osq])
                G = pG[it % 2]
                T.op(PE, lambda: nc.tensor.matmul(G.ap[:, 0:tq], ones.ap[:], osq.ap[:, 0:tq], start=True, stop=True),
                     reads=[ones, osq], writes=[G])
                T.op(ACT, lambda: nc.scalar.activation(orst.ap[:, 0:tq], G.ap[:, 0:tq], AF.Sqrt, bias=EPS, scale=1.0 / 128),
                     reads=[G], writes=[orst])
                T.op(DVE, lambda: nc.vector.reciprocal(orst.ap[:, 0:tq], orst.ap[:, 0:tq]), reads=[orst], writes=[orst])
                mo = mixo[(q0 // TQ) % 2]
                T.op(DVE, lambda: nc.vector.scalar_tensor_tensor(out=mo.ap[:, 0:tq], in0=o_sb.ap[:, 0:tq], scalar=gsbT.ap[:, h:h + 1],
                                                                 in1=orst.ap[:, 0:tq], op0=ALU.mult, op1=ALU.mult),
                     reads=[o_sb, orst, gsbT], writes=[mo])
                T.dma(SP, mixT_scr[h * 128:(h + 1) * 128, tok0 + q0:tok0 + q0 + tq], mo.ap[:, 0:tq], reads=[mo], writes=[d_mix],
                      sem_buf=mo)
        T.pop()

    def mlstm(tok0, L, nchunk, init):
        NT = L * nchunk
        SEGC = min(16, nchunk)
        SEGT = SEGC * L
        NV = DV // 128
        T.push()
        sel = T.sbuf("sel", [MLH, MLH * 128], F32)
        cmask = T.sbuf("cmask", [64, 64], F32)
        bi = T.sbuf("bi", [MLH, 1], F32); bfb = T.sbuf("bfb", [MLH, 1], F32)
        gml = T.sbuf("gml", [64, MLH * DV], F32)
        for dst, src in ((sel, sel_in), (cmask, cmask_in), (bi, bi_in), (bfb, bf_in), (gml, gml_in)):
            T.dma(SP, dst.ap[:], src[:], reads=[d_in], writes=[dst], sem_buf=sel)
        Mx = T.sbuf("Mx", [MLH, NT + 1], F32)
        m0 = T.sbuf("m0", [MLH, 1], F32)
        mout = T.sbuf("mout", [MLH, 1], F32)
        acol = T.sbuf("acol", [64, nchunk, MLH], F32)
        ccol = T.sbuf("ccol", [64, nchunk, MLH], F32)
        pg = T.psum("pg", [128, 512], F32)
        T.push()
        nblk = (NT + 127) // 128
        gtm = T.sbuf("gtm", [128, nblk, 2 * MLH], F32)
        nfull = NT // 128
        gcols = slice(c.o_g, c.o_g + 2 * MLH)
        if nfull:
            T.dma(SP, gtm.ap[:, 0:nfull, :], z_scr[tok0:tok0 + nfull * 128, gcols].rearrange("(b p) g -> p b g", p=128),
                  reads=[d_z], writes=[gtm])
        if NT - nfull * 128:
            T.dma(SP, gtm.ap[0:NT - nfull * 128, nfull, :], z_scr[tok0 + nfull * 128:tok0 + NT, gcols], reads=[d_z], writes=[gtm])
        GI = T.sbuf("GI", [MLH, NT], F32); GF = T.sbuf("GF", [MLH, NT], F32)
        Bc = T.sbuf("Bc", [MLH, NT], F32); av = T.sbuf("av", [MLH, NT], F32)
        nbm = T.sbuf("nbm", [MLH, NT], F32)
        onesr = T.sbuf("onesr", [MLH, NT], F32)
        T.op(POOL, lambda: nc.gpsimd.memset(onesr.ap[:], 1.0), writes=[onesr])
        if init is None:
            T.op(POOL, lambda: nc.gpsimd.memset(m0.ap[:], 0.0), writes=[m0])
        else:
            T.dma(SP, m0.ap[:], init[2][:], reads=[d_in], writes=[m0])
        for b in range(nblk):
            sz = min(128, NT - b * 128)
            T.op(PE, lambda b=b, sz=sz: nc.tensor.transpose(pg.ap[0:MLH, 0:sz], gtm.ap[0:sz, b, 0:MLH], ident.ap[0:sz, 0:sz]),
                 reads=[gtm, ident], writes=[pg])
            T.op(PE, lambda b=b, sz=sz: nc.tensor.transpose(pg.ap[0:MLH, 128:128 + sz], gtm.ap[0:sz, b, MLH:2 * MLH],
                                                           ident.ap[0:sz, 0:sz]), reads=[gtm, ident], writes=[pg])
            T.op(DVE, lambda b=b, sz=sz: nc.vector.tensor_copy(GI.ap[:, b * 128:b * 128 + sz], pg.ap[0:MLH, 0:sz]),
                 reads=[pg], writes=[GI])
            T.op(DVE, lambda b=b, sz=sz: nc.vector.tensor_copy(GF.ap[:, b * 128:b * 128 + sz], pg.ap[0:MLH, 128:128 + sz]),
                 reads=[pg], writes=[GF])
        T.op(DVE, lambda: nc.vector.tensor_scalar(out=GF.ap[:], in0=GF.ap[:], scalar1=bfb.ap[:, 0:1], scalar2=-1.0,
                                                  op0=ALU.add, op1=ALU.mult), reads=[GF, bfb], writes=[GF])
        T.op(ACT, lambda: nc.scalar.activation(GF.ap[:], GF.ap[:], AF.Exp), reads=[GF], writes=[GF])
        T.op(ACT, lambda: nc.scalar.activation(GF.ap[:], GF.ap[:], AF.Ln, bias=1.0), reads=[GF], writes=[GF])
        T.op(DVE, lambda: nc.vector.tensor_tensor_scan(out=Bc.ap[:], data0=onesr.ap[:], data1=GF.ap[:], initial=0.0,
                                                       op0=ALU.mult, op1=ALU.add), reads=[onesr, GF], writes=[Bc])
        T.op(DVE, lambda: nc.vector.scalar_tensor_tensor(out=av.ap[:], in0=GI.ap[:], scalar=bi.ap[:, 0:1], in1=Bc.ap[:],
                                                         op0=ALU.add, op1=ALU.add), reads=[GI, bi, Bc], writes=[av])
        T.op(DVE, lambda: nc.vector.tensor_copy(Mx.ap[:, 0:1], m0.ap[:]), reads=[m0], writes=[Mx])
        T.op(DVE, lambda: nc.vector.tensor_tensor_scan(out=Mx.ap[:, 1:NT + 1], data0=onesr.ap[:], data1=av.ap[:], initial=m0.ap[:, 0:1],
                                                       op0=ALU.mult, op1=ALU.max), reads=[onesr, av, m0, Mx], writes=[Mx])
        T.op(DVE, lambda: nc.vector.tensor_tensor(out=nbm.ap[:], in0=Bc.ap[:], in1=Mx.ap[:, 1:NT + 1], op=ALU.subtract),
             reads=[Bc, Mx], writes=[nbm])
        T.op(DVE, lambda: nc.vector.tensor_scalar(out=mout.ap[:], in0=nbm.ap[:, NT - 1:NT], scalar1=-1.0, scalar2=None, op0=ALU.mult),
             reads=[nbm], writes=[mout])
        out_m = mp_o if init is None else ms_o
        T.dma(SP, out_m[:], mout.ap[:], reads=[mout], writes=[d_out], sem_buf=mout)
        for ch in range(nchunk):
            T.op(PE, lambda ch=ch: nc.tensor.transpose(pg.ap[0:L, 0:MLH], av.ap[:, ch * L:(ch + 1) * L], ident.ap[0:MLH, 0:MLH]),
                 reads=[av, ident], writes=[pg])
            T.op(PE, lambda ch=ch: nc.tensor.transpose(pg.ap[0:L, 128:128 + MLH], nbm.ap[:, ch * L:(ch + 1) * L], ident.ap[0:MLH, 0:MLH]),
                 reads=[nbm, ident], writes=[pg])
            T.op(DVE, lambda ch=ch: nc.vector.tensor_copy(acol.ap[0:L, ch, :], pg.ap[0:L, 0:MLH]), reads=[pg], writes=[acol])
            T.op(ACT, lambda ch=ch: nc.scalar.activation(ccol.ap[0:L, ch, :], pg.ap[0:L, 128:128 + MLH], AF.Exp), reads=[pg], writes=[ccol])
        T.pop()
        Mb = T.sbuf("Mb", [128, SEGT + 1], F32)
        NMb = T.sbuf("NMb", [128, SEGT + 1], F32)
        qtm = T.sbuf("mq_tm", [L, SEGC, DQK], BF16)
        ktm = T.sbuf("mk_tm", [L, SEGC, DQK], BF16)
        vtm = T.sbuf("mv_tm", [L, SEGC, DV], BF16)
        qT = T.sbuf("mqT", [128, 2, SEGT], BF16)
        kT = T.sbuf("mkT", [128, 2, SEGT], BF16)
        mixml = T.sbuf("mixml", [128, NV, SEGT], BF16)
        CT = T.sbuf("CT", [128, 2, DV], F32)
        CTb = T.sbuf("CTb", [128, 2, DV], BF16)
        ncol = T.sbuf("ncol", [128, 2], F32)
        ncolb = T.sbuf("ncolb", [128, 2], BF16)
        cstage = T.sbuf("cstage", [128, NV, DQK], F32)
        WT = [T.sbuf(f"WT{i}", [64, 64], F32) for i in range(2)]
        sT = [T.sbuf(f"sT{i}", [64, 64], BF16) for i in range(2)]
        wib = [T.sbuf(f"wib{i}", [128, 64], F32) for i in range(2)]
        qs = [T.sbuf(f"qs{i}", [128, 2, 64], BF16) for i in range(2)]
        wtok = [T.sbuf(f"wtok{i}", [64, 1], F32) for i in range(2)]
        wprev = [T.sbuf(f"wprev{i}", [128, 1], F32) for i in range(2)]
        kst = [T.sbuf(f"kst{i}", [64, DQK], BF16) for i in range(2)]
        rr = [T.sbuf(f"rr{i}", [64, 2], F32) for i in range(2)]
        hsb = [T.sbuf(f"hsb{i}", [64, DV], F32) for i in range(2)]
        hsq = T.sbuf("hsq", [64, DV], F32)
        osb = [T.sbuf(f"osb{i}", [64, DV], F32) for i in range(2)]
        ymb = [T.sbuf(f"ymb{i}", [64, DV], BF16) for i in range(2)]
        p_qk = T.psum("p_qk", [128, 512], F32)
        p_num = T.psum("p_num", [128, 512], F32)
        p_den = T.psum("p_den", [128, 512], F32)
        p_up = [T.psum(f"p_up{i}", [128, 512], F32) for i in range(2)]
        p_trb = T.psum("p_trb", [128, 512], BF16)
        qscale = float(DQK) ** -0.5
        out_c, out_n = (cp_o, np_o) if init is None else (cs_o, ns_o)
        for h in range(MLH):
            if init is None:
                T.op(POOL, lambda: nc.gpsimd.memset(CT.ap[:], 0.0), writes=[CT])
                T.op(POOL, lambda: nc.gpsimd.memset(CTb.ap[:], 0.0), writes=[CTb])
                T.op(POOL, lambda: nc.gpsimd.memset(ncol.ap[:], 0.0), writes=[ncol])
                T.op(POOL, lambda: nc.gpsimd.memset(ncolb.ap[:], 0.0), writes=[ncolb])
            else:
                T.dma(SP, cstage.ap[:], init[0][h * DV:(h + 1) * DV, :].rearrange("(a p) d -> p a d", p=128), reads=[d_in], writes=[cstage])
                for dc in range(2):
                    for vc in range(NV):
                        T.op(PE, lambda dc=dc, vc=vc: nc.tensor.transpose(pg.ap[:, vc * 128:(vc + 1) * 128],
                                                                         cstage.ap[:, vc, dc * 128:(dc + 1) * 128], ident.ap[:]),
                             reads=[cstage, ident], writes=[pg])
                    T.op(DVE, lambda dc=dc: nc.vector.tensor_copy(CT.ap[:, dc, :], pg.ap[:, 0:DV]), reads=[pg], writes=[CT])
                    T.op(ACT, lambda dc=dc: nc.scalar.copy(CTb.ap[:, dc, :], pg.ap[:, 0:DV]), reads=[pg], writes=[CTb])
                T.dma(SP, ncol.ap[:], init[1][:, 2 * h:2 * h + 2], reads=[d_in], writes=[ncol])
                T.op(DVE, lambda: nc.vector.tensor_copy(ncolb.ap[:], ncol.ap[:]), reads=[ncol], writes=[ncolb])
            for seg0c in range(0, nchunk, SEGC):
                nsc = min(SEGC, nchunk - seg0c)
                s0 = seg0c * L
                st_ = nsc * L
                for c0 in range(0, st_ + 1, 512):
                    w = min(512, st_ + 1 - c0)
                    T.op(PE, lambda c0=c0, w=w: nc.tensor.matmul(pg.ap[:, 0:w], sel.ap[:, h * 128:(h + 1) * 128], Mx.ap[:, s0 + c0:s0 + c0 + w],
                                                                 start=True, stop=True), reads=[sel, Mx], writes=[pg])
                    T.op(DVE, lambda c0=c0, w=w: nc.vector.tensor_copy(Mb.ap[:, c0:c0 + w], pg.ap[:, 0:w]), reads=[pg], writes=[Mb])
                    T.op(ACT, lambda c0=c0, w=w: nc.scalar.mul(NMb.ap[:, c0:c0 + w], pg.ap[:, 0:w], -1.0), reads=[pg], writes=[NMb])
                r0 = tok0 + s0
                T.dma(POOL, qtm.ap[:, 0:nsc, :], z_scr[r0:r0 + st_, c.o_mq + h * DQK:c.o_mq + (h + 1) * DQK].rearrange("(c p) d -> p c d", p=L),
                      reads=[d_z], writes=[qtm])
                T.dma(POOL, ktm.ap[:, 0:nsc, :], z_scr[r0:r0 + st_, c.o_mk + h * DQK:c.o_mk + (h + 1) * DQK].rearrange("(c p) d -> p c d", p=L),
                      reads=[d_z], writes=[ktm])
                T.dma(POOL, vtm.ap[:, 0:nsc, :], z_scr[r0:r0 + st_, c.o_mv + h * DV:c.o_mv + (h + 1) * DV].rearrange("(c p) d -> p c d", p=L),
                      reads=[d_z], writes=[vtm])
                for (src, dst, scl) in ((qtm, qT, qscale), (ktm, kT, 1.0)):
                    for ch in range(nsc):
                        for dc in range(2):
                            T.op(PE, lambda ch=ch, dc=dc, src=src: nc.tensor.transpose(
                                p_trb.ap[:, dc * 64:dc * 64 + L], src.ap[0:L, ch, dc * 128:(dc + 1) * 128], identb.ap[0:L, 0:L]),
                                reads=[src, identb], writes=[p_trb])
                        T.op(ACT, lambda ch=ch, dst=dst, scl=scl: nc.scalar.mul(
                            dst.ap[:, :, ch * L:(ch + 1) * L], p_trb.ap[:, 0:128].rearrange("p (a b) -> p a b", a=2)[:, :, 0:L], scl),
                            reads=[p_trb], writes=[dst])
                for chl in range(nsc):
                    ch = seg0c + chl
                    i2 = ch % 2
                    t0 = chl * L
                    cs = slice(t0, t0 + L)
                    T.op(ACT, lambda: nc.scalar.activation(WT[i2].ap[0:L, 0:L], NMb.ap[0:L, 1 + t0:1 + t0 + L], AF.Exp,
                                                           bias=acol.ap[0:L, ch, h:h + 1]), reads=[NMb, acol], writes=[WT[i2]])
                    T.op(POOL, lambda: nc.gpsimd.tensor_tensor(out=WT[i2].ap[0:L, 0:L], in0=WT[i2].ap[0:L, 0:L], in1=cmask.ap[0:L, 0:L],
                                                               op=ALU.mult), reads=[WT[i2], cmask], writes=[WT[i2]])
                    T.op(ACT, lambda: nc.scalar.activation(wib[i2].ap[:, 0:L], NMb.ap[:, 1 + t0:1 + t0 + L], AF.Exp,
                                                           bias=Mb.ap[:, t0:t0 + 1]), reads=[NMb, Mb], writes=[wib[i2]])
                    T.op(ACT, lambda: nc.scalar.activation(wtok[i2].ap[0:L, :], acol.ap[0:L, ch, h:h + 1], AF.Exp,
                                                           bias=NMb.ap[0:L, t0 + L:t0 + L + 1]), reads=[NMb, acol], writes=[wtok[i2]])
                    T.op(ACT, lambda: nc.scalar.activation(wprev[i2].ap[:], NMb.ap[:, t0 + L:t0 + L + 1], AF.Exp,
                                                           bias=Mb.ap[:, t0:t0 + 1]), reads=[NMb, Mb], writes=[wprev[i2]])
                    for dc in range(2):
                        T.op(POOL, lambda dc=dc: nc.gpsimd.tensor_tensor(out=qs[i2].ap[:, dc, 0:L], in0=qT.ap[:, dc, cs], in1=wib[i2].ap[:, 0:L],
                                                                         op=ALU.mult), reads=[qT, wib[i2]], writes=[qs[i2]])
                    T.op(POOL, lambda: nc.gpsimd.tensor_scalar(out=kst[i2].ap[0:L, :], in0=ktm.ap[0:L, chl, :], scalar1=wtok[i2].ap[0:L, 0:1],
                                                               scalar2=None, op0=ALU.mult), reads=[ktm, wtok[i2]], writes=[kst[i2]])
                    for dc in range(2):
                        T.op(PE, lambda dc=dc: nc.tensor.matmul(p_qk.ap[0:L, 0:L], kT.ap[:, dc, cs], qT.ap[:, dc, cs], start=(dc == 0), stop=(dc == 1)),
                             reads=[kT, qT], writes=[p_qk])
                    T.op(DVE, lambda: nc.vector.tensor_tensor(out=sT[i2].ap[0:L, 0:L], in0=p_qk.ap[0:L, 0:L], in1=WT[i2].ap[0:L, 0:L], op=ALU.mult),
                         reads=[p_qk, WT[i2]], writes=[sT[i2]])
                    for dc in range(2):
                        T.op(PE, lambda dc=dc: nc.tensor.matmul(p_num.ap[0:L, 0:DV], qs[i2].ap[:, dc, 0:L], CTb.ap[:, dc, :], start=(dc == 0), stop=False),
                             reads=[qs[i2], CTb], writes=[p_num])
                    T.op(PE, lambda: nc.tensor.matmul(p_num.ap[0:L, 0:DV], sT[i2].ap[0:L, 0:L], vtm.ap[0:L, chl, :], start=False, stop=True),
                         reads=[sT[i2], vtm], writes=[p_num])
                    for dc in range(2):
                        T.op(PE, lambda dc=dc: nc.tensor.matmul(p_den.ap[0:L, 0:1], qs[i2].ap[:, dc, 0:L], ncolb.ap[:, dc:dc + 1], start=(dc == 0), stop=False),
                             reads=[qs[i2], ncolb], writes=[p_den])
                    T.op(PE, lambda: nc.tensor.matmul(p_den.ap[0:L, 0:1], sT[i2].ap[0:L, 0:L], onesb.ap[0:L, 0:1], start=False, stop=True),
                         reads=[sT[i2], onesb], writes=[p_den])
                    for dc in range(2):
                        T.op(PE, lambda dc=dc: nc.tensor.matmul(p_up[dc].ap[:, 0:DV], kst[i2].ap[0:L, dc * 128:(dc + 1) * 128], vtm.ap[0:L, chl, :],
                                                                start=True, stop=True), reads=[kst[i2], vtm], writes=[p_up[dc]])
                    for dc in range(2):
                        T.op(PE, lambda dc=dc: nc.tensor.matmul(p_den.ap[:, 8 + dc:9 + dc], kst[i2].ap[0:L, dc * 128:(dc + 1) * 128], onesb.ap[0:L, 0:1],
                                                                start=True, stop=True), reads=[kst[i2], onesb], writes=[p_den])
                    T.op(DVE, lambda: nc.vector.tensor_scalar(out=rr[i2].ap[0:L, 0:1], in0=p_den.ap[0:L, 0:1], scalar1=-1.0, scalar2=None, op0=ALU.mult),
                         reads=[p_den], writes=[rr[i2]])
                    T.op(DVE, lambda: nc.vector.tensor_tensor(out=rr[i2].ap[0:L, 0:1], in0=rr[i2].ap[0:L, 0:1], in1=p_den.ap[0:L, 0:1], op=ALU.max),
                         reads=[rr[i2], p_den], writes=[rr[i2]])
                    T.op(DVE, lambda: nc.vector.tensor_tensor(out=rr[i2].ap[0:L, 0:1], in0=rr[i2].ap[0:L, 0:1], in1=ccol.ap[0:L, ch, h:h + 1], op=ALU.max),
                         reads=[rr[i2], ccol], writes=[rr[i2]])
                    T.op(DVE, lambda: nc.vector.reciprocal(rr[i2].ap[0:L, 0:1], rr[i2].ap[0:L, 0:1]), reads=[rr[i2]], writes=[rr[i2]])
                    T.op(DVE, lambda: nc.vector.tensor_scalar(out=hsb[i2].ap[0:L, :], in0=p_num.ap[0:L, 0:DV], scalar1=rr[i2].ap[0:L, 0:1],
                                                              scalar2=None, op0=ALU.mult), reads=[p_num, rr[i2]], writes=[hsb[i2]])
                    for dc in range(2):
                        T.op(DVE, lambda dc=dc: nc.vector.scalar_tensor_tensor(out=CT.ap[:, dc, :], in0=CT.ap[:, dc, :], scalar=wprev[i2].ap[:, 0:1],
                                                                               in1=p_up[dc].ap[:, 0:DV], op0=ALU.mult, op1=ALU.add),
                             reads=[CT, wprev[i2], p_up[dc]], writes=[CT])
                    T.op(ACT, lambda: nc.scalar.copy(CTb.ap[:], CT.ap[:]), reads=[CT], writes=[CTb])
                    T.op(DVE, lambda: nc.vector.scalar_tensor_tensor(out=ncol.ap[:], in0=ncol.ap[:], scalar=wprev[i2].ap[:, 0:1],
                                                                     in1=p_den.ap[:, 8:10], op0=ALU.mult, op1=ALU.add),
                         reads=[ncol, wprev[i2], p_den], writes=[ncol])
                    T.op(DVE, lambda: nc.vector.tensor_copy(ncolb.ap[:], ncol.ap[:]), reads=[ncol], writes=[ncolb])
                    T.op(ACT, lambda: nc.scalar.activation(hsq.ap[0:L, :], hsb[i2].ap[0:L, :], AF.Square, accum_out=rr[i2].ap[0:L, 1:2]),
                         reads=[hsb[i2]], writes=[hsq, rr[i2]])
                    T.op(ACT, lambda: nc.scalar.activation(rr[i2].ap[0:L, 1:2], rr[i2].ap[0:L, 1:2], AF.Sqrt, bias=EPS, scale=1.0 / DV),
                         reads=[rr[i2]], writes=[rr[i2]])
                    T.op(DVE, lambda: nc.vector.reciprocal(rr[i2].ap[0:L, 1:2], rr[i2].ap[0:L, 1:2]), reads=[rr[i2]], writes=[rr[i2]])
                    T.dma(SP, osb[i2].ap[0:L, :], z_scr[r0 + t0:r0 + t0 + L, c.o_mo + h * DV:c.o_mo + (h + 1) * DV], reads=[d_z], writes=[osb[i2]])
                    T.op(ACT, lambda: nc.scalar.activation(osb[i2].ap[0:L, :], osb[i2].ap[0:L, :], AF.Sigmoid), reads=[osb[i2]], writes=[osb[i2]])
                    T.op(POOL, lambda: nc.gpsimd.tensor_tensor(out=osb[i2].ap[0:L, :], in0=osb[i2].ap[0:L, :], in1=gml.ap[0:L, h * DV:(h + 1) * DV],
                                                               op=ALU.mult), reads=[osb[i2], gml], writes=[osb[i2]])
                    T.op(DVE, lambda: nc.vector.scalar_tensor_tensor(out=ymb[i2].ap[0:L, :], in0=hsb[i2].ap[0:L, :], scalar=rr[i2].ap[0:L, 1:2],
                                                                     in1=osb[i2].ap[0:L, :], op0=ALU.mult, op1=ALU.mult),
                         reads=[hsb[i2], rr[i2], osb[i2]], writes=[ymb[i2]])
                    for vc in range(NV):
                        T.op(PE, lambda vc=vc: nc.tensor.transpose(p_trb.ap[:, 128 + vc * 64:128 + vc * 64 + L], ymb[i2].ap[0:L, vc * 128:(vc + 1) * 128],
                                                                   identb.ap[0:L, 0:L]), reads=[ymb[i2], identb], writes=[p_trb])
                    T.op(ACT, lambda: nc.scalar.copy(mixml.ap[:, :, cs], p_trb.ap[:, 128:128 + 64 * NV].rearrange("p (a b) -> p a b", b=64)[:, :, 0:L]),
                         reads=[p_trb], writes=[mixml])
                for vc in range(NV):
                    row = SBW + h * DV + vc * 128
                    T.dma(SP, mixT_scr[row:row + 128, r0:r0 + st_], mixml.ap[:, vc, 0:st_], reads=[mixml], writes=[d_mix], sem_buf=mixml)
            for vc in range(NV):
                for dc in range(2):
                    T.op(PE, lambda vc=vc, dc=dc: nc.tensor.transpose(pg.ap[:, dc * 128:(dc + 1) * 128], CT.ap[:, dc, vc * 128:(vc + 1) * 128], ident.ap[:]),
                         reads=[CT, ident], writes=[pg])
                T.op(DVE, lambda vc=vc: nc.vector.tensor_copy(cstage.ap[:, vc, :], pg.ap[:, 0:DQK]), reads=[pg], writes=[cstage])
            T.dma(SP, out_c[h * DV:(h + 1) * DV, :].rearrange("(a p) d -> p a d", p=128), cstage.ap[:], reads=[cstage], writes=[d_out], sem_buf=cstage)
            T.dma(SP, out_n[:, 2 * h:2 * h + 2], ncol.ap[:], reads=[ncol], writes=[d_out], sem_buf=ncol)
        T.pop()

    st = c.stages
    if "A" in st:
        T.push()
        B = phase_AC_buffers()
        for tile in tiles:
            phase_A(B, tile)
        T.pop()
    if "SBP" in st:
        sb_attention(0, SEQ, 512, [(z_scr[0:SEQ, c.o_k:c.o_k + SBW], z_scr[0:SEQ, c.o_v:c.o_v + SBW], SEQ)], SEQ)
    if "MLP" in st:
        mlstm(0, c.CHUNK, SEQ // c.CHUNK, None)
    if "SBS" in st:
        sb_attention(SEQ, NS, NS, [(ck, cv, PAST), (z_scr[SEQ:SEQ + NS, c.o_k:c.o_k + SBW], z_scr[SEQ:SEQ + NS, c.o_v:c.o_v + SBW], NS)], NS)
    if "MLS" in st:
        mlstm(SEQ, NS, 1, (sc_in, sn_in, sm_in))
    if "C" in st:
        T.push()
        B = phase_AC_buffers()
        for tile in tilesC:
            phase_C(B, tile)
        T.pop()
    T.finish()
    return nc, T


def _consts(cfg):
    MLH = cfg.MLH
    k = np.arange(128)
    tri = (k[:, None] >= k[None, :]).astype(np.float32)
    j = np.arange(896)
    maskw = ((j[None, :] - 384) > k[:, None]).astype(np.float32)
    s = np.arange(64)
    cmask = (s[:, None] <= s[None, :]).astype(np.float32)
    sel = np.zeros((MLH, MLH * 128), np.float32)
    for h in range(MLH):
        sel[h, h * 128:(h + 1) * 128] = 1.0
    return dict(ident=np.eye(128, dtype=np.float32), ones=np.ones((128, 128), np.float32), tri=tri, maskw=maskw,
                cmask=cmask, sel=sel)


def make_in_maps(cfg, inp, n_cores):
    c = cfg
    f = lambda a: np.ascontiguousarray(np.asarray(a, dtype=np.float32))
    B = inp["x_prompt"].shape[0]
    lns = [inp["ln_ffn1"][0], inp["ln_mix"][0], inp["ln_ffn2"][0], inp["ln_ple"][0], inp["ln_final"]]
    lnT = np.concatenate([f(v).reshape(c.KC, 128).T for v in lns], axis=1)
    common = dict(lnT=f(lnT), gsbT=f(f(inp["g_sb_head"][0]).T),
                  gml=f(np.broadcast_to(f(inp["g_ml_head"][0]).reshape(1, -1), (64, c.MLH * c.DV))),
                  bi=f(f(inp["b_if"][0])[:c.MLH].reshape(c.MLH, 1)), bf=f(f(inp["b_if"][0])[c.MLH:].reshape(c.MLH, 1)))
    common.update(_consts(c))
    for w in WEIGHTS:
        common[w] = f(inp[w][0])
    maps = []
    for core in range(n_cores):
        pb = core % B
        sb = core % inp["x_sample"].shape[0]
        m = dict(common)
        rank = (core // B) % c.NRANK
        nq = c.SEQ // c.NRANK
        tc_ = min(c.TT, nq)
        nj, nct = c.SEQ // tc_, nq // tc_
        m["xp"] = f(inp["x_prompt"][pb]); m["xs"] = f(inp["x_sample"][sb])
        m["ppq"] = f(inp["p_prompt"][0, pb, rank * nq:(rank + 1) * nq]); m["ps"] = f(inp["p_sample"][0, sb])
        selq = np.zeros((128, nct * nj), np.float32)
        for i in range(nct):
            selq[:, i * nj + rank * nct + i] = 1.0
        m["selq"] = selq
        m["ck"] = f(inp["cache_sb_k"][0, sb]).reshape(c.PAST, c.SBW)
        m["cv"] = f(inp["cache_sb_v"][0, sb]).reshape(c.PAST, c.SBW)
        m["sc"] = f(inp["state_ml_c"][0, sb]).reshape(c.MLH * c.DV, c.DQK)
        m["sn"] = f(f(inp["state_ml_n"][0, sb]).reshape(c.MLH * 2, 128).T)
        m["sm"] = f(inp["state_ml_m"][0, sb]).reshape(c.MLH, 1)
        maps.append(m)
    return maps


def assemble(cfg, res, B, DB):
    c = cfg
    r = res

    def n_un(a):
        return np.ascontiguousarray(a.T).reshape(c.MLH, c.DQK)

    yp = np.stack([np.concatenate([r[b + B * k]["ypq"] for k in range(c.NRANK)], axis=0) for b in range(B)])
    ys = np.stack([r[b]["ys"] for b in range(DB)])
    kp = np.stack([r[b]["kp"].reshape(c.SEQ, c.SBH, 128) for b in range(B)])[None]
    vp = np.stack([r[b]["vp"].reshape(c.SEQ, c.SBH, 128) for b in range(B)])[None]
    cp = np.stack([r[b]["cp"].reshape(c.MLH, c.DV, c.DQK) for b in range(B)])[None]
    np_ = np.stack([n_un(r[b]["np"]) for b in range(B)])[None]
    mp = np.stack([r[b]["mp"].reshape(c.MLH) for b in range(B)])[None]
    ks = np.stack([r[b]["ks"].reshape(c.DEC_SEQ, c.SBH, 128) for b in range(DB)])[None]
    vs = np.stack([r[b]["vs"].reshape(c.DEC_SEQ, c.SBH, 128) for b in range(DB)])[None]
    cs = np.stack([r[b]["cs"].reshape(c.MLH, c.DV, c.DQK) for b in range(DB)])[None]
    ns = np.stack([n_un(r[b]["ns"]) for b in range(DB)])[None]
    ms = np.stack([r[b]["ms"].reshape(c.MLH) for b in range(DB)])[None]
    return tuple(np.ascontiguousarray(a, dtype=np.float32) for a in (yp, ys, kp, vp, cp, np_, mp, ks, vs, cs, ns, ms))


def kernel(**inputs):
    cfg = Cfg()
    n = 8
    nc, _ = build(cfg)
    in_maps = make_in_maps(cfg, inputs, n)
    res = run_bass_kernel_spmd(nc, in_maps, core_ids=list(range(n)))
    return assemble(cfg, res.results, inputs["x_prompt"].shape[0], inputs["x_sample"].shape[0])
```

```python
import numpy as np
from contextlib import ExitStack
import concourse.bass as bass
import concourse.mybir as mybir
from concourse.bass_utils import run_bass_kernel_spmd

F32 = mybir.dt.float32
BF16 = mybir.dt.bfloat16
AF = mybir.ActivationFunctionType
ALU = mybir.AluOpType
AX = mybir.AxisListType
EPS = 1e-6


class _Sem:
    def __init__(self, handle, name):
        self.handle = handle
        self.name = name
        self.issued = 0


class _Ev:
    __slots__ = ("sem", "count", "is_dma")

    def __init__(self, sem, count, is_dma):
        self.sem = sem
        self.count = count
        self.is_dma = is_dma


class Buf:
    def __init__(self, ap, name):
        self.ap = ap
        self.name = name
        self.last_write = None
        self.reads = []
        self.dma_sem = None
        self.excl = False


class _Eng:
    def __init__(self, name, obj, sem, is_pe=False):
        self.name = name
        self.obj = obj
        self.sem = sem
        self.seen = {}
        self.is_pe = is_pe
        self.n_wait = 0
        self.n_ins = 0


class Tracker:
    def __init__(self, nc):
        self.nc = nc
        self.es = ExitStack()
        self.scopes = []
        self.nsem = 0
        self.PE = _Eng("pe", nc.tensor, self._sem("pe"), is_pe=True)
        self.ACT = _Eng("act", nc.scalar, self._sem("act"))
        self.DVE = _Eng("dve", nc.vector, self._sem("dve"))
        self.POOL = _Eng("pool", nc.gpsimd, self._sem("pool"))
        self.SP = _Eng("sp", nc.sync, self._sem("sp"))
        self.engs = [self.PE, self.ACT, self.DVE, self.POOL, self.SP]
        self.dma_sems = []
        self.nt = 0

    def _sem(self, name):
        self.nsem += 1
        h = self.es.enter_context(self.nc.semaphore(f"s{self.nsem}_{name}"))
        return _Sem(h, name)

    def push(self):
        self.scopes.append(ExitStack())

    def pop(self):
        self.barrier()
        self.scopes.pop().close()

    def _stack(self):
        return self.scopes[-1] if self.scopes else self.es

    def sbuf(self, name, shape, dtype):
        self.nt += 1
        t = self._stack().enter_context(self.nc.sbuf_tensor(f"{name}_{self.nt}", list(shape), dtype))
        return Buf(t, name)

    def psum(self, name, shape, dtype):
        self.nt += 1
        t = self._stack().enter_context(self.nc.psum_tensor(f"{name}_{self.nt}", list(shape), dtype))
        b = Buf(t, name)
        b.excl = True
        return b

    def view(self, ap, name="v"):
        return Buf(ap, name)

    def _deps(self, reads, writes):
        deps = []
        for b in reads:
            if b.last_write is not None:
                deps.append(b.last_write)
            if b.excl:
                deps.extend(b.reads)
        for b in writes:
            if b.last_write is not None:
                deps.append(b.last_write)
            deps.extend(b.reads)
        return deps

    def _do_waits(self, eng, deps):
        need = {}
        for ev in deps:
            if ev.sem is eng.sem and eng.is_pe:
                continue
            tgt = ev.sem.issued if ev.is_dma else ev.count
            if need.get(ev.sem, 0) < tgt:
                need[ev.sem] = tgt
        for sem, tgt in need.items():
            if eng.seen.get(sem, 0) >= tgt:
                continue
            eng.obj.wait_ge(sem.handle, tgt)
            eng.seen[sem] = tgt
            eng.n_wait += 1

    def _record(self, ev, reads, writes):
        for b in writes:
            b.last_write = ev
            b.reads = []
        for b in reads:
            if b in writes:
                continue
            b.reads = [e for e in b.reads if e.sem is not ev.sem]
            b.reads.append(ev)

    def op(self, eng, fn, reads=(), writes=()):
        self._do_waits(eng, self._deps(reads, writes))
        ins = fn()
        eng.sem.issued += 1
        eng.n_ins += 1
        ins.then_inc(eng.sem.handle, 1)
        self._record(_Ev(eng.sem, eng.sem.issued, False), reads, writes)
        return ins

    def dma(self, eng, out_ap, in_ap, reads=(), writes=(), sem_buf=None, no_waw=False, **kw):
        self._do_waits(eng, self._deps(reads, () if no_waw else writes))
        if sem_buf is None:
            sem_buf = writes[0]
        if sem_buf.dma_sem is None:
            sem_buf.dma_sem = self._sem("dma_" + sem_buf.name)
            self.dma_sems.append(sem_buf.dma_sem)
        sem = sem_buf.dma_sem
        ins = eng.obj.dma_start(out=out_ap, in_=in_ap, **kw)
        sem.issued += 16
        ins.then_inc(sem.handle, 16)
        eng.n_ins += 1
        self._record(_Ev(sem, sem.issued, True), reads, writes)
        return ins

    def collective(self, kind, groups, in_ap, out_ap, reads=(), writes=()):
        eng = self.POOL
        self._do_waits(eng, self._deps(reads, writes))
        sem_buf = writes[0]
        if sem_buf.dma_sem is None:
            sem_buf.dma_sem = self._sem("cc_" + sem_buf.name)
            self.dma_sems.append(sem_buf.dma_sem)
        sem = sem_buf.dma_sem
        ins = eng.obj.collective_compute(kind, mybir.AluOpType.bypass, replica_groups=groups, ins=[in_ap], outs=[out_ap])
        sem.issued += 16
        ins.then_inc(sem.handle, 16)
        eng.n_ins += 1
        self._record(_Ev(sem, sem.issued, True), reads, writes)
        return ins

    def barrier(self):
        for e in self.engs:
            for o in self.engs:
                if o is e or o.sem.issued == 0:
                    continue
                if e.seen.get(o.sem, 0) < o.sem.issued:
                    e.obj.wait_ge(o.sem.handle, o.sem.issued)
                    e.seen[o.sem] = o.sem.issued
            for sem in self.dma_sems:
                if sem.issued > 0 and e.seen.get(sem, 0) < sem.issued:
                    e.obj.wait_ge(sem.handle, sem.issued)
                    e.seen[sem] = sem.issued

    def finish(self):
        self.barrier()
        while self.scopes:
            self.scopes.pop().close()
        self.es.close()


class Cfg:
    def __init__(self, D=4096, SEQ=4096, DEC_SEQ=32, PAST=4096, PLE=256):
        self.D = D
        self.SEQ = SEQ
        self.DEC_SEQ = DEC_SEQ
        self.PAST = PAST
        self.PLE = PLE
        self.F = 256 * ((8 * D // 3 + 255) // 256)
        self.HD = 128
        self.SBW = D // 2
        self.SBH = self.SBW // 128
        self.DV = 512
        self.DQK = 256
        self.MLVW = D - self.SBW
        self.MLH = self.MLVW // self.DV
        self.MLQK = self.MLH * self.DQK
        self.INW = 3 * self.SBW + 2 * self.MLQK + 2 * self.MLVW + 2 * self.MLH
        self.o_q = 0
        self.o_k = self.SBW
        self.o_v = 2 * self.SBW
        self.o_mq = 3 * self.SBW
        self.o_mk = self.o_mq + self.MLQK
        self.o_mv = self.o_mk + self.MLQK
        self.o_mo = self.o_mv + self.MLVW
        self.o_g = self.o_mo + self.MLVW
        self.KC = D // 128
        self.CHUNK = 64
        self.TT = 512
        self.NRANK = 4
        self.stages = ("A", "SBP", "MLP", "SBS", "MLS", "C")


WEIGHTS = ["w_ffn1_gate", "w_ffn1_up", "w_ffn1_down", "w_in", "w_out", "w_ffn2_gate", "w_ffn2_up",
           "w_ffn2_down", "w_ple_gate", "w_ple_proj"]


def build(cfg):
    c = cfg
    D, KC, F, INW, SEQ, NS, PAST = c.D, c.KC, c.F, c.INW, c.SEQ, c.DEC_SEQ, c.PAST
    SBW, SBH, MLH, DV, DQK, PLE = c.SBW, c.SBH, c.MLH, c.DV, c.DQK, c.PLE
    nc = bass.Bass("TRN2", target_bir_lowering=False)

    def din(name, shape, dt=F32):
        return nc.dram_tensor(name, list(shape), dt, kind="ExternalInput").ap()

    def dout(name, shape, dt=F32):
        return nc.dram_tensor(name, list(shape), dt, kind="ExternalOutput").ap()

    def dscr(name, shape, dt):
        return nc.dram_tensor(name, list(shape), dt, kind="Internal").ap()

    xp = din("xp", [SEQ, D]); xs = din("xs", [NS, D])
    NQ = SEQ // c.NRANK
    TC = min(c.TT, NQ)
    NJ = SEQ // TC
    NCT = NQ // TC
    pp = din("ppq", [NQ, PLE]); ps_ = din("ps", [NS, PLE])
    selq_in = din("selq", [128, NCT * NJ])
    ck = din("ck", [PAST, SBW]); cv = din("cv", [PAST, SBW])
    sc_in = din("sc", [MLH * DV, DQK]); sn_in = din("sn", [128, MLH * 2]); sm_in = din("sm", [MLH, 1])
    W = {}
    W["w_ffn1_gate"] = din("w_ffn1_gate", [D, F]); W["w_ffn1_up"] = din("w_ffn1_up", [D, F])
    W["w_ffn1_down"] = din("w_ffn1_down", [F, D]); W["w_in"] = din("w_in", [D, INW])
    W["w_out"] = din("w_out", [D, D])
    W["w_ffn2_gate"] = din("w_ffn2_gate", [D, F]); W["w_ffn2_up"] = din("w_ffn2_up", [D, F])
    W["w_ffn2_down"] = din("w_ffn2_down", [F, D]); W["w_ple_gate"] = din("w_ple_gate", [D, D])
    W["w_ple_proj"] = din("w_ple_proj", [PLE, D])
    lnT_in = din("lnT", [128, 5 * KC])
    gsbT_in = din("gsbT", [128, SBH])
    gml_in = din("gml", [64, MLH * DV])
    bi_in = din("bi", [MLH, 1]); bf_in = din("bf", [MLH, 1])
    ident_in = din("ident", [128, 128]); ones_in = din("ones", [128, 128])
    tri_in = din("tri", [128, 128])
    maskw_in = din("maskw", [128, 896])
    cmask_in = din("cmask", [64, 64])
    sel_in = din("sel", [MLH, MLH * 128])
    yp = dout("ypq", [NQ, D]); ys = dout("ys", [NS, D])
    kp = dout("kp", [SEQ, SBW]); vp = dout("vp", [SEQ, SBW])
    cp_o = dout("cp", [MLH * DV, DQK]); np_o = dout("np", [128, MLH * 2]); mp_o = dout("mp", [MLH, 1])
    ks = dout("ks", [NS, SBW]); vs = dout("vs", [NS, SBW])
    cs_o = dout("cs", [MLH * DV, DQK]); ns_o = dout("ns", [128, MLH * 2]); ms_o = dout("ms", [MLH, 1])
    NTOK = SEQ + NS
    z_scr = dscr("z_scr", [NTOK, INW], F32)
    xT_scr = dscr("xT_scr", [128, KC, NTOK], F32)
    mixT_scr = dscr("mixT_scr", [D, NTOK], BF16)

    wbf = {name: dscr("wbf_" + name, list(W[name].shape), BF16) for name in W}
    T = Tracker(nc)
    PE, ACT, DVE, POOL, SP = T.PE, T.ACT, T.DVE, T.POOL, T.SP
    d_wbf = {name: Buf(None, "wbf_" + name) for name in W}

    def convert_weights():
        for name in WEIGHTS:
            rows = W[name].shape[0]
            for r0 in range(0, rows, 512):
                r1 = min(rows, r0 + 512)
                T.dma(POOL, wbf[name][r0:r1, :], W[name][r0:r1, :], reads=[d_in], writes=[d_wbf[name]],
                      sem_buf=d_wbf[name], no_waw=True)
            if name == "w_ffn1_down":
                for nm in ("w_ffn1_gate", "w_ffn1_up", "w_ffn1_down"):
                    sem = d_wbf[nm].dma_sem
                    POOL.obj.wait_ge(sem.handle, sem.issued)
                    POOL.seen[sem] = sem.issued

    d_in = Buf(None, "d_in")
    d_z = Buf(None, "d_z"); d_xT = Buf(None, "d_xT"); d_mix = Buf(None, "d_mix"); d_out = Buf(None, "d_out")
    d_kv = Buf(None, "d_kv")

    ident = T.sbuf("ident", [128, 128], F32)
    identb = T.sbuf("identb", [128, 128], BF16)
    ones = T.sbuf("ones", [128, 128], F32)
    onesb = T.sbuf("onesb", [128, 128], BF16)
    tri = T.sbuf("tri", [128, 128], F32)
    lnT = T.sbuf("lnT", [128, 5 * KC], F32)
    gsbT = T.sbuf("gsbT", [128, SBH], F32)
    for dst, src in ((ident, ident_in), (ones, ones_in), (tri, tri_in), (lnT, lnT_in), (gsbT, gsbT_in)):
        T.dma(SP, dst.ap[:], src[:], reads=[d_in], writes=[dst], sem_buf=ident)
    T.op(DVE, lambda: nc.vector.tensor_copy(identb.ap[:], ident.ap[:]), reads=[ident], writes=[identb])
    T.op(DVE, lambda: nc.vector.tensor_copy(onesb.ap[:], ones.ap[:]), reads=[ones], writes=[onesb])

    tiles = []
    for t0 in range(0, SEQ, c.TT):
        tt = min(c.TT, SEQ - t0)
        tiles.append(dict(g0=t0, tt=tt, x=xp[t0:t0 + tt, :], ko=kp[t0:t0 + tt, :], vo=vp[t0:t0 + tt, :]))
    tiles.append(dict(g0=SEQ, tt=NS, x=xs, ko=ks, vo=vs))
    tilesC = []
    for i in range(NCT):
        tilesC.append(dict(g0=None, ci=i, tt=TC, p=pp[i * TC:(i + 1) * TC, :], y=yp[i * TC:(i + 1) * TC, :]))
    tilesC.append(dict(g0=SEQ, tt=NS, p=ps_, y=ys))
    TTM = c.TT

    WCOLS = 256
    NSLOT = 3
    FPASS = 16

    def phase_AC_buffers():
        B = {}
        xT_t = T.sbuf("xT", [128, KC, TTM], F32)
        hT_t = T.sbuf("hT", [128, KC, TTM], BF16)
        aT_t = T.sbuf("aT", [128, FPASS, TTM], BF16)
        B["xT"] = [T.view(xT_t.ap[:, k, :], f"xT{k}") for k in range(KC)]
        B["hT"] = [T.view(hT_t.ap[:, k, :], f"hT{k}") for k in range(KC)]
        B["aT"] = [T.view(aT_t.ap[:, k, :], f"aT{k}") for k in range(FPASS)]
        B["xT_t"] = xT_t
        B["aT_t"] = aT_t
        B["hT_t"] = hT_t
        B["slots"] = [T.sbuf(f"wslot{i}", [128, max(KC, FPASS) * WCOLS], BF16) for i in range(NSLOT)]
        B["slot_i"] = 0
        B["stage"] = [T.sbuf(f"stage{i}", [128, 2048], F32) for i in range(2)]
        B["stage_i"] = 0
        B["sq"] = [T.sbuf(f"sq{i}", [128, TTM], F32) for i in range(2)]
        B["rstd"] = T.sbuf("rstd", [128, TTM], F32)
        B["sg"] = [T.sbuf(f"sg{i}", [128, TTM], F32) for i in range(2)]
        B["pT"] = T.sbuf("pT", [128, 2, TTM], BF16)
        B["ps_a"] = [T.psum(f"psa{i}", [128, 512], F32) for i in range(6)]
        B["ps_i"] = 0
        B["ps_ss"] = T.psum("ps_ss", [128, 512], F32)
        B["ps_tr"] = T.psum("ps_tr", [128, 512], F32)
        return B

    def next_ps(B):
        p = B["ps_a"][B["ps_i"] % len(B["ps_a"])]
        B["ps_i"] += 1
        return p

    def next_slot(B):
        s = B["slots"][B["slot_i"] % NSLOT]
        B["slot_i"] += 1
        return s

    def next_stage(B):
        s = B["stage"][B["stage_i"] % 2]
        B["stage_i"] += 1
        return s

    def load_w(B, w_ap, r0, nrows, c0, ncols):
        s = next_slot(B)
        nk = nrows // 128
        dst = s.ap[:, 0:nk * ncols].rearrange("p (k n) -> p k n", k=nk)
        src = wbf[w_ap][r0:r0 + nrows, c0:c0 + ncols].rearrange("(k p) n -> p k n", p=128)
        T.dma(SP, dst, src, reads=[d_wbf[w_ap]], writes=[s])
        return s, dst

    def load_xT(B, tile):
        tt = tile["tt"]
        for s0 in range(0, tt, 128):
            ts = min(128, tt - s0)
            for half in range(0, D, 2048):
                hw = min(2048, D - half)
                st = next_stage(B)
                T.dma(ACT, st.ap[0:ts, 0:hw], tile["x"][s0:s0 + ts, half:half + hw], reads=[d_in], writes=[st])
                for k4 in range(0, hw // 128, 4):
                    n4 = min(4, hw // 128 - k4)
                    pt = B["ps_tr"]
                    for j in range(n4):
                        kk = k4 + j
                        T.op(PE, lambda kk=kk, j=j: nc.tensor.transpose(
                            pt.ap[:, j * 128:j * 128 + ts], st.ap[0:ts, kk * 128:(kk + 1) * 128], ident.ap[0:ts, 0:ts]),
                            reads=[st, ident], writes=[pt])
                    for j in range(n4):
                        kc = half // 128 + k4 + j
                        T.op(DVE, lambda kc=kc, j=j: nc.vector.tensor_copy(
                            B["xT_t"].ap[:, kc, s0:s0 + ts], pt.ap[:, j * 128:j * 128 + ts]),
                            reads=[pt], writes=[B["xT"][kc]])

    def rms_stats(B, tt):
        pss = B["ps_ss"]
        for kc in range(KC):
            sq = B["sq"][kc % 2]
            T.op(ACT, lambda kc=kc, sq=sq: nc.scalar.activation(sq.ap[:, 0:tt], B["xT_t"].ap[:, kc, 0:tt], AF.Square),
                 reads=[B["xT"][kc]], writes=[sq])
            T.op(PE, lambda kc=kc, sq=sq: nc.tensor.matmul(pss.ap[:, 0:tt], ones.ap[:], sq.ap[:, 0:tt],
                                                         start=(kc == 0), stop=(kc == KC - 1)),
                 reads=[sq, ones], writes=[pss])
        r = B["rstd"]
        T.op(ACT, lambda: nc.scalar.activation(r.ap[:, 0:tt], pss.ap[:, 0:tt], AF.Sqrt, bias=EPS, scale=1.0 / D),
             reads=[pss], writes=[r])
        T.op(DVE, lambda: nc.vector.reciprocal(r.ap[:, 0:tt], r.ap[:, 0:tt]), reads=[r], writes=[r])

    def make_hT(B, tt, ln_idx):
        rms_stats(B, tt)
        r = B["rstd"]
        for kc in range(KC):
            eng, e = (DVE, nc.vector)
            T.op(eng, lambda kc=kc, e=e: e.scalar_tensor_tensor(
                out=B["hT"][kc].ap[:, 0:tt], in0=B["xT"][kc].ap[:, 0:tt],
                scalar=lnT.ap[:, ln_idx * KC + kc:ln_idx * KC + kc + 1], in1=r.ap[:, 0:tt],
                op0=ALU.mult, op1=ALU.mult), reads=[B["xT"][kc], r, lnT], writes=[B["hT"][kc]])

    def ffn(B, tt, wg, wu, wd):
        nft = F // 128
        for f0 in range(0, nft, FPASS):
            nf = min(FPASS, nft - f0)
            for fg in range(0, nf, 2):
                fcol = (f0 + fg) * 128
                sg_, gv = load_w(B, wg, 0, D, fcol, WCOLS)
                su_, uv = load_w(B, wu, 0, D, fcol, WCOLS)
                for j in range(2):
                    pg = next_ps(B); pu = next_ps(B)
                    for kc in range(KC):
                        T.op(PE, lambda kc=kc: nc.tensor.matmul(pg.ap[:, 0:tt], gv[:, kc, j * 128:(j + 1) * 128],
                                                                B["hT"][kc].ap[:, 0:tt], start=(kc == 0), stop=(kc == KC - 1)),
                             reads=[sg_, B["hT"][kc]], writes=[pg])
                    for kc in range(KC):
                        T.op(PE, lambda kc=kc: nc.tensor.matmul(pu.ap[:, 0:tt], uv[:, kc, j * 128:(j + 1) * 128],
                                                                B["hT"][kc].ap[:, 0:tt], start=(kc == 0), stop=(kc == KC - 1)),
                             reads=[su_, B["hT"][kc]], writes=[pu])
                    sgb = B["sg"][(fg + j) % 2]
                    T.op(ACT, lambda: nc.scalar.activation(sgb.ap[:, 0:tt], pg.ap[:, 0:tt], AF.Silu),
                         reads=[pg], writes=[sgb])
                    a = B["aT"][fg + j]
                    T.op(DVE, lambda a=a: nc.vector.tensor_tensor(out=a.ap[:, 0:tt], in0=sgb.ap[:, 0:tt],
                                                                  in1=pu.ap[:, 0:tt], op=ALU.mult),
                         reads=[sgb, pu], writes=[a])
            for dg in range(0, D, WCOLS):
                sd_, dvw = load_w(B, wd, f0 * 128, nf * 128, dg, WCOLS)
                for j in range(WCOLS // 128):
                    po = next_ps(B)
                    for fk in range(nf):
                        T.op(PE, lambda fk=fk: nc.tensor.matmul(po.ap[:, 0:tt], dvw[:, fk, j * 128:(j + 1) * 128],
                                                                B["aT"][fk].ap[:, 0:tt], start=(fk == 0), stop=(fk == nf - 1)),
                             reads=[sd_, B["aT"][fk]], writes=[po])
                    xk = B["xT"][dg // 128 + j]
                    T.op(DVE, lambda xk=xk: nc.vector.scalar_tensor_tensor(
                        out=xk.ap[:, 0:tt], in0=po.ap[:, 0:tt], scalar=0.5, in1=xk.ap[:, 0:tt],
                        op0=ALU.mult, op1=ALU.add), reads=[po, xk], writes=[xk])

    def in_proj(B, tile):
        tt, g0 = tile["tt"], tile["g0"]
        for c0 in range(0, INW, WCOLS):
            ncol = min(WCOLS, INW - c0)
            s_, wv = load_w(B, "w_in", 0, D, c0, ncol)
            for s0 in range(0, tt, 128):
                ts = min(128, tt - s0)
                po = next_ps(B)
                for kc in range(KC):
                    T.op(PE, lambda kc=kc: nc.tensor.matmul(po.ap[0:ts, 0:ncol], B["hT"][kc].ap[:, s0:s0 + ts],
                                                            wv[:, kc, :], start=(kc == 0), stop=(kc == KC - 1)),
                         reads=[s_, B["hT"][kc]], writes=[po])
                st = next_stage(B)
                T.op(ACT, lambda: nc.scalar.copy(st.ap[0:ts, 0:ncol], po.ap[0:ts, 0:ncol]), reads=[po], writes=[st])
                T.dma(ACT, z_scr[g0 + s0:g0 + s0 + ts, c0:c0 + ncol], st.ap[0:ts, 0:ncol], reads=[st], writes=[d_z], sem_buf=st)
                if c.o_k <= c0 < c.o_k + SBW:
                    T.dma(ACT, tile["ko"][s0:s0 + ts, c0 - c.o_k:c0 - c.o_k + ncol], st.ap[0:ts, 0:ncol],
                          reads=[st], writes=[d_out], sem_buf=st)
                if c.o_v <= c0 < c.o_v + SBW:
                    T.dma(ACT, tile["vo"][s0:s0 + ts, c0 - c.o_v:c0 - c.o_v + ncol], st.ap[0:ts, 0:ncol],
                          reads=[st], writes=[d_out], sem_buf=st)

    def phase_A(B, tile):
        tt, g0 = tile["tt"], tile["g0"]
        sub = getattr(c, "sub", 9)
        load_xT(B, tile)
        if tile is tiles[0]:
            convert_weights()
        if sub >= 2:
            make_hT(B, tt, 0)
        if sub >= 3:
            ffn(B, tt, "w_ffn1_gate", "w_ffn1_up", "w_ffn1_down")
        if sub >= 4:
            make_hT(B, tt, 1)
        if sub >= 5:
            in_proj(B, tile)
        for kc in range(KC):
            T.dma(ACT, xT_scr[:, kc, g0:g0 + tt], B["xT"][kc].ap[:, 0:tt], reads=[B["xT"][kc]], writes=[d_xT],
                  sem_buf=B["xT_t"])

    def proj_accum(B, tt, w_ap, rhs_bufs, nk, post):
        for dg in range(0, D, WCOLS):
            s_, wv = load_w(B, w_ap, 0, nk * 128, dg, WCOLS)
            for j in range(WCOLS // 128):
                po = next_ps(B)
                for k in range(nk):
                    T.op(PE, lambda k=k: nc.tensor.matmul(po.ap[:, 0:tt], wv[:, k, j * 128:(j + 1) * 128],
                                                          rhs_bufs[k].ap[:, 0:tt], start=(k == 0), stop=(k == nk - 1)),
                         reads=[s_, rhs_bufs[k]], writes=[po])
                post(dg // 128 + j, po)

    def phase_C(B, tile):
        tt, g0 = tile["tt"], tile["g0"]
        if g0 is None:
            if "selq" not in B:
                B["selq"] = T.sbuf("selq", [128, NCT * NJ], F32)
                T.dma(ACT, B["selq"].ap[:], selq_in[:], reads=[d_in], writes=[B["selq"]])
            selq = B["selq"]
            ci = tile["ci"]
            XCH = 2048 // TC
            HCH = min(NJ, (FPASS // 2 * TTM) // TC)
            for kc in range(KC):
                for j0 in range(0, NJ, XCH):
                    nj = min(XCH, NJ - j0)
                    st = next_stage(B)
                    T.dma(ACT, st.ap[:, 0:nj * TC], xT_scr[:, kc, j0 * TC:(j0 + nj) * TC], reads=[d_xT], writes=[st])
                    for j in range(nj):
                        sc_ = selq.ap[:, ci * NJ + j0 + j:ci * NJ + j0 + j + 1]
                        xk = B["xT"][kc]
                        if j0 + j == 0:
                            T.op(DVE, lambda: nc.vector.tensor_scalar(out=xk.ap[:, 0:tt], in0=st.ap[:, j * TC:(j + 1) * TC], scalar1=sc_,
                                                                      scalar2=None, op0=ALU.mult), reads=[st, selq], writes=[xk])
                        else:
                            T.op(DVE, lambda: nc.vector.scalar_tensor_tensor(out=xk.ap[:, 0:tt], in0=st.ap[:, j * TC:(j + 1) * TC], scalar=sc_,
                                                                             in1=xk.ap[:, 0:tt], op0=ALU.mult, op1=ALU.add),
                                 reads=[st, selq, xk], writes=[xk])
                for j0 in range(0, NJ, HCH):
                    nj = min(HCH, NJ - j0)
                    half = (B.get("hsel_i", 0) % 2) * (FPASS // 2)
                    B["hsel_i"] = B.get("hsel_i", 0) + 1
                    nb = (nj * TC + TTM - 1) // TTM
                    hb = B["aT"][half:half + nb]
                    flat = B["aT_t"].ap[:, half:half + FPASS // 2, :].rearrange("p a t -> p (a t)")
                    T.dma(ACT, flat[:, 0:nj * TC], mixT_scr[kc * 128:(kc + 1) * 128, j0 * TC:(j0 + nj) * TC], reads=[d_mix], writes=hb,
                          sem_buf=B["aT_t"])
                    for j in range(nj):
                        sc_ = selq.ap[:, ci * NJ + j0 + j:ci * NJ + j0 + j + 1]
                        hk = B["hT"][kc]
                        src = flat[:, j * TC:(j + 1) * TC]
                        if j0 + j == 0:
                            T.op(DVE, lambda: nc.vector.tensor_scalar(out=hk.ap[:, 0:tt], in0=src, scalar1=sc_, scalar2=None, op0=ALU.mult),
                                 reads=hb + [selq], writes=[hk])
                        else:
                            T.op(DVE, lambda: nc.vector.scalar_tensor_tensor(out=hk.ap[:, 0:tt], in0=src, scalar=sc_, in1=hk.ap[:, 0:tt],
                                                                             op0=ALU.mult, op1=ALU.add), reads=hb + [selq, hk], writes=[hk])
        else:
            for kc in range(KC):
                T.dma(ACT, B["xT"][kc].ap[:, 0:tt], xT_scr[:, kc, g0:g0 + tt], reads=[d_xT], writes=[B["xT"][kc]], sem_buf=B["xT_t"])
                T.dma(ACT, B["hT"][kc].ap[:, 0:tt], mixT_scr[kc * 128:(kc + 1) * 128, g0:g0 + tt], reads=[d_mix],
                      writes=[B["hT"][kc]], sem_buf=B["hT_t"])

        def post_add(dt, po):
            xk = B["xT"][dt]
            T.op(DVE, lambda: nc.vector.tensor_tensor(out=xk.ap[:, 0:tt], in0=po.ap[:, 0:tt], in1=xk.ap[:, 0:tt], op=ALU.add),
                 reads=[po, xk], writes=[xk])

        proj_accum(B, tt, "w_out", B["hT"], KC, post_add)
        make_hT(B, tt, 2)
        ffn(B, tt, "w_ffn2_gate", "w_ffn2_up", "w_ffn2_down")
        make_hT(B, tt, 3)
        pT = B["pT"]
        for s0 in range(0, tt, 128):
            ts = min(128, tt - s0)
            st = next_stage(B)
            T.dma(ACT, st.ap[0:ts, 0:PLE], tile["p"][s0:s0 + ts, :], reads=[d_in], writes=[st])
            pt = B["ps_tr"]
            for j in range(PLE // 128):
                T.op(PE, lambda j=j: nc.tensor.transpose(pt.ap[:, j * 128:j * 128 + ts], st.ap[0:ts, j * 128:(j + 1) * 128],
                                                         ident.ap[0:ts, 0:ts]), reads=[st, ident], writes=[pt])
            for j in range(PLE // 128):
                T.op(DVE, lambda j=j: nc.vector.tensor_copy(pT.ap[:, j, s0:s0 + ts], pt.ap[:, j * 128:j * 128 + ts]),
                     reads=[pt], writes=[pT])
        pT_views = [T.view(pT.ap[:, j, :], "pTv") for j in range(PLE // 128)]
        gate_sb = {}

        def post_gate(dt, po):
            g = B["aT"][dt % FPASS]
            T.op(ACT, lambda: nc.scalar.activation(g.ap[:, 0:tt], po.ap[:, 0:tt], AF.Sigmoid), reads=[po], writes=[g])
            gate_sb[dt] = g

        for dg in range(0, D, WCOLS):
            s_, wv = load_w(B, "w_ple_gate", 0, D, dg, WCOLS)
            s2_, wv2 = load_w(B, "w_ple_proj", 0, PLE, dg, WCOLS)
            for j in range(WCOLS // 128):
                dt = dg // 128 + j
                po = next_ps(B)
                for k in range(KC):
                    T.op(PE, lambda k=k: nc.tensor.matmul(po.ap[:, 0:tt], wv[:, k, j * 128:(j + 1) * 128],
                                                          B["hT"][k].ap[:, 0:tt], start=(k == 0), stop=(k == KC - 1)),
                         reads=[s_, B["hT"][k]], writes=[po])
                post_gate(dt, po)
                pq = next_ps(B)
                npk = PLE // 128
                for k in range(npk):
                    T.op(PE, lambda k=k: nc.tensor.matmul(pq.ap[:, 0:tt], wv2[:, k, j * 128:(j + 1) * 128],
                                                          pT.ap[:, k, 0:tt], start=(k == 0), stop=(k == npk - 1)),
                         reads=[s2_, pT], writes=[pq])
                g = gate_sb[dt]
                T.op(DVE, lambda g=g, pq=pq: nc.vector.tensor_tensor(out=g.ap[:, 0:tt], in0=g.ap[:, 0:tt], in1=pq.ap[:, 0:tt],
                                                                     op=ALU.mult), reads=[g, pq], writes=[g])
                xk = B["xT"][dt]
                T.op(POOL, lambda g=g, xk=xk: nc.gpsimd.tensor_tensor(out=xk.ap[:, 0:tt], in0=xk.ap[:, 0:tt], in1=g.ap[:, 0:tt],
                                                                      op=ALU.add), reads=[g, xk], writes=[xk])
        rms_stats(B, tt)
        r = B["rstd"]
        for kc in range(KC):
            xk = B["xT"][kc]
            T.op(DVE, lambda kc=kc, xk=xk: nc.vector.scalar_tensor_tensor(
                out=xk.ap[:, 0:tt], in0=xk.ap[:, 0:tt], scalar=lnT.ap[:, 4 * KC + kc:4 * KC + kc + 1], in1=r.ap[:, 0:tt],
                op0=ALU.mult, op1=ALU.mult), reads=[xk, r, lnT], writes=[xk])
        for s0 in range(0, tt, 128):
            ts = min(128, tt - s0)
            for half in range(0, D, 2048):
                hw = min(2048, D - half)
                st = next_stage(B)
                for k4 in range(0, hw // 128, 4):
                    n4 = min(4, hw // 128 - k4)
                    pt = B["ps_tr"]
                    for j in range(n4):
                        kc = half // 128 + k4 + j
                        T.op(PE, lambda kc=kc, j=j: nc.tensor.transpose(pt.ap[0:ts, j * 128:(j + 1) * 128],
                                                                       B["xT"][kc].ap[:, s0:s0 + ts], ident.ap[:]),
                             reads=[B["xT"][kc], ident], writes=[pt])
                    T.op(ACT, lambda k4=k4, n4=n4: nc.scalar.copy(st.ap[0:ts, k4 * 128:(k4 + n4) * 128], pt.ap[0:ts, 0:n4 * 128]),
                         reads=[pt], writes=[st])
                T.dma(ACT, tile["y"][s0:s0 + ts, half:half + hw], st.ap[0:ts, 0:hw], reads=[st], writes=[d_out], sem_buf=st)

    def sb_attention(tok0, nq_total, TQ, key_srcs, n_new):
        nkeys = sum(s[2] for s in key_srcs)
        nkb = (nkeys + 127) // 128
        n_old = nkeys - n_new
        scale = 128.0 ** -0.5
        T.push()
        maskw = T.sbuf("maskw", [128, 896], F32)
        T.dma(SP, maskw.ap[:], maskw_in[:], reads=[d_in], writes=[maskw])
        ktm = [T.sbuf(f"ktm{i}", [128, nkb, 128], BF16) for i in range(1)]
        qtm = T.sbuf("qtm", [128, (nq_total + 127) // 128, 128], BF16)
        vtm = [T.sbuf(f"vtm{i}", [128, nkb, 128], BF16) for i in range(2)]
        KTb = [T.sbuf(f"KT{i}", [128, nkb * 128], BF16) for i in range(2)]
        QTb = [T.sbuf(f"QT{i}", [128, ((nq_total + 127) // 128) * 128], BF16) for i in range(2)]
        e_sb = [T.sbuf(f"e{i}", [128, TQ], F32) for i in range(2)]
        sp_sb = [T.sbuf(f"sp{i}", [128, TQ], F32) for i in range(2)]
        t_sb = [T.sbuf(f"t{i}", [128, TQ], F32) for i in range(2)]
        u_sb = [T.sbuf(f"u{i}", [128, TQ], F32) for i in range(2)]
        A_sb = [T.sbuf(f"A{i}", [128, TQ], BF16) for i in range(2)]
        carry = T.sbuf("carry", [128, TQ], F32)
        o_sb = T.sbuf("o_sb", [128, TQ], F32)
        osq = T.sbuf("osq", [128, TQ], F32)
        orst = T.sbuf("orst", [128, TQ], F32)
        mixo = [T.sbuf(f"mixo{i}", [128, TQ], BF16) for i in range(2)]
        pS = [T.psum(f"pS{i}", [128, 512], F32) for i in range(2)]
        pG = [T.psum(f"pG{i}", [128, 512], F32) for i in range(2)]
        pR = [T.psum(f"pR{i}", [128, 512], F32) for i in range(2)]
        pO = T.psum("pO", [128, 512], F32)
        pT_ = T.psum("pTr", [128, 512], BF16)
        it = 0
        for h in range(SBH):
            KT = KTb[h % 2]; QT = QTb[h % 2]; V = vtm[h % 2]; K_ = ktm[0]
            r = 0
            for (kd, vd, n) in key_srcs:
                nfull = n // 128
                col = slice(h * 128, (h + 1) * 128)
                if nfull:
                    b0 = r // 128
                    T.dma(POOL, K_.ap[:, b0:b0 + nfull, :], kd[0:nfull * 128, col].rearrange("(b p) d -> p b d", p=128),
                          reads=[d_in, d_z, d_kv], writes=[K_])
                    T.dma(POOL, V.ap[:, b0:b0 + nfull, :], vd[0:nfull * 128, col].rearrange("(b p) d -> p b d", p=128),
                          reads=[d_in, d_z, d_kv], writes=[V])
                rem = n - nfull * 128
                if rem:
                    b0 = (r + nfull * 128) // 128
                    T.dma(POOL, K_.ap[0:rem, b0, :], kd[nfull * 128:n, col], reads=[d_in, d_z, d_kv], writes=[K_])
                    T.dma(POOL, V.ap[0:rem, b0, :], vd[nfull * 128:n, col], reads=[d_in, d_z, d_kv], writes=[V])
                r += n
                assert r % 128 == 0 or (kd is key_srcs[-1][0])
            nqb = (nq_total + 127) // 128
            qfull = nq_total // 128
            qcol = slice(c.o_q + h * 128, c.o_q + (h + 1) * 128)
            if qfull:
                T.dma(POOL, qtm.ap[:, 0:qfull, :], z_scr[tok0:tok0 + qfull * 128, qcol].rearrange("(b p) d -> p b d", p=128),
                      reads=[d_z], writes=[qtm])
            if nq_total - qfull * 128:
                rem = nq_total - qfull * 128
                T.dma(POOL, qtm.ap[0:rem, qfull, :], z_scr[tok0 + qfull * 128:tok0 + nq_total, qcol], reads=[d_z], writes=[qtm])

            def transpose_blocks(src, dstT, ntok):
                nb = (ntok + 127) // 128
                for b4 in range(0, nb, 4):
                    n4 = min(4, nb - b4)
                    for j in range(n4):
                        b = b4 + j
                        sz = min(128, ntok - b * 128)
                        T.op(PE, lambda b=b, j=j, sz=sz: nc.tensor.transpose(pT_.ap[:, j * 128:j * 128 + sz], src.ap[0:sz, b, :],
                                                                            identb.ap[0:sz, 0:sz]),
                             reads=[src, identb], writes=[pT_])
                    w = min(n4 * 128, ntok - b4 * 128)
                    T.op(DVE, lambda b4=b4, w=w: nc.vector.tensor_copy(dstT.ap[:, b4 * 128:b4 * 128 + w], pT_.ap[:, 0:w]),
                         reads=[pT_], writes=[dstT])

            lvl = getattr(c, 'sblvl', 9)
            if lvl >= 1:
                transpose_blocks(K_, KT, nkeys)
                transpose_blocks(qtm, QT, nq_total)
            if lvl < 2:
                continue
            for q0 in range(0, nq_total, TQ):
                tq = min(TQ, nq_total - q0)
                kmax = n_old + min(n_new, q0 + tq)
                kbs = [(kb * 128, min(128, kmax - kb * 128)) for kb in range((kmax + 127) // 128)]
                T.op(POOL, lambda: nc.gpsimd.memset(carry.ap[:, 0:tq], 0.0), reads=[], writes=[carry])
                npairs = len(kbs)

                def stage1(idx, k0, ksz):
                    nonlocal it
                    i2 = it % 2
                    it += 1
                    S = pS[i2]; G = pG[i2]; R = pR[i2]
                    e = e_sb[i2]; spb = sp_sb[i2]
                    last = idx == npairs - 1
                    T.op(PE, lambda: nc.tensor.matmul(S.ap[0:ksz, 0:tq], KT.ap[:, k0:k0 + ksz], QT.ap[:, q0:q0 + tq],
                                                      start=True, stop=True), reads=[KT, QT], writes=[S])
                    T.op(ACT, lambda: nc.scalar.activation(e.ap[0:ksz, 0:tq], S.ap[0:ksz, 0:tq], AF.Exp, scale=scale),
                         reads=[S], writes=[e])
                    T.op(ACT, lambda: nc.scalar.activation(spb.ap[0:ksz, 0:tq], e.ap[0:ksz, 0:tq], AF.Ln, bias=1.0),
                         reads=[e], writes=[spb])
                    need_mask = (k0 + ksz - n_old) > q0
                    mslice = None
                    if need_mask:
                        off = (k0 - n_old) - q0
                        mslice = maskw.ap[0:ksz, 384 - off:384 - off + tq]
                        assert 0 <= 384 - off and 384 - off + tq <= 896, (off, tq)
                        T.op(POOL, lambda: nc.gpsimd.tensor_tensor(out=spb.ap[0:ksz, 0:tq], in0=spb.ap[0:ksz, 0:tq], in1=mslice,
                                                                   op=ALU.mult), reads=[spb, maskw], writes=[spb])
                    T.op(PE, lambda: nc.tensor.matmul(G.ap[0:ksz, 0:tq], tri.ap[0:ksz, 0:ksz], spb.ap[0:ksz, 0:tq],
                                                      start=True, stop=True), reads=[tri, spb], writes=[G])
                    if not last:
                        T.op(PE, lambda: nc.tensor.matmul(R.ap[:, 0:tq], ones.ap[0:ksz, :], spb.ap[0:ksz, 0:tq],
                                                          start=True, stop=True), reads=[ones, spb], writes=[R])
                    return dict(i2=i2, idx=idx, k0=k0, ksz=ksz, last=last, need_mask=need_mask, mslice=mslice)

                def stage2(cx):
                    i2, k0, ksz, last = cx["i2"], cx["k0"], cx["ksz"], cx["last"]
                    S = pS[i2]; G = pG[i2]; R = pR[i2]
                    tb = t_sb[i2]; ub = u_sb[i2]; Ab = A_sb[i2]
                    mslice = cx["mslice"]
                    first = cx["idx"] == 0
                    T.op(DVE, lambda: nc.vector.scalar_tensor_tensor(out=tb.ap[0:ksz, 0:tq], in0=S.ap[0:ksz, 0:tq], scalar=scale,
                                                                     in1=carry.ap[0:ksz, 0:tq], op0=ALU.mult, op1=ALU.subtract),
                         reads=[S, carry], writes=[tb])
                    T.op(DVE, lambda: nc.vector.tensor_tensor(out=ub.ap[0:ksz, 0:tq], in0=tb.ap[0:ksz, 0:tq], in1=G.ap[0:ksz, 0:tq],
                                                              op=ALU.subtract), reads=[tb, G], writes=[ub])
                    T.op(ACT, lambda: nc.scalar.activation(Ab.ap[0:ksz, 0:tq], ub.ap[0:ksz, 0:tq], AF.Exp), reads=[ub], writes=[Ab])
                    if cx["need_mask"]:
                        T.op(POOL, lambda: nc.gpsimd.tensor_tensor(out=Ab.ap[0:ksz, 0:tq], in0=Ab.ap[0:ksz, 0:tq], in1=mslice,
                                                                   op=ALU.mult), reads=[Ab, maskw], writes=[Ab])
                    if not last:
                        T.op(DVE, lambda: nc.vector.tensor_tensor(out=carry.ap[:, 0:tq], in0=carry.ap[:, 0:tq], in1=R.ap[:, 0:tq],
                                                                  op=ALU.add), reads=[carry, R], writes=[carry])
                    T.op(PE, lambda: nc.tensor.matmul(pO.ap[:, 0:tq], V.ap[0:ksz, k0 // 128, :], Ab.ap[0:ksz, 0:tq],
                                                      start=first, stop=last), reads=[V, Ab], writes=[pO])

                prev = None
                for idx, (k0, ksz) in enumerate(reversed(kbs)):
                    cx = stage1(idx, k0, ksz)
                    if prev is not None:
                        stage2(prev)
                    prev = cx
                stage2(prev)
                if lvl < 8:
                    continue
                T.op(DVE, lambda: nc.vector.tensor_copy(o_sb.ap[:, 0:tq], pO.ap[:, 0:tq]), reads=[pO], writes=[o_sb])
                T.op(ACT, lambda: nc.scalar.activation(osq.ap[:, 0:tq], o_sb.ap[:, 0:tq], AF.Square), reads=[o_sb], writes=[osq])
                G = pG[it % 2]
                T.op(PE, lambda: nc.tensor.matmul(G.ap[:, 0:tq], ones.ap[:], osq.ap[:, 0:tq], start=True, stop=True),
                     reads=[ones, osq], writes=[G])
                T.op(ACT, lambda: nc.scalar.activation(orst.ap[:, 0:tq], G.ap[:, 0:tq], AF.Sqrt, bias=EPS, scale=1.0 / 128),
                     reads=[G], writes=[orst])
                T.op(DVE, lambda: nc.vector.reciprocal(orst.ap[:, 0:tq], orst.ap[:, 0:tq]), reads=[orst], writes=[orst])
                mo = mixo[(q0 // TQ) % 2]
                T.op(DVE, lambda: nc.vector.scalar_tensor_tensor(out=mo.ap[:, 0:tq], in0=o_sb.ap[:, 0:tq], scalar=gsbT.ap[:, h:h + 1],
                                                                 in1=orst.ap[:, 0:tq], op0=ALU.mult, op1=ALU.mult),
                     reads=[o_sb, orst, gsbT], writes=[mo])
                T.dma(SP, mixT_scr[h * 128:(h + 1) * 128, tok0 + q0:tok0 + q0 + tq], mo.ap[:, 0:tq], reads=[mo], writes=[d_mix],
                      sem_buf=mo)
        T.pop()

    def mlstm(tok0, L, nchunk, init):
        NT = L * nchunk
        SEGC = min(16, nchunk)
        SEGT = SEGC * L
        NV = DV // 128
        T.push()
        sel = T.sbuf("sel", [MLH, MLH * 128], F32)
        cmask = T.sbuf("cmask", [64, 64], F32)
        bi = T.sbuf("bi", [MLH, 1], F32); bfb = T.sbuf("bfb", [MLH, 1], F32)
        gml = T.sbuf("gml", [64, MLH * DV], F32)
        for dst, src in ((sel, sel_in), (cmask, cmask_in), (bi, bi_in), (bfb, bf_in), (gml, gml_in)):
            T.dma(SP, dst.ap[:], src[:], reads=[d_in], writes=[dst], sem_buf=sel)
        Mx = T.sbuf("Mx", [MLH, NT + 1], F32)
        m0 = T.sbuf("m0", [MLH, 1], F32)
        mout = T.sbuf("mout", [MLH, 1], F32)
        acol = T.sbuf("acol", [64, nchunk, MLH], F32)
        ccol = T.sbuf("ccol", [64, nchunk, MLH], F32)
        pg = T.psum("pg", [128, 512], F32)
        T.push()
        nblk = (NT + 127) // 128
        gtm = T.sbuf("gtm", [128, nblk, 2 * MLH], F32)
        nfull = NT // 128
        gcols = slice(c.o_g, c.o_g + 2 * MLH)
        if nfull:
            T.dma(SP, gtm.ap[:, 0:nfull, :], z_scr[tok0:tok0 + nfull * 128, gcols].rearrange("(b p) g -> p b g", p=128),
                  reads=[d_z], writes=[gtm])
        if NT - nfull * 128:
            T.dma(SP, gtm.ap[0:NT - nfull * 128, nfull, :], z_scr[tok0 + nfull * 128:tok0 + NT, gcols], reads=[d_z], writes=[gtm])
        GI = T.sbuf("GI", [MLH, NT], F32); GF = T.sbuf("GF", [MLH, NT], F32)
        Bc = T.sbuf("Bc", [MLH, NT], F32); av = T.sbuf("av", [MLH, NT], F32)
        nbm = T.sbuf("nbm", [MLH, NT], F32)
        onesr = T.sbuf("onesr", [MLH, NT], F32)
        T.op(POOL, lambda: nc.gpsimd.memset(onesr.ap[:], 1.0), writes=[onesr])
        if init is None:
            T.op(POOL, lambda: nc.gpsimd.memset(m0.ap[:], 0.0), writes=[m0])
        else:
            T.dma(SP, m0.ap[:], init[2][:], reads=[d_in], writes=[m0])
        for b in range(nblk):
            sz = min(128, NT - b * 128)
            T.op(PE, lambda b=b, sz=sz: nc.tensor.transpose(pg.ap[0:MLH, 0:sz], gtm.ap[0:sz, b, 0:MLH], ident.ap[0:sz, 0:sz]),
                 reads=[gtm, ident], writes=[pg])
            T.op(PE, lambda b=b, sz=sz: nc.tensor.transpose(pg.ap[0:MLH, 128:128 + sz], gtm.ap[0:sz, b, MLH:2 * MLH],
                                                           ident.ap[0:sz, 0:sz]), reads=[gtm, ident], writes=[pg])
            T.op(DVE, lambda b=b, sz=sz: nc.vector.tensor_copy(GI.ap[:, b * 128:b * 128 + sz], pg.ap[0:MLH, 0:sz]),
                 reads=[pg], writes=[GI])
            T.op(DVE, lambda b=b, sz=sz: nc.vector.tensor_copy(GF.ap[:, b * 128:b * 128 + sz], pg.ap[0:MLH, 128:128 + sz]),
                 reads=[pg], writes=[GF])
        T.op(DVE, lambda: nc.vector.tensor_scalar(out=GF.ap[:], in0=GF.ap[:], scalar1=bfb.ap[:, 0:1], scalar2=-1.0,
                                                  op0=ALU.add, op1=ALU.mult), reads=[GF, bfb], writes=[GF])
        T.op(ACT, lambda: nc.scalar.activation(GF.ap[:], GF.ap[:], AF.Exp), reads=[GF], writes=[GF])
        T.op(ACT, lambda: nc.scalar.activation(GF.ap[:], GF.ap[:], AF.Ln, bias=1.0), reads=[GF], writes=[GF])
        T.op(DVE, lambda: nc.vector.tensor_tensor_scan(out=Bc.ap[:], data0=onesr.ap[:], data1=GF.ap[:], initial=0.0,
                                                       op0=ALU.mult, op1=ALU.add), reads=[onesr, GF], writes=[Bc])
        T.op(DVE, lambda: nc.vector.scalar_tensor_tensor(out=av.ap[:], in0=GI.ap[:], scalar=bi.ap[:, 0:1], in1=Bc.ap[:],
                                                         op0=ALU.add, op1=ALU.add), reads=[GI, bi, Bc], writes=[av])
        T.op(DVE, lambda: nc.vector.tensor_copy(Mx.ap[:, 0:1], m0.ap[:]), reads=[m0], writes=[Mx])
        T.op(DVE, lambda: nc.vector.tensor_tensor_scan(out=Mx.ap[:, 1:NT + 1], data0=onesr.ap[:], data1=av.ap[:], initial=m0.ap[:, 0:1],
                                                       op0=ALU.mult, op1=ALU.max), reads=[onesr, av, m0, Mx], writes=[Mx])
        T.op(DVE, lambda: nc.vector.tensor_tensor(out=nbm.ap[:], in0=Bc.ap[:], in1=Mx.ap[:, 1:NT + 1], op=ALU.subtract),
             reads=[Bc, Mx], writes=[nbm])
        T.op(DVE, lambda: nc.vector.tensor_scalar(out=mout.ap[:], in0=nbm.ap[:, NT - 1:NT], scalar1=-1.0, scalar2=None, op0=ALU.mult),
             reads=[nbm], writes=[mout])
        out_m = mp_o if init is None else ms_o
        T.dma(SP, out_m[:], mout.ap[:], reads=[mout], writes=[d_out], sem_buf=mout)
        for ch in range(nchunk):
            T.op(PE, lambda ch=ch: nc.tensor.transpose(pg.ap[0:L, 0:MLH], av.ap[:, ch * L:(ch + 1) * L], ident.ap[0:MLH, 0:MLH]),
                 reads=[av, ident], writes=[pg])
            T.op(PE, lambda ch=ch: nc.tensor.transpose(pg.ap[0:L, 128:128 + MLH], nbm.ap[:, ch * L:(ch + 1) * L], ident.ap[0:MLH, 0:MLH]),
                 reads=[nbm, ident], writes=[pg])
            T.op(DVE, lambda ch=ch: nc.vector.tensor_copy(acol.ap[0:L, ch, :], pg.ap[0:L, 0:MLH]), reads=[pg], writes=[acol])
            T.op(ACT, lambda ch=ch: nc.scalar.activation(ccol.ap[0:L, ch, :], pg.ap[0:L, 128:128 + MLH], AF.Exp), reads=[pg], writes=[ccol])
        T.pop()
        Mb = T.sbuf("Mb", [128, SEGT + 1], F32)
        NMb = T.sbuf("NMb", [128, SEGT + 1], F32)
        qtm = T.sbuf("mq_tm", [L, SEGC, DQK], BF16)
        ktm = T.sbuf("mk_tm", [L, SEGC, DQK], BF16)
        vtm = T.sbuf("mv_tm", [L, SEGC, DV], BF16)
        qT = T.sbuf("mqT", [128, 2, SEGT], BF16)
        kT = T.sbuf("mkT", [128, 2, SEGT], BF16)
        mixml = T.sbuf("mixml", [128, NV, SEGT], BF16)
        CT = T.sbuf("CT", [128, 2, DV], F32)
        CTb = T.sbuf("CTb", [128, 2, DV], BF16)
        ncol = T.sbuf("ncol", [128, 2], F32)
        ncolb = T.sbuf("ncolb", [128, 2], BF16)
        cstage = T.sbuf("cstage", [128, NV, DQK], F32)
        WT = [T.sbuf(f"WT{i}", [64, 64], F32) for i in range(2)]
        sT = [T.sbuf(f"sT{i}", [64, 64], BF16) for i in range(2)]
        wib = [T.sbuf(f"wib{i}", [128, 64], F32) for i in range(2)]
        qs = [T.sbuf(f"qs{i}", [128, 2, 64], BF16) for i in range(2)]
        wtok = [T.sbuf(f"wtok{i}", [64, 1], F32) for i in range(2)]
        wprev = [T.sbuf(f"wprev{i}", [128, 1], F32) for i in range(2)]
        kst = [T.sbuf(f"kst{i}", [64, DQK], BF16) for i in range(2)]
        rr = [T.sbuf(f"rr{i}", [64, 2], F32) for i in range(2)]
        hsb = [T.sbuf(f"hsb{i}", [64, DV], F32) for i in range(2)]
        hsq = T.sbuf("hsq", [64, DV], F32)
        osb = [T.sbuf(f"osb{i}", [64, DV], F32) for i in range(2)]
        ymb = [T.sbuf(f"ymb{i}", [64, DV], BF16) for i in range(2)]
        p_qk = T.psum("p_qk", [128, 512], F32)
        p_num = T.psum("p_num", [128, 512], F32)
        p_den = T.psum("p_den", [128, 512], F32)
        p_up = [T.psum(f"p_up{i}", [128, 512], F32) for i in range(2)]
        p_trb = T.psum("p_trb", [128, 512], BF16)
        qscale = float(DQK) ** -0.5
        out_c, out_n = (cp_o, np_o) if init is None else (cs_o, ns_o)
        for h in range(MLH):
            if init is None:
                T.op(POOL, lambda: nc.gpsimd.memset(CT.ap[:], 0.0), writes=[CT])
                T.op(POOL, lambda: nc.gpsimd.memset(CTb.ap[:], 0.0), writes=[CTb])
                T.op(POOL, lambda: nc.gpsimd.memset(ncol.ap[:], 0.0), writes=[ncol])
                T.op(POOL, lambda: nc.gpsimd.memset(ncolb.ap[:], 0.0), writes=[ncolb])
            else:
                T.dma(SP, cstage.ap[:], init[0][h * DV:(h + 1) * DV, :].rearrange("(a p) d -> p a d", p=128), reads=[d_in], writes=[cstage])
                for dc in range(2):
                    for vc in range(NV):
                        T.op(PE, lambda dc=dc, vc=vc: nc.tensor.transpose(pg.ap[:, vc * 128:(vc + 1) * 128],
                                                                         cstage.ap[:, vc, dc * 128:(dc + 1) * 128], ident.ap[:]),
                             reads=[cstage, ident], writes=[pg])
                    T.op(DVE, lambda dc=dc: nc.vector.tensor_copy(CT.ap[:, dc, :], pg.ap[:, 0:DV]), reads=[pg], writes=[CT])
                    T.op(ACT, lambda dc=dc: nc.scalar.copy(CTb.ap[:, dc, :], pg.ap[:, 0:DV]), reads=[pg], writes=[CTb])
                T.dma(SP, ncol.ap[:], init[1][:, 2 * h:2 * h + 2], reads=[d_in], writes=[ncol])
                T.op(DVE, lambda: nc.vector.tensor_copy(ncolb.ap[:], ncol.ap[:]), reads=[ncol], writes=[ncolb])
            for seg0c in range(0, nchunk, SEGC):
                nsc = min(SEGC, nchunk - seg0c)
                s0 = seg0c * L
                st_ = nsc * L
                for c0 in range(0, st_ + 1, 512):
                    w = min(512, st_ + 1 - c0)
                    T.op(PE, lambda c0=c0, w=w: nc.tensor.matmul(pg.ap[:, 0:w], sel.ap[:, h * 128:(h + 1) * 128], Mx.ap[:, s0 + c0:s0 + c0 + w],
                                                                 start=True, stop=True), reads=[sel, Mx], writes=[pg])
                    T.op(DVE, lambda c0=c0, w=w: nc.vector.tensor_copy(Mb.ap[:, c0:c0 + w], pg.ap[:, 0:w]), reads=[pg], writes=[Mb])
                    T.op(ACT, lambda c0=c0, w=w: nc.scalar.mul(NMb.ap[:, c0:c0 + w], pg.ap[:, 0:w], -1.0), reads=[pg], writes=[NMb])
                r0 = tok0 + s0
                T.dma(POOL, qtm.ap[:, 0:nsc, :], z_scr[r0:r0 + st_, c.o_mq + h * DQK:c.o_mq + (h + 1) * DQK].rearrange("(c p) d -> p c d", p=L),
                      reads=[d_z], writes=[qtm])
                T.dma(POOL, ktm.ap[:, 0:nsc, :], z_scr[r0:r0 + st_, c.o_mk + h * DQK:c.o_mk + (h + 1) * DQK].rearrange("(c p) d -> p c d", p=L),
                      reads=[d_z], writes=[ktm])
                T.dma(POOL, vtm.ap[:, 0:nsc, :], z_scr[r0:r0 + st_, c.o_mv + h * DV:c.o_mv + (h + 1) * DV].rearrange("(c p) d -> p c d", p=L),
                      reads=[d_z], writes=[vtm])
                for (src, dst, scl) in ((qtm, qT, qscale), (ktm, kT, 1.0)):
                    for ch in range(nsc):
                        for dc in range(2):
                            T.op(PE, lambda ch=ch, dc=dc, src=src: nc.tensor.transpose(
                                p_trb.ap[:, dc * 64:dc * 64 + L], src.ap[0:L, ch, dc * 128:(dc + 1) * 128], identb.ap[0:L, 0:L]),
                                reads=[src, identb], writes=[p_trb])
                        T.op(ACT, lambda ch=ch, dst=dst, scl=scl: nc.scalar.mul(
                            dst.ap[:, :, ch * L:(ch + 1) * L], p_trb.ap[:, 0:128].rearrange("p (a b) -> p a b", a=2)[:, :, 0:L], scl),
                            reads=[p_trb], writes=[dst])
                for chl in range(nsc):
                    ch = seg0c + chl
                    i2 = ch % 2
                    t0 = chl * L
                    cs = slice(t0, t0 + L)
                    T.op(ACT, lambda: nc.scalar.activation(WT[i2].ap[0:L, 0:L], NMb.ap[0:L, 1 + t0:1 + t0 + L], AF.Exp,
                                                           bias=acol.ap[0:L, ch, h:h + 1]), reads=[NMb, acol], writes=[WT[i2]])
                    T.op(POOL, lambda: nc.gpsimd.tensor_tensor(out=WT[i2].ap[0:L, 0:L], in0=WT[i2].ap[0:L, 0:L], in1=cmask.ap[0:L, 0:L],
                                                               op=ALU.mult), reads=[WT[i2], cmask], writes=[WT[i2]])
                    T.op(ACT, lambda: nc.scalar.activation(wib[i2].ap[:, 0:L], NMb.ap[:, 1 + t0:1 + t0 + L], AF.Exp,
                                                           bias=Mb.ap[:, t0:t0 + 1]), reads=[NMb, Mb], writes=[wib[i2]])
                    T.op(ACT, lambda: nc.scalar.activation(wtok[i2].ap[0:L, :], acol.ap[0:L, ch, h:h + 1], AF.Exp,
                                                           bias=NMb.ap[0:L, t0 + L:t0 + L + 1]), reads=[NMb, acol], writes=[wtok[i2]])
                    T.op(ACT, lambda: nc.scalar.activation(wprev[i2].ap[:], NMb.ap[:, t0 + L:t0 + L + 1], AF.Exp,
                                                           bias=Mb.ap[:, t0:t0 + 1]), reads=[NMb, Mb], writes=[wprev[i2]])
                    for dc in range(2):
                        T.op(POOL, lambda dc=dc: nc.gpsimd.tensor_tensor(out=qs[i2].ap[:, dc, 0:L], in0=qT.ap[:, dc, cs], in1=wib[i2].ap[:, 0:L],
                                                                         op=ALU.mult), reads=[qT, wib[i2]], writes=[qs[i2]])
                    T.op(POOL, lambda: nc.gpsimd.tensor_scalar(out=kst[i2].ap[0:L, :], in0=ktm.ap[0:L, chl, :], scalar1=wtok[i2].ap[0:L, 0:1],
                                                               scalar2=None, op0=ALU.mult), reads=[ktm, wtok[i2]], writes=[kst[i2]])
                    for dc in range(2):
                        T.op(PE, lambda dc=dc: nc.tensor.matmul(p_qk.ap[0:L, 0:L], kT.ap[:, dc, cs], qT.ap[:, dc, cs], start=(dc == 0), stop=(dc == 1)),
                             reads=[kT, qT], writes=[p_qk])
                    T.op(DVE, lambda: nc.vector.tensor_tensor(out=sT[i2].ap[0:L, 0:L], in0=p_qk.ap[0:L, 0:L], in1=WT[i2].ap[0:L, 0:L], op=ALU.mult),
                         reads=[p_qk, WT[i2]], writes=[sT[i2]])
                    for dc in range(2):
                        T.op(PE, lambda dc=dc: nc.tensor.matmul(p_num.ap[0:L, 0:DV], qs[i2].ap[:, dc, 0:L], CTb.ap[:, dc, :], start=(dc == 0), stop=False),
                             reads=[qs[i2], CTb], writes=[p_num])
                    T.op(PE, lambda: nc.tensor.matmul(p_num.ap[0:L, 0:DV], sT[i2].ap[0:L, 0:L], vtm.ap[0:L, chl, :], start=False, stop=True),
                         reads=[sT[i2], vtm], writes=[p_num])
                    for dc in range(2):
                        T.op(PE, lambda dc=dc: nc.tensor.matmul(p_den.ap[0:L, 0:1], qs[i2].ap[:, dc, 0:L], ncolb.ap[:, dc:dc + 1], start=(dc == 0), stop=False),
                             reads=[qs[i2], ncolb], writes=[p_den])
                    T.op(PE, lambda: nc.tensor.matmul(p_den.ap[0:L, 0:1], sT[i2].ap[0:L, 0:L], onesb.ap[0:L, 0:1], start=False, stop=True),
                         reads=[sT[i2], onesb], writes=[p_den])
                    for dc in range(2):
                        T.op(PE, lambda dc=dc: nc.tensor.matmul(p_up[dc].ap[:, 0:DV], kst[i2].ap[0:L, dc * 128:(dc + 1) * 128], vtm.ap[0:L, chl, :],
                                                                start=True, stop=True), reads=[kst[i2], vtm], writes=[p_up[dc]])
                    for dc in range(2):
                        T.op(PE, lambda dc=dc: nc.tensor.matmul(p_den.ap[:, 8 + dc:9 + dc], kst[i2].ap[0:L, dc * 128:(dc + 1) * 128], onesb.ap[0:L, 0:1],
                                                                start=True, stop=True), reads=[kst[i2], onesb], writes=[p_den])
                    T.op(DVE, lambda: nc.vector.tensor_scalar(out=rr[i2].ap[0:L, 0:1], in0=p_den.ap[0:L, 0:1], scalar1=-1.0, scalar2=None, op0=ALU.mult),
                         reads=[p_den], writes=[rr[i2]])
                    T.op(DVE, lambda: nc.vector.tensor_tensor(out=rr[i2].ap[0:L, 0:1], in0=rr[i2].ap[0:L, 0:1], in1=p_den.ap[0:L, 0:1], op=ALU.max),
                         reads=[rr[i2], p_den], writes=[rr[i2]])
                    T.op(DVE, lambda: nc.vector.tensor_tensor(out=rr[i2].ap[0:L, 0:1], in0=rr[i2].ap[0:L, 0:1], in1=ccol.ap[0:L, ch, h:h + 1], op=ALU.max),
                         reads=[rr[i2], ccol], writes=[rr[i2]])
                    T.op(DVE, lambda: nc.vector.reciprocal(rr[i2].ap[0:L, 0:1], rr[i2].ap[0:L, 0:1]), reads=[rr[i2]], writes=[rr[i2]])
                    T.op(DVE, lambda: nc.vector.tensor_scalar(out=hsb[i2].ap[0:L, :], in0=p_num.ap[0:L, 0:DV], scalar1=rr[i2].ap[0:L, 0:1],
                                                              scalar2=None, op0=ALU.mult), reads=[p_num, rr[i2]], writes=[hsb[i2]])
                    for dc in range(2):
                        T.op(DVE, lambda dc=dc: nc.vector.scalar_tensor_tensor(out=CT.ap[:, dc, :], in0=CT.ap[:, dc, :], scalar=wprev[i2].ap[:, 0:1],
                                                                               in1=p_up[dc].ap[:, 0:DV], op0=ALU.mult, op1=ALU.add),
                             reads=[CT, wprev[i2], p_up[dc]], writes=[CT])
                    T.op(ACT, lambda: nc.scalar.copy(CTb.ap[:], CT.ap[:]), reads=[CT], writes=[CTb])
                    T.op(DVE, lambda: nc.vector.scalar_tensor_tensor(out=ncol.ap[:], in0=ncol.ap[:], scalar=wprev[i2].ap[:, 0:1],
                                                                     in1=p_den.ap[:, 8:10], op0=ALU.mult, op1=ALU.add),
                         reads=[ncol, wprev[i2], p_den], writes=[ncol])
                    T.op(DVE, lambda: nc.vector.tensor_copy(ncolb.ap[:], ncol.ap[:]), reads=[ncol], writes=[ncolb])
                    T.op(ACT, lambda: nc.scalar.activation(hsq.ap[0:L, :], hsb[i2].ap[0:L, :], AF.Square, accum_out=rr[i2].ap[0:L, 1:2]),
                         reads=[hsb[i2]], writes=[hsq, rr[i2]])
                    T.op(ACT, lambda: nc.scalar.activation(rr[i2].ap[0:L, 1:2], rr[i2].ap[0:L, 1:2], AF.Sqrt, bias=EPS, scale=1.0 / DV),
                         reads=[rr[i2]], writes=[rr[i2]])
                    T.op(DVE, lambda: nc.vector.reciprocal(rr[i2].ap[0:L, 1:2], rr[i2].ap[0:L, 1:2]), reads=[rr[i2]], writes=[rr[i2]])
                    T.dma(SP, osb[i2].ap[0:L, :], z_scr[r0 + t0:r0 + t0 + L, c.o_mo + h * DV:c.o_mo + (h + 1) * DV], reads=[d_z], writes=[osb[i2]])
                    T.op(ACT, lambda: nc.scalar.activation(osb[i2].ap[0:L, :], osb[i2].ap[0:L, :], AF.Sigmoid), reads=[osb[i2]], writes=[osb[i2]])
                    T.op(POOL, lambda: nc.gpsimd.tensor_tensor(out=osb[i2].ap[0:L, :], in0=osb[i2].ap[0:L, :], in1=gml.ap[0:L, h * DV:(h + 1) * DV],
                                                               op=ALU.mult), reads=[osb[i2], gml], writes=[osb[i2]])
                    T.op(DVE, lambda: nc.vector.scalar_tensor_tensor(out=ymb[i2].ap[0:L, :], in0=hsb[i2].ap[0:L, :], scalar=rr[i2].ap[0:L, 1:2],
                                                                     in1=osb[i2].ap[0:L, :], op0=ALU.mult, op1=ALU.mult),
                         reads=[hsb[i2], rr[i2], osb[i2]], writes=[ymb[i2]])
                    for vc in range(NV):
                        T.op(PE, lambda vc=vc: nc.tensor.transpose(p_trb.ap[:, 128 + vc * 64:128 + vc * 64 + L], ymb[i2].ap[0:L, vc * 128:(vc + 1) * 128],
                                                                   identb.ap[0:L, 0:L]), reads=[ymb[i2], identb], writes=[p_trb])
                    T.op(ACT, lambda: nc.scalar.copy(mixml.ap[:, :, cs], p_trb.ap[:, 128:128 + 64 * NV].rearrange("p (a b) -> p a b", b=64)[:, :, 0:L]),
                         reads=[p_trb], writes=[mixml])
                for vc in range(NV):
                    row = SBW + h * DV + vc * 128
                    T.dma(SP, mixT_scr[row:row + 128, r0:r0 + st_], mixml.ap[:, vc, 0:st_], reads=[mixml], writes=[d_mix], sem_buf=mixml)
            for vc in range(NV):
                for dc in range(2):
                    T.op(PE, lambda vc=vc, dc=dc: nc.tensor.transpose(pg.ap[:, dc * 128:(dc + 1) * 128], CT.ap[:, dc, vc * 128:(vc + 1) * 128], ident.ap[:]),
                         reads=[CT, ident], writes=[pg])
                T.op(DVE, lambda vc=vc: nc.vector.tensor_copy(cstage.ap[:, vc, :], pg.ap[:, 0:DQK]), reads=[pg], writes=[cstage])
            T.dma(SP, out_c[h * DV:(h + 1) * DV, :].rearrange("(a p) d -> p a d", p=128), cstage.ap[:], reads=[cstage], writes=[d_out], sem_buf=cstage)
            T.dma(SP, out_n[:, 2 * h:2 * h + 2], ncol.ap[:], reads=[ncol], writes=[d_out], sem_buf=ncol)
        T.pop()

    st = c.stages
    if "A" in st:
        T.push()
        B = phase_AC_buffers()
        for tile in tiles:
            phase_A(B, tile)
        T.pop()
    if "SBP" in st:
        sb_attention(0, SEQ, 512, [(z_scr[0:SEQ, c.o_k:c.o_k + SBW], z_scr[0:SEQ, c.o_v:c.o_v + SBW], SEQ)], SEQ)
    if "MLP" in st:
        mlstm(0, c.CHUNK, SEQ // c.CHUNK, None)
    if "SBS" in st:
        sb_attention(SEQ, NS, NS, [(ck, cv, PAST), (z_scr[SEQ:SEQ + NS, c.o_k:c.o_k + SBW], z_scr[SEQ:SEQ + NS, c.o_v:c.o_v + SBW], NS)], NS)
    if "MLS" in st:
        mlstm(SEQ, NS, 1, (sc_in, sn_in, sm_in))
    if "C" in st:
        T.push()
        B = phase_AC_buffers()
        for tile in tilesC:
            phase_C(B, tile)
        T.pop()
    T.finish()
    return nc, T


def _consts(cfg):
    MLH = cfg.MLH
    k = np.arange(128)
    tri = (k[:, None] >= k[None, :]).astype(np.float32)
    j = np.arange(896)
    maskw = ((j[None, :] - 384) > k[:, None]).astype(np.float32)
    s = np.arange(64)
    cmask = (s[:, None] <= s[None, :]).astype(np.float32)
    sel = np.zeros((MLH, MLH * 128), np.float32)
    for h in range(MLH):
        sel[h, h * 128:(h + 1) * 128] = 1.0
    return dict(ident=np.eye(128, dtype=np.float32), ones=np.ones((128, 128), np.float32), tri=tri, maskw=maskw,
                cmask=cmask, sel=sel)


def make_in_maps(cfg, inp, n_cores):
    c = cfg
    f = lambda a: np.ascontiguousarray(np.asarray(a, dtype=np.float32))
    B = inp["x_prompt"].shape[0]
    lns = [inp["ln_ffn1"][0], inp["ln_mix"][0], inp["ln_ffn2"][0], inp["ln_ple"][0], inp["ln_final"]]
    lnT = np.concatenate([f(v).reshape(c.KC, 128).T for v in lns], axis=1)
    common = dict(lnT=f(lnT), gsbT=f(f(inp["g_sb_head"][0]).T),
                  gml=f(np.broadcast_to(f(inp["g_ml_head"][0]).reshape(1, -1), (64, c.MLH * c.DV))),
                  bi=f(f(inp["b_if"][0])[:c.MLH].reshape(c.MLH, 1)), bf=f(f(inp["b_if"][0])[c.MLH:].reshape(c.MLH, 1)))
    common.update(_consts(c))
    for w in WEIGHTS:
        common[w] = f(inp[w][0])
    maps = []
    for core in range(n_cores):
        pb = core % B
        sb = core % inp["x_sample"].shape[0]
        m = dict(common)
        rank = (core // B) % c.NRANK
        nq = c.SEQ // c.NRANK
        tc_ = min(c.TT, nq)
        nj, nct = c.SEQ // tc_, nq // tc_
        m["xp"] = f(inp["x_prompt"][pb]); m["xs"] = f(inp["x_sample"][sb])
        m["ppq"] = f(inp["p_prompt"][0, pb, rank * nq:(rank + 1) * nq]); m["ps"] = f(inp["p_sample"][0, sb])
        selq = np.zeros((128, nct * nj), np.float32)
        for i in range(nct):
            selq[:, i * nj + rank * nct + i] = 1.0
        m["selq"] = selq
        m["ck"] = f(inp["cache_sb_k"][0, sb]).reshape(c.PAST, c.SBW)
        m["cv"] = f(inp["cache_sb_v"][0, sb]).reshape(c.PAST, c.SBW)
        m["sc"] = f(inp["state_ml_c"][0, sb]).reshape(c.MLH * c.DV, c.DQK)
        m["sn"] = f(f(inp["state_ml_n"][0, sb]).reshape(c.MLH * 2, 128).T)
        m["sm"] = f(inp["state_ml_m"][0, sb]).reshape(c.MLH, 1)
        maps.append(m)
    return maps


def assemble(cfg, res, B, DB):
    c = cfg
    r = res

    def n_un(a):
        return np.ascontiguousarray(a.T).reshape(c.MLH, c.DQK)

    yp = np.stack([np.concatenate([r[b + B * k]["ypq"] for k in range(c.NRANK)], axis=0) for b in range(B)])
    ys = np.stack([r[b]["ys"] for b in range(DB)])
    kp = np.stack([r[b]["kp"].reshape(c.SEQ, c.SBH, 128) for b in range(B)])[None]
    vp = np.stack([r[b]["vp"].reshape(c.SEQ, c.SBH, 128) for b in range(B)])[None]
    cp = np.stack([r[b]["cp"].reshape(c.MLH, c.DV, c.DQK) for b in range(B)])[None]
    np_ = np.stack([n_un(r[b]["np"]) for b in range(B)])[None]
    mp = np.stack([r[b]["mp"].reshape(c.MLH) for b in range(B)])[None]
    ks = np.stack([r[b]["ks"].reshape(c.DEC_SEQ, c.SBH, 128) for b in range(DB)])[None]
    vs = np.stack([r[b]["vs"].reshape(c.DEC_SEQ, c.SBH, 128) for b in range(DB)])[None]
    cs = np.stack([r[b]["cs"].reshape(c.MLH, c.DV, c.DQK) for b in range(DB)])[None]
    ns = np.stack([n_un(r[b]["ns"]) for b in range(DB)])[None]
    ms = np.stack([r[b]["ms"].reshape(c.MLH) for b in range(DB)])[None]
    return tuple(np.ascontiguousarray(a, dtype=np.float32) for a in (yp, ys, kp, vp, cp, np_, mp, ks, vs, cs, ns, ms))


def kernel(**inputs):
    cfg = Cfg()
    n = 8
    nc, _ = build(cfg)
    in_maps = make_in_maps(cfg, inputs, n)
    res = run_bass_kernel_spmd(nc, in_maps, core_ids=list(range(n)))
    return assemble(cfg, res.results, inputs["x_prompt"].shape[0], inputs["x_sample"].shape[0])
```
